# Optimizing a Trainium2 kernel written in Bass

```python
import jax, jax.numpy as jnp
from jax import lax
import numpy as np

D_MODEL = 1024
BATCH = 2
SEQ = 16384
DEPTH = 4

CHUNK = 64
N_MIXERS = 2
N_LAYERS_A = (DEPTH + 1) // 2
N_LAYERS_B = DEPTH // 2
SGU_BLOCK = 128
SGU_WIDTH = 2 * D_MODEL
SGU_GROUPS = 16
SGU_GROUP_DIM = SGU_WIDTH // SGU_GROUPS
RWKV_HEAD_DIM = 64
RWKV_HEADS = D_MODEL // RWKV_HEAD_DIM
DECAY_LORA = 64
AAA_LORA = 64
MV_LORA = 32
GATE_LORA = 128
FFN_HIDDEN = 4 * D_MODEL
RMS_EPS = 1e-6
LN_EPS = 1e-5
GN_EPS = 64e-5

kernel_name = "hybrid_sgu_rwkv7_trunk"


def _rmsnorm(x, g):
    xf = x.astype(jnp.float32)
    y = xf * lax.rsqrt(jnp.mean(xf * xf, axis=-1, keepdims=True) + RMS_EPS)
    return (y * g.astype(jnp.float32)).astype(x.dtype)


def _layernorm(x, g, b, eps):
    xf = x.astype(jnp.float32)
    mu = jnp.mean(xf, axis=-1, keepdims=True)
    var = jnp.mean(jnp.square(xf - mu), axis=-1, keepdims=True)
    y = (xf - mu) * lax.rsqrt(var + eps)
    return (y * g.astype(jnp.float32) + b.astype(jnp.float32)).astype(x.dtype)


def _sgu_mask():
    chunk_id = jnp.arange(SGU_BLOCK) // CHUNK
    return chunk_id[None, :] <= chunk_id[:, None]


def _spatial_gating_mixer(h, w_in, b_in, ln_g, ln_b, w_s, b_s, w_out):
    bsz, seq, _ = h.shape
    z = jax.nn.gelu(h @ w_in + b_in, approximate=False)
    u, v = jnp.split(z, 2, axis=-1)
    v = _layernorm(v, ln_g, ln_b, LN_EPS)
    v = v.reshape(bsz, seq // SGU_BLOCK, SGU_BLOCK, SGU_GROUPS, SGU_GROUP_DIM)
    w_s = w_s * _sgu_mask()[None].astype(w_s.dtype)
    s = jnp.einsum('gij,bnjgc->bnigc', w_s, v) + b_s.T[None, None, :, :, None]
    y = u * s.reshape(bsz, seq, SGU_WIDTH)
    return y @ w_out


def _wkv7_scan(r, w, k, v, kk, kka):
    bsz, _, nh, n = r.shape
    xs = tuple(jnp.moveaxis(t, 1, 0) for t in (r, w, k, v, kk, kka))

    def step(S, inp):
        r_t, w_t, k_t, v_t, kk_t, kka_t = inp
        sa = jnp.einsum('bhvk,bhk->bhv', S, -kk_t)
        S = (S * w_t[:, :, None, :] + sa[..., None] * kka_t[:, :, None, :]
             + v_t[..., None] * k_t[:, :, None, :])
        y = jnp.einsum('bhvk,bhk->bhv', S, r_t)
        return S, y

    S0 = jnp.zeros((bsz, nh, n, n), jnp.float32)
    _, y = lax.scan(step, S0, xs)
    return jnp.moveaxis(y, 0, 1)


def _rwkv7_mixer(h, mu, w_rkv, w0, w_la, w_lb, a0, a_la, a_lb, g_la, g_lb,
                 k_k, k_a, r_k, ln_g, ln_b, w_o, v_first, v_mix):
    bsz, seq, d = h.shape
    f32 = jnp.float32

    def heads(t):
        return t.reshape(bsz, seq, RWKV_HEADS, RWKV_HEAD_DIM)

    xx = jnp.pad(h, ((0, 0), (1, 0), (0, 0)))[:, :-1] - h
    xs = h[None] + xx[None] * mu[:, None, None, :]
    rkv = jnp.einsum('pbtd,pde->pbte', xs[:3], w_rkv)
    r, k, v = rkv[0], rkv[1], rkv[2]
    xv, xw, xa, xg = xs[2], xs[3], xs[4], xs[5]
    log_w = -jax.nn.softplus(-(w0 + jnp.tanh(xw @ w_la) @ w_lb)) - 0.5
    decay = jnp.exp(-jnp.exp(log_w.astype(f32)))
    if v_mix is None:
        v_first = v
    else:
        v0, v_la, v_lb = v_mix
        v = v + (v_first - v) * jax.nn.sigmoid(v0 + (xv @ v_la) @ v_lb)
    a = jax.nn.sigmoid(a0 + (xa @ a_la) @ a_lb)
    g = jax.nn.sigmoid(xg @ g_la) @ g_lb
    kk = heads(k * k_k).astype(f32)
    kk = kk / jnp.maximum(jnp.sqrt(jnp.sum(kk * kk, axis=-1, keepdims=True)), 1e-12)
    k = k * (1.0 + (a - 1.0) * k_a)
    r_h, k_h, v_h = heads(r).astype(f32), heads(k).astype(f32), heads(v).astype(f32)
    y = _wkv7_scan(r_h, heads(decay), k_h, v_h, kk, kk * heads(a).astype(f32))
    mu_y = jnp.mean(y, axis=-1, keepdims=True)
    var_y = jnp.mean(jnp.square(y - mu_y), axis=-1, keepdims=True)
    y = ((y - mu_y) * lax.rsqrt(var_y + GN_EPS)).reshape(bsz, seq, d)
    y = y * ln_g.astype(f32) + ln_b.astype(f32)
    bonus = jnp.sum(r_h * k_h * r_k.astype(f32), axis=-1, keepdims=True) * v_h
    y = (y + bonus.reshape(bsz, seq, d)) * g.astype(f32)
    return y.astype(h.dtype) @ w_o, v_first


def _squared_relu_mlp(h, w1, w2):
    return jnp.square(jax.nn.relu(h @ w1)) @ w2


def setup_inputs(seed: int = 0) -> dict:
    key = jax.random.key(seed)
    ks = jax.random.split(key, 32)
    f32 = jnp.float32
    D, E, G, H, N = D_MODEL, SGU_WIDTH, SGU_GROUPS, RWKV_HEADS, RWKV_HEAD_DIM
    nA, nB = N_LAYERS_A, N_LAYERS_B

    def nrm(k, shape, scale):
        return jax.random.normal(k, shape, f32) * scale

    return {
        "x": nrm(ks[0], (BATCH, SEQ, D), 1.0),
        "norm_mix_g": 1.0 + nrm(ks[1], (DEPTH, D), 0.05),
        "norm_ffn_g": 1.0 + nrm(ks[2], (DEPTH, D), 0.05),
        "final_norm_g": 1.0 + nrm(ks[3], (D,), 0.05),
        "ffn_w1": nrm(ks[4], (DEPTH, D, FFN_HIDDEN), D ** -0.5),
        "ffn_w2": nrm(ks[5], (DEPTH, FFN_HIDDEN, D), 0.5 * FFN_HIDDEN ** -0.5),
        "sgu_w_in": nrm(ks[6], (nA, D, 2 * E), D ** -0.5),
        "sgu_b_in": nrm(ks[7], (nA, 2 * E), 0.02),
        "sgu_ln_g": 1.0 + nrm(ks[8], (nA, E), 0.05),
        "sgu_ln_b": nrm(ks[9], (nA, E), 0.02),
        "sgu_w_s": nrm(ks[10], (nA, G, SGU_BLOCK, SGU_BLOCK), SGU_BLOCK ** -0.5),
        "sgu_b_s": 1.0 + nrm(ks[11], (nA, G, SGU_BLOCK), 0.1),
        "sgu_w_out": nrm(ks[12], (nA, E, D), 0.5 * E ** -0.5),
        "rwkv_mu": jax.random.uniform(ks[13], (nB, 6, D), f32),
        "rwkv_w_rkv": nrm(ks[14], (nB, 3, D, D), D ** -0.5),
        "rwkv_w0": jax.random.uniform(ks[15], (nB, D), f32, -6.0, 0.5),
        "rwkv_w_lora_a": nrm(ks[16], (nB, D, DECAY_LORA), D ** -0.5),
        "rwkv_w_lora_b": nrm(ks[17], (nB, DECAY_LORA, D), 0.3 * DECAY_LORA ** -0.5),
        "rwkv_a0": nrm(ks[18], (nB, D), 0.1),
        "rwkv_a_lora_a": nrm(ks[19], (nB, D, AAA_LORA), D ** -0.5),
        "rwkv_a_lora_b": nrm(ks[20], (nB, AAA_LORA, D), 0.3 * AAA_LORA ** -0.5),
        "rwkv_v0": 0.5 + nrm(ks[21], (nB - 1, D), 0.1),
        "rwkv_v_lora_a": nrm(ks[22], (nB - 1, D, MV_LORA), D ** -0.5),
        "rwkv_v_lora_b": nrm(ks[23], (nB - 1, MV_LORA, D), 0.3 * MV_LORA ** -0.5),
        "rwkv_g_lora_a": nrm(ks[24], (nB, D, GATE_LORA), D ** -0.5),
        "rwkv_g_lora_b": nrm(ks[25], (nB, GATE_LORA, D), GATE_LORA ** -0.5),
        "rwkv_k_k": 0.85 + nrm(ks[26], (nB, D), 0.05),
        "rwkv_k_a": 1.0 + nrm(ks[27], (nB, D), 0.05),
        "rwkv_r_k": nrm(ks[28], (nB, H, N), 0.1),
        "rwkv_ln_g": 1.0 + nrm(ks[29], (nB, D), 0.05),
        "rwkv_ln_b": nrm(ks[30], (nB, D), 0.02),
        "rwkv_w_o": nrm(ks[31], (nB, D, D), 0.5 * D ** -0.5),
    }


def reference(x, norm_mix_g, norm_ffn_g, final_norm_g, ffn_w1, ffn_w2,
              sgu_w_in, sgu_b_in, sgu_ln_g, sgu_ln_b, sgu_w_s, sgu_b_s, sgu_w_out,
              rwkv_mu, rwkv_w_rkv, rwkv_w0, rwkv_w_lora_a, rwkv_w_lora_b,
              rwkv_a0, rwkv_a_lora_a, rwkv_a_lora_b,
              rwkv_v0, rwkv_v_lora_a, rwkv_v_lora_b,
              rwkv_g_lora_a, rwkv_g_lora_b, rwkv_k_k, rwkv_k_a, rwkv_r_k,
              rwkv_ln_g, rwkv_ln_b, rwkv_w_o):
    v_first = None
    for i in range(DEPTH):
        h = _rmsnorm(x, norm_mix_g[i])
        j = i // N_MIXERS
        if i % N_MIXERS == 0:
            y = _spatial_gating_mixer(h, sgu_w_in[j], sgu_b_in[j], sgu_ln_g[j], sgu_ln_b[j],
                                      sgu_w_s[j], sgu_b_s[j], sgu_w_out[j])
        else:
            v_mix = None if j == 0 else (rwkv_v0[j - 1], rwkv_v_lora_a[j - 1], rwkv_v_lora_b[j - 1])
            y, v_first = _rwkv7_mixer(h, rwkv_mu[j], rwkv_w_rkv[j], rwkv_w0[j],
                                      rwkv_w_lora_a[j], rwkv_w_lora_b[j],
                                      rwkv_a0[j], rwkv_a_lora_a[j], rwkv_a_lora_b[j],
                                      rwkv_g_lora_a[j], rwkv_g_lora_b[j],
                                      rwkv_k_k[j], rwkv_k_a[j], rwkv_r_k[j],
                                      rwkv_ln_g[j], rwkv_ln_b[j], rwkv_w_o[j],
                                      v_first, v_mix)
        x = x + y
        x = x + _squared_relu_mlp(_rmsnorm(x, norm_ffn_g[i]), ffn_w1[i], ffn_w2[i])
    return _rmsnorm(x, final_norm_g)
```

```python
import numpy as np
from contextlib import ExitStack
import concourse.bass as bass
import concourse.mybir as mybir
from concourse.bass_utils import run_bass_kernel_spmd

F32 = mybir.dt.float32
BF16 = mybir.dt.bfloat16
AF = mybir.ActivationFunctionType
ALU = mybir.AluOpType

NCORES = 8
D = 1024
FF = 4096
SAME_ENGINE_SYNC = True


class Buf:
    __slots__ = ("w", "r", "name")

    def __init__(self, name=""):
        self.w = None
        self.r = {}
        self.name = name


class DSem:
    def __init__(self, sem, name):
        self.sem = sem
        self.n = 0
        self.name = name


class Sched:
    ENGS = ("pe", "act", "dve", "pool", "sp")

    def __init__(self, nc, es):
        self.nc = nc
        self.es = es
        self.ops = {e: [] for e in self.ENGS}
        self.cnt = {e: 0 for e in self.ENGS}
        self.sem = {e: es.enter_context(nc.semaphore("s_" + e)) for e in self.ENGS if e != "sp"}
        self.seen = {e: {} for e in self.ENGS}
        self.pool = []
        self.ptr = 0
        self.nphase = 0

    def dsem(self, name=None):
        if self.ptr == len(self.pool):
            nm = "p%d" % len(self.pool)
            self.pool.append(DSem(self.es.enter_context(self.nc.semaphore("ds_" + nm)), nm))
        d = self.pool[self.ptr]
        self.ptr += 1
        return d

    def new_engine_sems(self):
        for e in self.ENGS:
            if e == "sp":
                continue
            self.sem[e] = self.es.enter_context(self.nc.semaphore("s_%s_%d" % (e, self.nphase)))
            self.cnt[e] = 0
        self.nphase += 1
        for e in self.ENGS:
            for f in self.ENGS:
                self.seen[e].pop("E" + f, None)

    def barrier(self):
        for e in self.ENGS:
            waits = []
            for f in ("pe", "act", "dve", "pool"):
                if f != e and self.cnt[f] > self.seen[e].get("E" + f, 0):
                    waits.append((self.sem[f], self.cnt[f]))
                    self.seen[e]["E" + f] = self.cnt[f]
            for d in self.pool:
                if d.n > self.seen[e].get("D" + d.name, 0):
                    waits.append((d.sem, d.n))
                    self.seen[e]["D" + d.name] = d.n
            self.ops[e].append((waits, None, None))

    def _need(self, e, ev, waits):
        if ev is None:
            return
        key, sem, val, owner = ev
        if owner == e and (e == "pe" or not SAME_ENGINE_SYNC):
            return
        if self.seen[e].get(key, 0) >= val:
            return
        if key not in waits or waits[key][1] < val:
            waits[key] = (sem, val)

    def op(self, e, fn, reads=(), writes=(), dsem=None, inc=16):
        waits = {}
        for b in reads:
            self._need(e, b.w, waits)
        for b in writes:
            self._need(e, b.w, waits)
            for ev in b.r.values():
                self._need(e, ev, waits)
        for key, (sem, val) in waits.items():
            self.seen[e][key] = val
        if dsem is None:
            self.cnt[e] += 1
            ev = ("E" + e, self.sem[e], self.cnt[e], e)
            inc = (self.sem[e], 1)
        else:
            dsem.n += inc
            ev = ("D" + dsem.name, dsem.sem, dsem.n, None)
            inc = (dsem.sem, inc)
        self.ops[e].append((list(waits.values()), fn, inc))
        for b in reads:
            b.r[ev[0]] = ev
        for b in writes:
            b.w = ev
            b.r = {}

    def finish(self, dsems):
        waits = [(d.sem, d.n) for d in dsems if d.n > 0]
        self.ops["sp"].append((waits, None, None))

    def _run(self, e, eng):
        for waits, fn, inc in self.ops[e]:
            for sem, val in waits:
                eng.wait_ge(sem, val)
            if fn is not None:
                ins = fn(eng)
                ins.then_inc(inc[0], inc[1])

    def emit(self):
        nc = self.nc
        with nc.Block() as block:
            @block.tensor
            def _(eng):
                self._run("pe", eng)

            @block.scalar
            def _(eng):
                self._run("act", eng)

            @block.vector
            def _(eng):
                self._run("dve", eng)

            @block.gpsimd
            def _(eng):
                self._run("pool", eng)

            @block.sync
            def _(eng):
                self._run("sp", eng)


class Ctx:
    def __init__(self):
        self.nc = bass.Bass("TRN2", target_bir_lowering=False)
        self.es = ExitStack()
        self.S = Sched(self.nc, self.es)
        self.nbuf = 0

    def dram_in(self, name, shape, dt=F32):
        return self.nc.dram_tensor(name, list(shape), dt, kind="ExternalInput").ap()

    def dram_out(self, name, shape, dt=F32):
        return self.nc.dram_tensor(name, list(shape), dt, kind="ExternalOutput").ap()

    def sb(self, name, shape, dt):
        return self.es.enter_context(self.nc.sbuf_tensor("p%d_%s" % (getattr(self, "phase", 0), name), list(shape), dt))

    def ps(self, name, shape, dt=F32):
        return self.es.enter_context(self.nc.psum_tensor("p%d_%s" % (getattr(self, "phase", 0), name), list(shape), dt))

    def buf(self, name=""):
        self.nbuf += 1
        return Buf(name or "b%d" % self.nbuf)

    def dma(self, out, in_, dsem, reads=(), writes=(), q="sp", slow=False):
        self.S.op(q, lambda eng, o=out, i=in_: eng.dma_start(out=o, in_=i, allow_slow_non_contiguous=slow), reads, writes, dsem=dsem)

    def mm(self, out, lhsT, rhs, start, stop, reads, writes):
        self.S.op("pe", lambda eng, o=out, l=lhsT, r=rhs, s=start, t=stop: eng.matmul(o, l, r, start=s, stop=t),
                  reads, writes)

    def act(self, out, in_, func, reads, writes, bias=None, scale=None):
        kw = {}
        if bias is not None:
            kw["bias"] = bias
        if scale is not None:
            kw["scale"] = scale
        self.S.op("act", lambda eng, o=out, i=in_, f=func, kw=kw: eng.activation(o, i, f, **kw), reads, writes)

    def tt(self, e, out, in0, in1, op, reads, writes):
        self.S.op(e, lambda eng, o=out, a=in0, b=in1, p=op: eng.tensor_tensor(o, a, b, p), reads, writes)

    def ts(self, e, out, in0, s1, s2, op0, op1, reads, writes):
        if op1 is None:
            self.S.op(e, lambda eng, o=out, a=in0, x=s1, p=op0: eng.tensor_scalar(o, a, x, None, p), reads, writes)
        else:
            self.S.op(e, lambda eng, o=out, a=in0, x=s1, y=s2, p=op0, q=op1: eng.tensor_scalar(o, a, x, y, p, q),
                      reads, writes)

    def stt(self, out, in0, scalar, in1, op0, op1, reads, writes):
        self.S.op("dve", lambda eng, o=out, a=in0, s=scalar, b=in1, p=op0, q=op1:
                  eng.scalar_tensor_tensor(o, a, s, b, p, q), reads, writes)

    def copy(self, e, out, in_, reads, writes):
        if e == "act":
            self.S.op(e, lambda eng, o=out, i=in_: eng.copy(o, i), reads, writes)
        else:
            self.S.op(e, lambda eng, o=out, i=in_: eng.tensor_copy(o, i), reads, writes)

    def memset(self, e, ap, val, writes):
        self.S.op(e, lambda eng, a=ap, v=val: eng.memset(a, v), (), writes)

    def recip(self, out, in_, reads, writes):
        self.S.op("dve", lambda eng, o=out, i=in_: eng.reciprocal(o, i), reads, writes)

    def finish(self, dsems):
        self.S.finish(dsems)
        self.S.emit()
        self.es.close()
        return self.nc

    def phase_begin(self):
        self.phase = getattr(self, "phase", 0) + 1
        self._saved = self.es
        self.es = ExitStack()
        self.S.ptr = 0
        self.S.new_engine_sems()

    def phase_end(self):
        self.S.barrier()
        self.es.close()
        self.es = self._saved

    def scratch(self, name, shape, dt):
        return self.nc.dram_tensor(name, list(shape), dt).ap()


def load_weight_bf16(c, dst, src, dsem, wbuf, nsplit=1):
    kc = src.shape[0] // 128
    srcv = src.rearrange("(c p) n -> p c n", p=128)
    step = max(1, kc // nsplit)
    for k0 in range(0, kc, step):
        c.dma(dst[:, k0:k0 + step, :], srcv[:, k0:k0 + step, :], dsem, writes=[wbuf], q="pool")


def emit_rmsnorm(c, x, xb, n, sq, sqb, ones, onesb, ps, psb, rstd, rstdb, g, gb, h, hb):
    if isinstance(sq, list):
        for k in range(8):
            c.act(sq[k % 2][:, :n], x[:, k, :n], AF.Square, [xb], [sqb[k % 2]])
            c.mm(ps[:, :n], ones[:, :], sq[k % 2][:, :n], k == 0, k == 7, [sqb[k % 2], onesb], [psb])
    else:
        for k in range(8):
            c.act(sq[:, k, :n], x[:, k, :n], AF.Square, [xb], [sqb])
        for k in range(8):
            c.mm(ps[:, :n], ones[:, :], sq[:, k, :n], k == 0, k == 7, [sqb, onesb], [psb])
    c.act(rstd[:, :n], ps[:, :n], AF.Sqrt, [psb], [rstdb], bias=1e-6, scale=1.0 / D)
    c.recip(rstd[:, :n], rstd[:, :n], [rstdb], [rstdb])
    for k in range(8):
        c.stt(h[:, k, :n], x[:, k, :n], g[:, k:k + 1], rstd[:, :n], ALU.mult, ALU.mult, [xb, gb, rstdb], [hb])


def build_ffn(NT, TT=256, final=False):
    c = Ctx()
    nc = c.nc
    xT = c.dram_in("xT", [D, NT])
    gin = c.dram_in("g", [128, 8])
    w1 = c.dram_in("w1", [D, FF])
    w2 = c.dram_in("w2", [FF, D])
    if final:
        gfin = c.dram_in("gf", [128, 8])
    yT = c.dram_out("yT", [D, NT])
    xv = xT.rearrange("(c p) n -> p c n", p=128)
    yv = yT.rearrange("(c p) n -> p c n", p=128)

    w1s = c.sb("w1s", [128, 8, FF], BF16)
    w2s = c.sb("w2s", [128, 32, D], BF16)
    gs = c.sb("gs", [128, 8], F32)
    ones = c.sb("ones", [128, 128], BF16)
    w1b, w2b, gb, onesb = c.buf("w1"), c.buf("w2"), c.buf("g"), c.buf("ones")
    dw1, dw2, dg = c.S.dsem("w1"), c.S.dsem("w2"), c.S.dsem("g")
    c.dma(gs[:, :], gin, dg, writes=[gb])
    if final:
        gfs = c.sb("gfs", [128, 8], F32)
        gfb = c.buf("gf")
        dgf = c.S.dsem("gf")
        c.dma(gfs[:, :], gfin, dgf, writes=[gfb])
    c.memset("pool", ones[:, :], 1.0, [onesb])
    NX = 2
    xs = [c.sb("x%d" % i, [128, 8, TT], F32) for i in range(NX)]
    xbs = [c.buf("x%d" % i) for i in range(NX)]
    dxl = [c.S.dsem("xl%d" % i) for i in range(NX)]
    dxs = [c.S.dsem("xs%d" % i) for i in range(NX)]
    ntile = NT // TT
    c.dma(xs[0][:, :, :], xv[:, :, 0:TT], dxl[0], writes=[xbs[0]])
    load_weight_bf16(c, w1s, w1, dw1, w1b, nsplit=4)
    load_weight_bf16(c, w2s, w2, dw2, w2b, nsplit=4)

    sq = c.sb("sq", [128, 8, TT], BF16)
    h = c.sb("h", [128, 8, TT], BF16)
    hid = c.sb("hid", [128, 32, TT], BF16)
    rstd = c.sb("rstd", [128, TT], F32)
    NR = 2
    rl = [c.sb("rl%d" % i, [128, TT], F32) for i in range(NR)]
    rlb = [c.buf() for i in range(NR)]
    sqb, hb, hidb, rstdb = c.buf("sq"), c.buf("h"), c.buf("hid"), c.buf("rstd")
    NP = 3
    pss = [c.ps("ps%d" % i, [128, 512]) for i in range(NP)]
    psb = [c.buf("ps%d" % i) for i in range(NP)]
    psn = c.ps("psn", [128, 512])
    psnb = c.buf("psn")
    if final:
        xo = c.sb("xo", [128, 8, TT], F32)
        xob = c.buf("xo")
    pi = 0
    for t in range(ntile):
        s = t % NX
        x, xb = xs[s], xbs[s]
        if t + 1 < ntile:
            s2 = (t + 1) % NX
            c.dma(xs[s2][:, :, :], xv[:, :, (t + 1) * TT:(t + 2) * TT], dxl[s2], writes=[xbs[s2]])
        emit_rmsnorm(c, x, xb, TT, sq, sqb, ones, onesb, psn, psnb, rstd, rstdb, gs, gb, h, hb)
        for m in range(32):
            ps, pb = pss[pi % NP], psb[pi % NP]
            pi += 1
            for k in range(8):
                c.mm(ps[:, :TT], w1s[:, k, m * 128:(m + 1) * 128], h[:, k, :], k == 0, k == 7, [w1b, hb], [pb])
            r, rb = rl[m % NR], rlb[m % NR]
            c.act(r[:, :], ps[:, :TT], AF.Relu, [pb], [rb])
            c.tt("dve", hid[:, m, :], ps[:, :TT], r[:, :], ALU.mult, [pb, rb], [hidb])
        for n in range(8):
            ps, pb = pss[pi % NP], psb[pi % NP]
            pi += 1
            for m in range(32):
                c.mm(ps[:, :TT], w2s[:, m, n * 128:(n + 1) * 128], hid[:, m, :], m == 0, m == 31, [w2b, hidb], [pb])
            c.tt("dve", x[:, n, :], ps[:, :TT], x[:, n, :], ALU.add, [pb, xb], [xb])
        if final:
            emit_rmsnorm(c, x, xb, TT, sq, sqb, ones, onesb, psn, psnb, rstd, rstdb, gfs, gfb, xo, xob)
            c.dma(yv[:, :, t * TT:(t + 1) * TT], xo[:, :, :], dxs[s], reads=[xob])
        else:
            c.dma(yv[:, :, t * TT:(t + 1) * TT], x[:, :, :], dxs[s], reads=[xb])
    return c.finish(dxs)


_PROG_CACHE = {}


def get_prog(key, builder):
    if key not in _PROG_CACHE:
        _PROG_CACHE[key] = builder()
    return _PROG_CACHE[key]


def to_pc(v):
    return np.ascontiguousarray(np.asarray(v, np.float32).reshape(-1, 128).T)


def run_ffn(xT_cores, g, w1, w2, gf=None):
    NT = xT_cores[0].shape[1]
    final = gf is not None
    nc = get_prog(("ffn", NT, final), lambda: build_ffn(NT, final=final))
    base = {"g": to_pc(g), "w1": np.ascontiguousarray(w1, dtype=np.float32),
            "w2": np.ascontiguousarray(w2, dtype=np.float32)}
    if final:
        base["gf"] = to_pc(gf)
    in_maps = [dict(base, xT=np.ascontiguousarray(x)) for x in xT_cores]
    res = run_bass_kernel_spmd(nc, in_maps, core_ids=list(range(len(xT_cores))))
    return [r["yT"] for r in res.results]


E2 = 2048


def build_sgu(NT, TT=256):
    c = Ctx()
    xT = c.dram_in("xT", [D, NT])
    gin = c.dram_in("g", [128, 8])
    w_in = c.dram_in("w_in", [D, 2 * E2])
    b_u = c.dram_in("b_u", [128, 16])
    b_v = c.dram_in("b_v", [1, E2])
    lng = c.dram_in("ln_g", [128, 16])
    lnb = c.dram_in("ln_b", [128, 16])
    wsT = c.dram_in("wsT", [128, 16, 128])
    b_s = c.dram_in("b_s", [16, 128])
    w_out = c.dram_in("w_out", [E2, D])
    yT = c.dram_out("yT", [D, NT])
    xv = xT.rearrange("(c p) n -> p c n", p=128)
    yv = yT.rearrange("(c p) n -> p c n", p=128)
    NB = TT // 128

    wis = c.sb("wis", [128, 8, 2 * E2], BF16)
    wos = c.sb("wos", [128, 16, D], BF16)
    wss = c.sb("wss", [128, 16, 128], BF16)
    gs = c.sb("gs", [128, 8], F32)
    bus = c.sb("bus", [128, 16], F32)
    bvs = c.sb("bvs", [1, E2], F32)
    lgs = c.sb("lgs", [128, 16], F32)
    lbs = c.sb("lbs", [128, 16], F32)
    bss = c.sb("bss", [128, 16, 128], F32)
    biasT = c.sb("biasT", [128, 16, TT], F32)
    ones = c.sb("ones", [128, 128], BF16)
    ones32 = c.sb("ones32", [1, 128], F32)
    wib, wob, wsb, constb, onesb, biasb = c.buf("wi"), c.buf("wo"), c.buf("ws"), c.buf("const"), c.buf("ones"), c.buf("bias")
    dc = c.S.dsem("const")
    for dst, src in ((gs, gin), (bus, b_u), (bvs, b_v), (lgs, lng), (lbs, lnb)):
        c.dma(dst[:, :], src, dc, writes=[constb])
    c.dma(bss[:, :, :], b_s.partition_broadcast(128), dc, writes=[constb])
    c.memset("pool", ones[:, :], 1.0, [onesb])
    c.memset("pool", ones32[:, :], 1.0, [onesb])
    dws = c.S.dsem("ws")
    c.dma(wss[:, :, :], wsT, dws, writes=[wsb], q="pool")
    c.memset("pool", wss[64:128, :, 0:64], 0.0, [wsb])

    NX = 2
    xs = [c.sb("x%d" % i, [128, 8, TT], F32) for i in range(NX)]
    xbs = [c.buf("x%d" % i) for i in range(NX)]
    dxl = [c.S.dsem("xl%d" % i) for i in range(NX)]
    dxs = [c.S.dsem("xs%d" % i) for i in range(NX)]
    ntile = NT // TT
    c.dma(xs[0][:, :, :], xv[:, :, 0:TT], dxl[0], writes=[xbs[0]])
    dwi, dwo = c.S.dsem("wi"), c.S.dsem("wo")
    load_weight_bf16(c, wis, w_in, dwi, wib, nsplit=4)
    load_weight_bf16(c, wos, w_out, dwo, wob, nsplit=2)

    sq = c.sb("sq", [128, 8, TT], BF16)
    h = c.sb("h", [128, 8, TT], BF16)
    u = c.sb("u", [128, 16, TT], F32)
    vg = c.sb("vg", [128, E2], F32)
    vn = [c.sb("vn%d" % i, [128, E2], BF16) for i in range(NB)]
    y = c.sb("y", [128, 16, TT], BF16)
    tmp = [c.sb("tmp%d" % i, [128, TT], F32) for i in range(2)]
    rstd = c.sb("rstd", [128, TT], F32)
    stats = c.sb("stats", [128, 4, 6], F32)
    mv = c.sb("mv", [128, 2], F32)
    sqb, hb, ub, vgb, yb, rstdb, statb, mvb = (c.buf(n) for n in ("sq", "h", "u", "vg", "y", "rstd", "stats", "mv"))
    vnb = [c.buf("vn%d" % i) for i in range(NB)]
    tmpb = [c.buf() for i in range(2)]
    NP = 4
    pss = [c.ps("ps%d" % i, [128, 512]) for i in range(NP)]
    psb = [c.buf("ps%d" % i) for i in range(NP)]
    psn = c.ps("psn", [128, 512])
    psnb = c.buf("psn")

    for g in range(16):
        c.mm(psn[:, :128], ones[:, :], wss[:, g, :], True, True, [onesb, wsb], [psnb])
        for tb in range(NB):
            c.stt(biasT[:, g, tb * 128:(tb + 1) * 128], psn[:, :128], lbs[:, g:g + 1], bss[:, g, :],
                  ALU.mult, ALU.add, [psnb, constb], [biasb])

    pi = 0
    for t in range(ntile):
        s = t % NX
        x, xb = xs[s], xbs[s]
        if t + 1 < ntile:
            s2 = (t + 1) % NX
            c.dma(xs[s2][:, :, :], xv[:, :, (t + 1) * TT:(t + 2) * TT], dxl[s2], writes=[xbs[s2]])
        emit_rmsnorm(c, x, xb, TT, sq, sqb, ones, onesb, psn, psnb, rstd, rstdb, gs, constb, h, hb)
        for m in range(16):
            ps, pb = pss[pi % NP], psb[pi % NP]
            pi += 1
            for k in range(8):
                c.mm(ps[:, :TT], wis[:, k, m * 128:(m + 1) * 128], h[:, k, :], k == 0, k == 7, [wib, hb], [pb])
            c.act(u[:, m, :], ps[:, :TT], AF.Gelu, [pb, constb], [ub], bias=bus[:, m:m + 1])
        for tb in range(NB):
            for nb in range(4):
                ps, pb = pss[pi % NP], psb[pi % NP]
                pi += 1
                for k in range(8):
                    c.mm(ps[:, :], h[:, k, tb * 128:(tb + 1) * 128], wis[:, k, E2 + nb * 512:E2 + (nb + 1) * 512],
                         k == 0, False, [wib, hb], [pb])
                c.mm(ps[:, :], ones32[:, :], bvs[:, nb * 512:(nb + 1) * 512], False, True, [onesb, constb], [pb])
                c.act(vg[:, nb * 512:(nb + 1) * 512], ps[:, :], AF.Gelu, [pb], [vgb])
                c.S.op("dve", lambda eng, o=stats[:, nb, :], i=vg[:, nb * 512:(nb + 1) * 512]: eng.bn_stats(o, i),
                       [vgb], [statb])
            c.S.op("dve", lambda eng, o=mv[:, :], i=stats[:, :, :].rearrange("p a b -> p (a b)"): eng.bn_aggr(o, i),
                   [statb], [mvb])
            c.act(mv[:, 1:2], mv[:, 1:2], AF.Sqrt, [mvb], [mvb], bias=1e-5, scale=1.0)
            c.recip(mv[:, 1:2], mv[:, 1:2], [mvb], [mvb])
            c.ts("dve", vn[tb][:, :], vg[:, :], mv[:, 0:1], mv[:, 1:2], ALU.subtract, ALU.mult, [vgb, mvb], [vnb[tb]])
        for g in range(16):
            ps, pb = pss[pi % NP], psb[pi % NP]
            pi += 1
            for tb in range(NB):
                c.mm(ps[:, tb * 128:(tb + 1) * 128], vn[tb][:, g * 128:(g + 1) * 128], wss[:, g, :], True, True,
                     [vnb[tb], wsb], [pb])
            tm, tmb = tmp[g % 2], tmpb[g % 2]
            c.stt(tm[:, :], ps[:, :TT], lgs[:, g:g + 1], biasT[:, g, :], ALU.mult, ALU.add, [pb, constb, biasb], [tmb])
            c.tt("pool", y[:, g, :], tm[:, :], u[:, g, :], ALU.mult, [tmb, ub], [yb])
        for n in range(8):
            ps, pb = pss[pi % NP], psb[pi % NP]
            pi += 1
            for m in range(16):
                c.mm(ps[:, :TT], wos[:, m, n * 128:(n + 1) * 128], y[:, m, :], m == 0, m == 15, [wob, yb], [pb])
            c.tt("dve", x[:, n, :], ps[:, :TT], x[:, n, :], ALU.add, [pb, xb], [xb])
        c.dma(yv[:, :, t * TT:(t + 1) * TT], x[:, :, :], dxs[s], reads=[xb])
    return c.finish(dxs)


def run_sgu(xT_cores, g, w_in, b_in, ln_g, ln_b, w_s, b_s, w_out):
    NT = xT_cores[0].shape[1]
    nc = get_prog(("sgu", NT), lambda: build_sgu(NT))
    f = np.float32
    base = {"g": to_pc(g), "w_in": np.ascontiguousarray(w_in, dtype=f),
            "b_u": to_pc(np.asarray(b_in)[:E2]), "b_v": np.ascontiguousarray(np.asarray(b_in, f)[None, E2:]),
            "ln_g": to_pc(ln_g), "ln_b": to_pc(ln_b),
            "wsT": np.ascontiguousarray(np.transpose(np.asarray(w_s, f), (2, 0, 1))),
            "b_s": np.ascontiguousarray(b_s, dtype=f), "w_out": np.ascontiguousarray(w_out, dtype=f)}
    in_maps = [dict(base, xT=np.ascontiguousarray(x)) for x in xT_cores]
    res = run_bass_kernel_spmd(nc, in_maps, core_ids=list(range(len(xT_cores))))
    return [r["yT"] for r in res.results]


C0 = float(np.exp(-0.5))
CH = 64


def build_rwkv_proj(NT, vmix, TT=256):
    c = Ctx()
    xT = c.dram_in("xT", [D, NT + 1])
    gin = c.dram_in("g", [128, 8])
    mu = c.dram_in("mu", [128, 6, 8])
    w_rkv = c.dram_in("w_rkv", [3, D, D])
    pcs = {n: c.dram_in(n, [128, 8]) for n in ("w0", "a0", "k_k", "k_a", "r_k") + (("v0",) if vmix else ())}
    w_la = c.dram_in("w_la", [D, 64])
    w_lb = c.dram_in("w_lb", [64, D])
    a_la = c.dram_in("a_la", [D, 64])
    a_lb = c.dram_in("a_lb", [64, D])
    g_la = c.dram_in("g_la", [D, 128])
    g_lb = c.dram_in("g_lb", [128, D])
    if vmix:
        v_la = c.dram_in("v_la", [D, 32])
        v_lb = c.dram_in("v_lb", [32, D])
        vfT = c.dram_in("vfT", [D, NT])
    onames = ["at", "rt", "bt", "kt", "bh", "kh", "vv"]
    outs = {n: c.dram_out(n, [D, NT], BF16) for n in onames}
    o_gl = c.dram_out("gl", [D, NT // CH])
    o_bonus = c.dram_out("bonusv", [D, NT])
    o_gate = c.dram_out("gate", [D, NT])
    if not vmix:
        o_vf = c.dram_out("vfirst", [D, NT])
    fm = lambda ap: ap.rearrange("(c p) n -> p c n", p=128)
    xv = fm(xT)
    NCK = TT // CH

    wr = c.sb("wr", [128, 3, 8, D], BF16)
    wla = c.sb("wla", [128, 8, 64], BF16)
    ala = c.sb("ala", [128, 8, 64], BF16)
    gla = c.sb("gla", [128, 8, 128], BF16)
    wlb = c.sb("wlb", [64, D], BF16)
    alb = c.sb("alb", [64, D], BF16)
    glb = c.sb("glb", [128, D], BF16)
    wb, constb, onesb = c.buf("w"), c.buf("const"), c.buf("ones")
    dw, dc = c.S.dsem("w"), c.S.dsem("const")
    gs = c.sb("gs", [128, 8], F32)
    mus = c.sb("mus", [128, 6, 8], F32)
    c.dma(gs[:, :], gin, dc, writes=[constb])
    c.dma(mus[:, :, :], mu, dc, writes=[constb])
    pc = {}
    for n, ap in pcs.items():
        pc[n] = c.sb("pc_" + n, [128, 8], F32)
        c.dma(pc[n][:, :], ap, dc, writes=[constb])
    pc["omka"] = c.sb("pc_omka", [128, 8], F32)
    c.ts("dve", pc["omka"][:, :], pc["k_a"][:, :], -1.0, 1.0, ALU.mult, ALU.add, [constb], [constb])
    NX = 2
    xs_ = [c.sb("x%d" % i, [128, 8, TT + 1], F32) for i in range(NX)]
    xbs = [c.buf("x%d" % i) for i in range(NX)]
    dxl = [c.S.dsem("xl%d" % i) for i in range(NX)]
    c.dma(xs_[0][:, :, :], xv[:, :, 0:TT + 1], dxl[0], writes=[xbs[0]])
    for i in range(3):
        c.dma(wr[:, i, :, :], w_rkv[i].rearrange("(c p) n -> p c n", p=128), dw, writes=[wb], q="pool")
    for dst, src in ((wla, w_la), (ala, a_la), (gla, g_la)):
        c.dma(dst[:, :, :], src.rearrange("(c p) n -> p c n", p=128), dw, writes=[wb], q="pool")
    for dst, src in ((wlb, w_lb), (alb, a_lb), (glb, g_lb)):
        c.dma(dst[:, :], src, dw, writes=[wb], q="pool")
    if vmix:
        vla = c.sb("vla", [128, 8, 32], BF16)
        vlb = c.sb("vlb", [32, D], BF16)
        c.dma(vla[:, :, :], v_la.rearrange("(c p) n -> p c n", p=128), dw, writes=[wb], q="pool")
        c.dma(vlb[:, :], v_lb, dw, writes=[wb], q="pool")
    ones = c.sb("ones", [128, 128], BF16)
    bd = c.sb("bd", [128, 128], F32)
    mreset = c.sb("mreset", [128, TT], F32)
    c.memset("pool", ones[:, :], 1.0, [onesb])
    c.memset("pool", bd[:, :], 0.0, [onesb])
    c.memset("pool", bd[0:64, 0:64], 1.0, [onesb])
    c.memset("pool", bd[64:128, 64:128], 1.0, [onesb])
    c.memset("pool", mreset[:, :], 1.0, [onesb])
    c.memset("pool", mreset[:, :].rearrange("p (c t) -> p c t", t=CH)[:, :, 0:1], 0.0, [onesb])

    TE = TT + 1
    sq = c.sb("sq", [128, 8, TE], BF16)
    h32 = c.sb("h32", [128, 8, TE], F32)
    xx = c.sb("xx", [128, 8, TT], F32)
    rstd = c.sb("rstd", [128, TE], F32)
    xs6 = [c.sb("xs%d" % i, [128, 8, TT], BF16) for i in range(6)]
    sqb, hb, xxb, rstdb = c.buf("sq"), c.buf("h32"), c.buf("xx"), c.buf("rstd")
    xs6b = [c.buf("xs%d" % i) for i in range(6)]
    t_w = c.sb("t_w", [64, TT], BF16)
    t_a = c.sb("t_a", [64, TT], BF16)
    t_g = c.sb("t_g", [128, TT], BF16)
    t_wb, t_ab, t_gb = c.buf("t_w"), c.buf("t_a"), c.buf("t_g")
    if vmix:
        t_v = c.sb("t_v", [32, TT], BF16)
        t_vb = c.buf("t_v")
        vf = c.sb("vf", [128, 8, TT], F32)
        vfb = c.buf("vf")
        dvf = c.S.dsem("vf")
    tn = ["r32", "k32", "v32", "sg", "a", "cs", "e1", "e2", "gin_", "gex", "ginv", "gsuf", "kk0", "sqk", "nrm",
          "kk", "tk", "k", "b", "rkr", "sv", "d"]
    T = {n: c.sb("T_" + n, [128, TT], F32) for n in tn}
    Tb = {n: c.buf("T_" + n) for n in tn}
    stg = {n: c.sb("S_" + n, [128, 8, TT], BF16) for n in onames}
    stgb = {n: c.buf("S_" + n) for n in onames}
    dst_ = {n: c.S.dsem("o_" + n) for n in onames}
    s_gl = c.sb("s_gl", [128, 8, NCK], F32)
    s_bonus = c.sb("s_bonus", [128, 8, TT], F32)
    s_gate = c.sb("s_gate", [128, 8, TT], F32)
    s_glb, s_bonusb, s_gateb = c.buf("s_gl"), c.buf("s_bonus"), c.buf("s_gate")
    d_gl, d_bonus, d_gate = c.S.dsem("o_gl"), c.S.dsem("o_bonus"), c.S.dsem("o_gate")
    if not vmix:
        s_vf = c.sb("s_vf", [128, 8, TT], F32)
        s_vfb = c.buf("s_vf")
        d_vf = c.S.dsem("o_vf")
    psn = c.ps("psn", [128, 512])
    psnb = c.buf("psn")
    pnames = ["r", "k", "v", "w", "a", "g", "vm", "tw", "ta", "tg", "tv", "n2", "rk"]
    banks = [c.ps("bank%d" % i, [128, 512]) for i in range(7)]
    P = {n: banks[i // 2][:, (i % 2) * 256:(i % 2) * 256 + TT] for i, n in enumerate(pnames)}
    Pb = {n: c.buf("P_" + n) for n in pnames}

    ntile = NT // TT
    for t in range(ntile):
        s = t % NX
        x, xb = xs_[s], xbs[s]
        if t + 1 < ntile:
            s2 = (t + 1) % NX
            c.dma(xs_[s2][:, :, :], xv[:, :, (t + 1) * TT:(t + 1) * TT + TE], dxl[s2], writes=[xbs[s2]])
        if vmix:
            c.dma(vf[:, :, :], fm(vfT)[:, :, t * TT:(t + 1) * TT], dvf, writes=[vfb])
        emit_rmsnorm(c, x, xb, TE, sq, sqb, ones, onesb, psn, psnb, rstd, rstdb, gs, constb, h32, hb)
        for k in range(8):
            c.tt("pool", xx[:, k, :], h32[:, k, 0:TT], h32[:, k, 1:TE], ALU.subtract, [hb], [xxb])
        for i in range(6):
            for k in range(8):
                c.stt(xs6[i][:, k, :], xx[:, k, :], mus[:, i, k:k + 1], h32[:, k, 1:TE], ALU.mult, ALU.add,
                      [xxb, hb, constb], [xs6b[i]])
        for k in range(8):
            c.mm(P["tw"][0:64, :], wla[:, k, :], xs6[3][:, k, :], k == 0, k == 7, [wb, xs6b[3]], [Pb["tw"]])
        c.act(t_w[:, :], P["tw"][0:64, :], AF.Tanh, [Pb["tw"]], [t_wb])
        for k in range(8):
            c.mm(P["ta"][0:64, :], ala[:, k, :], xs6[4][:, k, :], k == 0, k == 7, [wb, xs6b[4]], [Pb["ta"]])
        c.copy("dve", t_a[:, :], P["ta"][0:64, :], [Pb["ta"]], [t_ab])
        for k in range(8):
            c.mm(P["tg"][:, :], gla[:, k, :], xs6[5][:, k, :], k == 0, k == 7, [wb, xs6b[5]], [Pb["tg"]])
        c.act(t_g[:, :], P["tg"][:, :], AF.Sigmoid, [Pb["tg"]], [t_gb])
        if vmix:
            for k in range(8):
                c.mm(P["tv"][0:32, :], vla[:, k, :], xs6[2][:, k, :], k == 0, k == 7, [wb, xs6b[2]], [Pb["tv"]])
            c.copy("dve", t_v[:, :], P["tv"][0:32, :], [Pb["tv"]], [t_vb])
        for m in range(8):
            ms = slice(m * 128, (m + 1) * 128)
            col = lambda n: pc[n][:, m:m + 1]
            for i, n in enumerate(("r", "k", "v")):
                for k in range(8):
                    c.mm(P[n][:, :], wr[:, i, k, ms], xs6[i][:, k, :], k == 0, k == 7, [wb, xs6b[i]], [Pb[n]])
            c.mm(P["w"][:, :], wlb[:, ms], t_w[:, :], True, True, [wb, t_wb], [Pb["w"]])
            c.mm(P["a"][:, :], alb[:, ms], t_a[:, :], True, True, [wb, t_ab], [Pb["a"]])
            c.mm(P["g"][:, :], glb[:, ms], t_g[:, :], True, True, [wb, t_gb], [Pb["g"]])
            if vmix:
                c.mm(P["vm"][:, :], vlb[:, ms], t_v[:, :], True, True, [wb, t_vb], [Pb["vm"]])
            c.act(T["sg"][:, :], P["w"][:, :], AF.Sigmoid, [Pb["w"], constb], [Tb["sg"]], bias=col("w0"))
            c.act(T["a"][:, :], P["a"][:, :], AF.Sigmoid, [Pb["a"], constb], [Tb["a"]], bias=col("a0"))
            if vmix:
                c.act(T["sv"][:, :], P["vm"][:, :], AF.Sigmoid, [Pb["vm"], constb], [Tb["sv"]], bias=col("v0"))
            c.copy("act", T["r32"][:, :], P["r"][:, :], [Pb["r"]], [Tb["r32"]])
            c.copy("act", T["k32"][:, :], P["k"][:, :], [Pb["k"]], [Tb["k32"]])
            c.copy("act", T["v32"][:, :], P["v"][:, :], [Pb["v"]], [Tb["v32"]])
            c.copy("act", s_gate[:, m, :], P["g"][:, :], [Pb["g"]], [s_gateb])
            if vmix:
                c.tt("pool", T["d"][:, :], vf[:, m, :], T["v32"][:, :], ALU.subtract, [vfb, Tb["v32"]], [Tb["d"]])
                c.tt("pool", T["d"][:, :], T["d"][:, :], T["sv"][:, :], ALU.mult, [Tb["d"], Tb["sv"]], [Tb["d"]])
                c.tt("pool", T["v32"][:, :], T["v32"][:, :], T["d"][:, :], ALU.add, [Tb["v32"], Tb["d"]], [Tb["v32"]])
            else:
                c.copy("pool", s_vf[:, m, :], T["v32"][:, :], [Tb["v32"]], [s_vfb])
            c.S.op("dve", lambda eng, o=T["cs"][:, :], a=mreset[:, :], b=T["sg"][:, :]:
                   eng.tensor_tensor_scan(o, a, b, 0.0, ALU.mult, ALU.add), [onesb, Tb["sg"]], [Tb["cs"]])
            cs3 = T["cs"][:, :].rearrange("p (c t) -> p c t", t=CH)
            c.tt("pool", T["e1"][:, :], T["cs"][:, :], T["sg"][:, :], ALU.subtract, [Tb["cs"], Tb["sg"]], [Tb["e1"]])
            c.tt("pool", T["e2"][:, :].rearrange("p (c t) -> p c t", t=CH), cs3[:, :, CH - 1:CH].to_broadcast([128, NCK, CH]),
                 cs3, ALU.subtract, [Tb["cs"]], [Tb["e2"]])
            c.act(T["gin_"][:, :], T["cs"][:, :], AF.Exp, [Tb["cs"]], [Tb["gin_"]], scale=-C0)
            c.act(T["ginv"][:, :], T["cs"][:, :], AF.Exp, [Tb["cs"]], [Tb["ginv"]], scale=C0)
            c.act(T["gex"][:, :], T["e1"][:, :], AF.Exp, [Tb["e1"]], [Tb["gex"]], scale=-C0)
            c.act(T["gsuf"][:, :], T["e2"][:, :], AF.Exp, [Tb["e2"]], [Tb["gsuf"]], scale=-C0)
            c.act(s_gl[:, m, :], cs3[:, :, CH - 1], AF.Exp, [Tb["cs"]], [s_glb], scale=-C0)
            c.ts("pool", T["kk0"][:, :], T["k32"][:, :], col("k_k"), None, ALU.mult, None, [Tb["k32"], constb], [Tb["kk0"]])
            c.tt("pool", T["sqk"][:, :], T["kk0"][:, :], T["kk0"][:, :], ALU.mult, [Tb["kk0"]], [Tb["sqk"]])
            c.mm(P["n2"][:, :], bd[:, :], T["sqk"][:, :], True, True, [onesb, Tb["sqk"]], [Pb["n2"]])
            c.act(T["nrm"][:, :], P["n2"][:, :], AF.Sqrt, [Pb["n2"]], [Tb["nrm"]])
            c.ts("dve", T["nrm"][:, :], T["nrm"][:, :], 1e-12, None, ALU.max, None, [Tb["nrm"]], [Tb["nrm"]])
            c.recip(T["nrm"][:, :], T["nrm"][:, :], [Tb["nrm"]], [Tb["nrm"]])
            c.tt("dve", T["kk"][:, :], T["kk0"][:, :], T["nrm"][:, :], ALU.mult, [Tb["kk0"], Tb["nrm"]], [Tb["kk"]])
            c.ts("dve", T["tk"][:, :], T["a"][:, :], col("k_a"), col("omka"), ALU.mult, ALU.add, [Tb["a"], constb], [Tb["tk"]])
            c.tt("dve", T["k"][:, :], T["k32"][:, :], T["tk"][:, :], ALU.mult, [Tb["k32"], Tb["tk"]], [Tb["k"]])
            c.tt("pool", T["b"][:, :], T["kk"][:, :], T["a"][:, :], ALU.mult, [Tb["kk"], Tb["a"]], [Tb["b"]])
            c.stt(stg["at"][:, m, :], T["kk"][:, :], -1.0, T["gex"][:, :], ALU.mult, ALU.mult, [Tb["kk"], Tb["gex"]], [stgb["at"]])
            c.tt("pool", stg["rt"][:, m, :], T["r32"][:, :], T["gin_"][:, :], ALU.mult, [Tb["r32"], Tb["gin_"]], [stgb["rt"]])
            c.tt("dve", stg["bt"][:, m, :], T["b"][:, :], T["ginv"][:, :], ALU.mult, [Tb["b"], Tb["ginv"]], [stgb["bt"]])
            c.tt("pool", stg["kt"][:, m, :], T["k"][:, :], T["ginv"][:, :], ALU.mult, [Tb["k"], Tb["ginv"]], [stgb["kt"]])
            c.tt("dve", stg["bh"][:, m, :], T["b"][:, :], T["gsuf"][:, :], ALU.mult, [Tb["b"], Tb["gsuf"]], [stgb["bh"]])
            c.tt("pool", stg["kh"][:, m, :], T["k"][:, :], T["gsuf"][:, :], ALU.mult, [Tb["k"], Tb["gsuf"]], [stgb["kh"]])
            c.copy("pool", stg["vv"][:, m, :], T["v32"][:, :], [Tb["v32"]], [stgb["vv"]])
            c.stt(T["rkr"][:, :], T["r32"][:, :], col("r_k"), T["k"][:, :], ALU.mult, ALU.mult, [Tb["r32"], Tb["k"], constb], [Tb["rkr"]])
            c.mm(P["rk"][:, :], bd[:, :], T["rkr"][:, :], True, True, [onesb, Tb["rkr"]], [Pb["rk"]])
            c.tt("dve", s_bonus[:, m, :], P["rk"][:, :], T["v32"][:, :], ALU.mult, [Pb["rk"], Tb["v32"]], [s_bonusb])
        tsl = slice(t * TT, (t + 1) * TT)
        for n in onames:
            c.dma(fm(outs[n])[:, :, tsl], stg[n][:, :, :], dst_[n], reads=[stgb[n]])
        c.dma(fm(o_gl)[:, :, t * NCK:(t + 1) * NCK], s_gl[:, :, :], d_gl, reads=[s_glb])
        c.dma(fm(o_bonus)[:, :, tsl], s_bonus[:, :, :], d_bonus, reads=[s_bonusb])
        c.dma(fm(o_gate)[:, :, tsl], s_gate[:, :, :], d_gate, reads=[s_gateb])
        if not vmix:
            c.dma(fm(o_vf)[:, :, tsl], s_vf[:, :, :], d_vf, reads=[s_vfb])
    return c.finish(list(dst_.values()) + [d_gl, d_bonus, d_gate] + ([] if vmix else [d_vf]))


def run_rwkv_proj(xT_ext_cores, p, vfT_cores=None):
    NT = xT_ext_cores[0].shape[1] - 1
    vmix = vfT_cores is not None
    nc = get_prog(("rproj", NT, vmix), lambda: build_rwkv_proj(NT, vmix))
    f = np.float32
    A = lambda a: np.ascontiguousarray(a, dtype=f)
    base = {"g": to_pc(p["g"]), "mu": np.ascontiguousarray(np.stack([to_pc(p["mu"][i]) for i in range(6)], axis=1)),
            "w_rkv": A(p["w_rkv"]), "w0": to_pc(p["w0"]), "a0": to_pc(p["a0"]), "k_k": to_pc(p["k_k"]),
            "k_a": to_pc(p["k_a"]), "r_k": to_pc(np.asarray(p["r_k"]).reshape(-1)),
            "w_la": A(p["w_la"]), "w_lb": A(p["w_lb"]), "a_la": A(p["a_la"]), "a_lb": A(p["a_lb"]),
            "g_la": A(p["g_la"]), "g_lb": A(p["g_lb"])}
    if vmix:
        base.update({"v0": to_pc(p["v0"]), "v_la": A(p["v_la"]), "v_lb": A(p["v_lb"])})
    in_maps = []
    for i, x in enumerate(xT_ext_cores):
        m = dict(base, xT=np.ascontiguousarray(x))
        if vmix:
            m["vfT"] = np.ascontiguousarray(vfT_cores[i])
        in_maps.append(m)
    res = run_bass_kernel_spmd(nc, in_maps, core_ids=list(range(len(in_maps))))
    return res.results


def build_rwkv_scan(T, NI=2):
    c = Ctx()
    NB = T // CH
    G = 4
    FMh = c.dram_in("FM", [NB, 64, G, 256], BF16)
    ABh = c.dram_in("AB", [NB, 64, G, 2, 64], BF16)
    VKh = c.dram_in("VK", [NB, 64, G, 2, 64], BF16)
    GLh = c.dram_in("GL", [64, NB * G])
    CMh = c.dram_in("CM", [128, 4, 64])
    yT = c.dram_out("yT", [G, 64, T])
    yv = yT.rearrange("h v t -> v h t")

    cm = c.sb("cm", [128, 4, 64], F32)
    gl = c.sb("gl", [64, NB * G], F32)
    constb = c.buf("const")
    dc = c.S.dsem("const")
    c.dma(cm[:, :, :], CMh, dc, writes=[constb])
    c.dma(gl[:, :], GLh, dc, writes=[constb])

    NSET = NI + 1
    sets = []
    for i in range(NSET):
        st = {}
        st["W"] = c.sb("W%d" % i, [64, G, 640], BF16)
        st["B"] = c.sb("B%d" % i, [128, G, 192], BF16)
        st["ZTt"] = [c.sb("ZTt%d_%d" % (i, j), [64, G, 128], BF16) for j in range(2)]
        st["ZT"] = [c.sb("ZT%d_%d" % (i, j), [64, G, 64], BF16) for j in range(2)]
        st["MT"] = c.sb("MT%d" % i, [64, G, 64], BF16)
        st["AN"] = c.sb("AN%d" % i, [64, G, 128], BF16)
        st["PW"] = c.sb("PW%d" % i, [128, G, 128], BF16)
        st["YT"] = c.sb("YT%d" % i, [64, G, 64], F32)
        for n in ("Wd", "Wa", "Wc", "Bd", "Bs", "Bp", "Bb", "ZTt0", "ZTt1", "ZT0", "ZT1", "MT", "AN", "PW", "YT"):
            st["b_" + n] = c.buf(n + str(i))
        st["dW"] = c.S.dsem("W%d" % i)
        st["dB"] = c.S.dsem("B%d" % i)
        st["dY"] = c.S.dsem("Y%d" % i)
        sets.append(st)
    NPS = min(NI, 2)
    pset = []
    for i in range(NPS):
        p = {"P1": c.ps("P1_%d" % i, [128, G, 128]), "PA": c.ps("PA_%d" % i, [64, G, 128]),
             "PB": c.ps("PB_%d" % i, [64, G, 128])}
        for n in ("P1", "PA", "PB"):
            p["b_" + n] = c.buf(n + str(i))
        pset.append(p)
    PSY = [c.ps("PSS", [64, G, 128]), c.ps("PSY", [64, G, 128])]
    psyb = [c.buf("pss"), c.buf("psy")]

    c.memset("pool", sets[0]["B"][0:64, :, 0:64], 0.0, [sets[0]["b_Bs"]])

    def batch(bi):
        st = sets[bi % NSET]
        nx = sets[(bi + 1) % NSET]
        p = pset[bi % NPS]
        W, B = st["W"], st["B"]
        c.dma(W[:, :, 0:256], FMh[bi], st["dW"], writes=[st["b_Wd"]])
        Wv = W[:, :, 256:640].rearrange("p g (a b) -> p g a b", b=64)
        c.dma(W[:, :, 320:384], ABh[bi][:, :, 0, :], st["dW"], writes=[st["b_Wd"]])
        c.dma(W[:, :, 576:640], ABh[bi][:, :, 1, :], st["dW"], writes=[st["b_Wd"]])
        Bv = B[64:128, :, :].rearrange("p g (a b) -> p g a b", b=64)
        c.dma(B[64:128, :, 0:64], VKh[bi][:, :, 0, :], st["dB"], writes=[st["b_Bd"]])
        c.dma(B[64:128, :, 128:192], VKh[bi][:, :, 1, :], st["dB"], writes=[st["b_Bd"]])
        c.copy("pool", B[0:64, :, 64:128], W[:, :, 192:256], [st["b_Wd"]], [st["b_Bp"]])
        c.tt("pool", B[0:64, :, 128:192], cm[0:64, 3:4, :].to_broadcast([64, G, 64]),
             gl[:, bi * G:(bi + 1) * G].unsqueeze(2).to_broadcast([64, G, 64]), ALU.mult, [constb], [st["b_Bp"]])
        yield
        for g in range(G):
            c.mm(p["P1"][:, g, :], W[:, g, 0:128], W[:, g, 128:256], True, True, [st["b_Wd"]], [p["b_P1"]])
        for g in range(G):
            c.mm(p["PA"][:, g, :], W[:, g, 128:192], W[:, g, 0:128], True, True, [st["b_Wd"]], [p["b_PA"]])
        c.tt("dve", W[:, :, 448:576], p["P1"][0:64, :, :],
             cm[0:64, 0:2, :].rearrange("p a b -> p (a b)").unsqueeze(1).to_broadcast([64, G, 128]), ALU.mult,
             [p["b_P1"], constb], [st["b_Wa"]])
        c.tt("dve", B[64:128, :, 64:128], p["P1"][64:128, :, 64:128], cm[64:128, 1:2, :].to_broadcast([64, G, 64]),
             ALU.mult, [p["b_P1"], constb], [st["b_Bb"]])
        c.tt("dve", Wv[:, :, 0:3:2, :], p["PA"][:, :, :].rearrange("p g (a b) -> p g a b", b=64),
             cm[0:64, 2:3, :].unsqueeze(1).to_broadcast([64, G, 2, 64]), ALU.mult, [p["b_PA"], constb], [st["b_Wc"]])
        c.tt("pool", st["ZTt"][1][:, :, 64:128], W[:, :, 448:512], cm[0:64, 3:4, :].to_broadcast([64, G, 64]), ALU.add,
             [st["b_Wa"], constb], [st["b_ZTt1"]])
        yield
        for g in range(G):
            c.mm(p["PA"][:, g, 0:64], W[:, g, 256:320], W[:, g, 448:512], True, True, [st["b_Wc"], st["b_Wa"]], [p["b_PA"]])
        for g in range(G):
            c.mm(p["PB"][:, g, 0:64], W[:, g, 448:512], W[:, g, 256:320], True, True, [st["b_Wc"], st["b_Wa"]], [p["b_PB"]])
        c.copy("act", st["ZTt"][1][:, :, 0:64], p["PA"][:, :, 0:64], [p["b_PA"]], [st["b_ZTt1"]])
        c.copy("dve", st["ZT"][1][:, :, :], p["PB"][:, :, 0:64], [p["b_PB"]], [st["b_ZT1"]])
        yield
        for j in range(1, 6):
            cur, nxt = j % 2, (j + 1) % 2
            ZTt, ZT = st["ZTt"][cur], st["ZT"][cur]
            bz, bzt = st["b_ZTt%d" % cur], st["b_ZT%d" % cur]
            nz, nzt = st["b_ZTt%d" % nxt], st["b_ZT%d" % nxt]
            last = j == 5
            for g in range(G):
                if j <= 3:
                    c.mm(p["PA"][:, g, :], ZT[:, g, :], ZTt[:, g, :], True, True, [bz, bzt], [p["b_PA"]])
                else:
                    c.mm(p["PA"][:, g, 64:128], ZT[:, g, :], ZTt[:, g, 64:128], True, True, [bz, bzt], [p["b_PA"]])
            if j <= 4:
                for g in range(G):
                    c.mm(p["PB"][:, g, 0:64], ZTt[:, g, 0:64], ZT[:, g, :], True, True, [bz, bzt], [p["b_PB"]])
            if j <= 3:
                c.copy("act", st["ZTt"][nxt][:, :, 0:64], p["PA"][:, :, 0:64], [p["b_PA"]], [nz])
            if last:
                c.tt("dve", st["MT"][:, :, :], p["PA"][:, :, 64:128], ZTt[:, :, 64:128], ALU.add, [p["b_PA"], bz], [st["b_MT"]])
            else:
                c.tt("dve", st["ZTt"][nxt][:, :, 64:128], p["PA"][:, :, 64:128], ZTt[:, :, 64:128], ALU.add, [p["b_PA"], bz], [nz])
            if j <= 4:
                c.copy("act", st["ZT"][nxt][:, :, :], p["PB"][:, :, 0:64], [p["b_PB"]], [nzt])
            yield
        for g in range(G):
            c.mm(p["PA"][:, g, :], st["MT"][:, g, :], W[:, g, 320:448], True, True, [st["b_MT"], st["b_Wd"], st["b_Wc"]], [p["b_PA"]])
        c.copy("act", st["AN"][:, :, :], p["PA"][:, :, :], [p["b_PA"]], [st["b_AN"]])
        yield
        for g in range(G):
            c.mm(p["P1"][:, g, :], st["AN"][:, g, :], W[:, g, 512:640], True, True, [st["b_AN"], st["b_Wa"], st["b_Wd"]], [p["b_P1"]])
        c.tt("dve", st["PW"][:, :, :], p["P1"][:, :, :], B[:, :, 64:192], ALU.add,
             [p["b_P1"], st["b_Bp"], st["b_Bb"], st["b_Bd"]], [st["b_PW"]])
        yield
        for g in range(G):
            c.mm(PSY[0][:, g, 0:64], st["PW"][:, g, 64:128], B[:, g, 0:64], True, True, [st["b_PW"], st["b_Bs"], st["b_Bd"]], [psyb[0]])
        for g in range(G):
            c.mm(PSY[1][:, g, 0:64], B[:, g, 0:64], st["PW"][:, g, 0:64], True, True, [st["b_PW"], st["b_Bs"], st["b_Bd"]], [psyb[1]])
        if bi + 1 < NB:
            c.copy("act", nx["B"][0:64, :, 0:64], PSY[0][:, :, 0:64], [psyb[0]], [nx["b_Bs"]])
        c.copy("dve", st["YT"][:, :, :], PSY[1][:, :, 0:64], [psyb[1]], [st["b_YT"]])
        c.dma(yv[:, :, bi * CH:(bi + 1) * CH], st["YT"][:, :, :], st["dY"], reads=[st["b_YT"]])
        yield

    gens = []
    nxt_b = 0
    while nxt_b < NB or gens:
        if len(gens) < NI and nxt_b < NB:
            gens.append(batch(nxt_b))
            nxt_b += 1
        alive = []
        for gnr in gens:
            try:
                next(gnr)
                alive.append(gnr)
            except StopIteration:
                pass
        gens = alive
    return c.finish([st["dY"] for st in sets])


def scan_consts():
    i = np.arange(64)
    mu_s = (i[:, None] < i[None, :]).astype(np.float32)
    mu_i = (i[:, None] <= i[None, :]).astype(np.float32)
    ml_s = (i[None, :] < i[:, None]).astype(np.float32)
    ident = np.eye(64, dtype=np.float32)
    cmh = np.stack([mu_s, mu_i, ml_s, ident], axis=1)
    return np.ascontiguousarray(np.concatenate([cmh, cmh], axis=0))


def prep_scan_core(at, rt, bt, kt, bh, kh, vv, gl):
    T = at.shape[1]
    NB = T // CH
    fmr = lambda a: a.reshape(4, 64, NB, 64)
    FM = np.stack([fmr(bt), fmr(kt), fmr(at), fmr(rt)], axis=0)
    FM = np.ascontiguousarray(FM.transpose(3, 2, 1, 0, 4)).reshape(NB, 64, 4, 256)
    tm = lambda a: a.reshape(4, 64, NB, 64).transpose(2, 3, 0, 1)
    AB = np.ascontiguousarray(np.stack([tm(at), tm(bh)], axis=3))
    VK = np.ascontiguousarray(np.stack([tm(vv), tm(kh)], axis=3))
    GL = np.ascontiguousarray(gl.reshape(4, 64, NB).transpose(1, 2, 0)).reshape(64, NB * 4)
    return {"FM": FM, "AB": AB, "VK": VK, "GL": GL.astype(np.float32), "CM": scan_consts()}


def run_rwkv_scan(core_inputs, NI=2):
    T = core_inputs[0]["FM"].shape[0] * CH
    nc = get_prog(("rscan", T, NI), lambda: build_rwkv_scan(T, NI))
    res = run_bass_kernel_spmd(nc, core_inputs, core_ids=list(range(len(core_inputs))))
    return [r["yT"] for r in res.results]


def build_rwkv_out(NT, TT=256):
    c = Ctx()
    xT = c.dram_in("xT", [D, NT])
    yTi = c.dram_in("yin", [D, NT])
    bon = c.dram_in("bonusv", [D, NT])
    gate = c.dram_in("gate", [D, NT])
    lng = c.dram_in("ln_g", [128, 8])
    lnb = c.dram_in("ln_b", [128, 8])
    w_o = c.dram_in("w_o", [D, D])
    yT = c.dram_out("yT", [D, NT])
    fm = lambda ap: ap.rearrange("(c p) n -> p c n", p=128)

    wos = c.sb("wos", [128, 8, D], BF16)
    lgs = c.sb("lgs", [128, 8], F32)
    lbs = c.sb("lbs", [128, 8], F32)
    bd = c.sb("bd", [128, 128], F32)
    wob, constb, onesb = c.buf("wo"), c.buf("const"), c.buf("ones")
    dc, dwo = c.S.dsem("const"), c.S.dsem("wo")
    c.dma(lgs[:, :], lng, dc, writes=[constb])
    c.dma(lbs[:, :], lnb, dc, writes=[constb])
    c.memset("pool", bd[:, :], 0.0, [onesb])
    c.memset("pool", bd[0:64, 0:64], 1.0, [onesb])
    c.memset("pool", bd[64:128, 64:128], 1.0, [onesb])
    NX = 2
    names = ("x", "y", "bo", "ga")
    srcs = {"x": fm(xT), "y": fm(yTi), "bo": fm(bon), "ga": fm(gate)}
    tiles = {n: [c.sb("%s%d" % (n, i), [128, 8, TT], F32) for i in range(NX)] for n in names}
    tb = {n: [c.buf("%s%d" % (n, i)) for i in range(NX)] for n in names}
    dl = {n: [c.S.dsem("l%s%d" % (n, i)) for i in range(NX)] for n in names}
    dxs = [c.S.dsem("xs%d" % i) for i in range(NX)]
    ntile = NT // TT

    def load(t):
        s = t % NX
        for n in names:
            c.dma(tiles[n][s][:, :, :], srcs[n][:, :, t * TT:(t + 1) * TT], dl[n][s], writes=[tb[n][s]])
    load(0)
    load_weight_bf16(c, wos, w_o, dwo, wob, nsplit=2)
    o = c.sb("o", [128, 8, TT], BF16)
    ob = c.buf("o")
    tn = ["ysq", "mean", "m2", "var", "d", "o1"]
    Tt = {n: c.sb("T_" + n, [128, TT], F32) for n in tn}
    Tb = {n: c.buf("T_" + n) for n in tn}
    pm = [c.ps("pm%d" % i, [128, 512]) for i in range(2)]
    pq = [c.ps("pq%d" % i, [128, 512]) for i in range(2)]
    pmb = [c.buf() for i in range(2)]
    pqb = [c.buf() for i in range(2)]
    NP = 3
    pss = [c.ps("ps%d" % i, [128, 512]) for i in range(NP)]
    psb = [c.buf("ps%d" % i) for i in range(NP)]
    pi = 0
    for t in range(ntile):
        s = t % NX
        if t + 1 < ntile:
            load(t + 1)
        x, y, bo, ga = (tiles[n][s] for n in names)
        xb, yb, bob, gab = (tb[n][s] for n in names)
        for m in range(8):
            P1, P1b, P2, P2b = pm[m % 2], pmb[m % 2], pq[m % 2], pqb[m % 2]
            c.tt("pool", Tt["ysq"][:, :], y[:, m, :], y[:, m, :], ALU.mult, [yb], [Tb["ysq"]])
            c.mm(P1[:, :TT], bd[:, :], y[:, m, :], True, True, [onesb, yb], [P1b])
            c.mm(P2[:, :TT], bd[:, :], Tt["ysq"][:, :], True, True, [onesb, Tb["ysq"]], [P2b])
            c.act(Tt["mean"][:, :], P1[:, :TT], AF.Copy, [P1b], [Tb["mean"]], scale=1.0 / 64)
            c.tt("pool", Tt["m2"][:, :], Tt["mean"][:, :], Tt["mean"][:, :], ALU.mult, [Tb["mean"]], [Tb["m2"]])
            c.stt(Tt["var"][:, :], P2[:, :TT], 1.0 / 64, Tt["m2"][:, :], ALU.mult, ALU.subtract, [P2b, Tb["m2"]], [Tb["var"]])
            c.act(Tt["var"][:, :], Tt["var"][:, :], AF.Sqrt, [Tb["var"]], [Tb["var"]], bias=64e-5, scale=1.0)
            c.recip(Tt["var"][:, :], Tt["var"][:, :], [Tb["var"]], [Tb["var"]])
            c.tt("pool", Tt["d"][:, :], y[:, m, :], Tt["mean"][:, :], ALU.subtract, [yb, Tb["mean"]], [Tb["d"]])
            c.tt("dve", Tt["d"][:, :], Tt["d"][:, :], Tt["var"][:, :], ALU.mult, [Tb["d"], Tb["var"]], [Tb["d"]])
            c.ts("dve", Tt["o1"][:, :], Tt["d"][:, :], lgs[:, m:m + 1], lbs[:, m:m + 1], ALU.mult, ALU.add, [Tb["d"], constb], [Tb["o1"]])
            c.tt("pool", Tt["o1"][:, :], Tt["o1"][:, :], bo[:, m, :], ALU.add, [Tb["o1"], bob], [Tb["o1"]])
            c.tt("dve", o[:, m, :], Tt["o1"][:, :], ga[:, m, :], ALU.mult, [Tb["o1"], gab], [ob])
        for n in range(8):
            ps, pb = pss[pi % NP], psb[pi % NP]
            pi += 1
            for m in range(8):
                c.mm(ps[:, :TT], wos[:, m, n * 128:(n + 1) * 128], o[:, m, :], m == 0, m == 7, [wob, ob], [pb])
            c.tt("dve", x[:, n, :], ps[:, :TT], x[:, n, :], ALU.add, [pb, xb], [xb])
        c.dma(fm(yT)[:, :, t * TT:(t + 1) * TT], x[:, :, :], dxs[s], reads=[xb])
    return c.finish(dxs)


def run_rwkv_out(cores, ln_g, ln_b, w_o):
    NT = cores[0]["xT"].shape[1]
    nc = get_prog(("rout", NT), lambda: build_rwkv_out(NT))
    base = {"ln_g": to_pc(ln_g), "ln_b": to_pc(ln_b), "w_o": np.ascontiguousarray(w_o, dtype=np.float32)}
    in_maps = [dict(base, **{k: np.ascontiguousarray(v) for k, v in cm.items()}) for cm in cores]
    res = run_bass_kernel_spmd(nc, in_maps, core_ids=list(range(len(in_maps))))
    return [r["yT"] for r in res.results]


def rwkv_layer(xT_tok, tok_cores, B, T, p, vfirst_tok=None):
    NT = xT_tok[0].shape[1]
    ext = []
    for i, (b, t0) in enumerate(tok_cores):
        if t0 == 0:
            prev = np.zeros((D, 1), np.float32)
        else:
            j = tok_cores.index((b, t0 - NT))
            prev = xT_tok[j][:, -1:]
        ext.append(np.concatenate([prev, xT_tok[i]], axis=1))
    pr = run_rwkv_proj(ext, p, vfirst_tok)
    names = ("at", "rt", "bt", "kt", "bh", "kh", "vv", "gl")
    full = {}
    for n in names:
        w = NT // CH if n == "gl" else NT
        arr = np.empty((B, D, (T // CH) if n == "gl" else T), dtype=pr[0][n].dtype)
        for i, (b, t0) in enumerate(tok_cores):
            o0 = (t0 // CH) if n == "gl" else t0
            arr[b, :, o0:o0 + w] = pr[i][n]
        full[n] = arr
    scan_cores = [(b, hg) for b in range(B) for hg in range(4)]
    y_full = np.empty((B, D, T), np.float32)
    for s0 in range(0, len(scan_cores), NCORES):
        grp = scan_cores[s0:s0 + NCORES]
        cin = [prep_scan_core(*[full[n][b, hg * 256:(hg + 1) * 256] for n in names]) for (b, hg) in grp]
        ys = run_rwkv_scan(cin)
        for (b, hg), yy in zip(grp, ys):
            y_full[b, hg * 256:(hg + 1) * 256] = yy.reshape(256, T)
    cores = []
    for i, (b, t0) in enumerate(tok_cores):
        cores.append({"xT": xT_tok[i], "yin": y_full[b, :, t0:t0 + NT], "bonusv": pr[i]["bonusv"], "gate": pr[i]["gate"]})
    out = run_rwkv_out(cores, p["ln_g"], p["ln_b"], p["w_o"])
    vf = vfirst_tok if vfirst_tok is not None else [pr[i]["vfirst"] for i in range(len(tok_cores))]
    return out, vf


def fm_ap(ap):
    return ap.rearrange("(c p) n -> p c n", p=128)


def emit_ffn(c, src, dst, NT, gin, w1, w2, add=None, gfin=None, TT=512):
    c.phase_begin()
    xv, yv = fm_ap(src), fm_ap(dst)
    w1s = c.sb("w1s", [128, 8, FF], BF16)
    w2s = c.sb("w2s", [128, 32, D], BF16)
    gs = c.sb("gs", [128, 8], F32)
    ones = c.sb("ones", [128, 128], BF16)
    w1b, w2b, gb, onesb = c.buf("w1"), c.buf("w2"), c.buf("g"), c.buf("ones")
    dw1, dw2, dg = c.S.dsem(), c.S.dsem(), c.S.dsem()
    c.dma(gs[:, :], gin, dg, writes=[gb])
    final = gfin is not None
    if final:
        gfs = c.sb("gfs", [128, 8], F32)
        c.dma(gfs[:, :], gfin, dg, writes=[gb])
    c.memset("pool", ones[:, :], 1.0, [onesb])
    NX = 2 if (add is None or TT < 512) else 1
    xs = [c.sb("x%d" % i, [128, 8, TT], F32) for i in range(NX)]
    xbs = [c.buf("x%d" % i) for i in range(NX)]
    dxl = [c.S.dsem() for i in range(NX)]
    dxs = [c.S.dsem() for i in range(NX)]
    if add is not None:
        av = fm_ap(add)
        ad = c.sb("ad", [128, 8, TT], F32)
        adb = c.buf("ad")
        dad = c.S.dsem()
    ntile = NT // TT

    def load(t):
        s = t % NX
        c.dma(xs[s][:, :, :], xv[:, :, t * TT:(t + 1) * TT], dxl[s], writes=[xbs[s]])
    load(0)
    load_weight_bf16(c, w1s, w1, dw1, w1b, nsplit=4)
    load_weight_bf16(c, w2s, w2, dw2, w2b, nsplit=4)
    sq = [c.sb("sq%d" % i, [128, TT], BF16) for i in range(2)]
    h = c.sb("h", [128, 8, TT], BF16)
    hid = c.sb("hid", [128, 32, TT], BF16)
    rstd = c.sb("rstd", [128, TT], F32)
    NR = 1 if TT >= 512 else 2
    rl = [c.sb("rl%d" % i, [128, TT], F32) for i in range(NR)]
    rlb = [c.buf() for i in range(NR)]
    hb, hidb, rstdb = c.buf("h"), c.buf("hid"), c.buf("rstd")
    sqb = [c.buf(), c.buf()]
    NP = 3
    pss = [c.ps("ps%d" % i, [128, 512]) for i in range(NP)]
    psb = [c.buf("ps%d" % i) for i in range(NP)]
    psn = c.ps("psn", [128, 512])
    psnb = c.buf("psn")
    pi = 0
    for t in range(ntile):
        s = t % NX
        x, xb = xs[s], xbs[s]
        if NX == 1:
            if t > 0:
                load(t)
        elif t + 1 < ntile:
            load(t + 1)
        if add is not None:
            c.dma(ad[:, :, :], av[:, :, t * TT:(t + 1) * TT], dad, writes=[adb])
            c.tt("pool", x[:, :, :], x[:, :, :], ad[:, :, :], ALU.add, [xb, adb], [xb])
        emit_rmsnorm(c, x, xb, TT, sq, sqb, ones, onesb, psn, psnb, rstd, rstdb, gs, gb, h, hb)
        for m in range(32):
            ps, pb = pss[pi % NP], psb[pi % NP]
            pi += 1
            for k in range(8):
                c.mm(ps[:, :TT], w1s[:, k, m * 128:(m + 1) * 128], h[:, k, :], k == 0, k == 7, [w1b, hb], [pb])
            r, rb = rl[m % NR], rlb[m % NR]
            c.act(r[:, :], ps[:, :TT], AF.Relu, [pb], [rb])
            c.tt("dve", hid[:, m, :], ps[:, :TT], r[:, :], ALU.mult, [pb, rb], [hidb])
        for n in range(8):
            ps, pb = pss[pi % NP], psb[pi % NP]
            pi += 1
            for m in range(32):
                c.mm(ps[:, :TT], w2s[:, m, n * 128:(n + 1) * 128], hid[:, m, :], m == 0, m == 31, [w2b, hidb], [pb])
            c.tt("dve", x[:, n, :], ps[:, :TT], x[:, n, :], ALU.add, [pb, xb], [xb])
        if final:
            emit_rmsnorm(c, x, xb, TT, sq, sqb, ones, onesb, psn, psnb, rstd, rstdb, gfs, gb, x, xb)
        c.dma(yv[:, :, t * TT:(t + 1) * TT], x[:, :, :], dxs[s], reads=[xb])
    c.phase_end()


def emit_sgu(c, src, dst, NT, W, TT=256):
    c.phase_begin()
    xv, yv = fm_ap(src), fm_ap(dst)
    NB = TT // 128
    wis = c.sb("wis", [128, 8, 2 * E2], BF16)
    wos = c.sb("wos", [128, 16, D], BF16)
    wss = c.sb("wss", [128, 16, 128], BF16)
    gs = c.sb("gs", [128, 8], F32)
    bus = c.sb("bus", [128, 16], F32)
    bvs = c.sb("bvs", [1, E2], F32)
    lgs = c.sb("lgs", [128, 16], F32)
    lbs = c.sb("lbs", [128, 16], F32)
    bss = c.sb("bss", [128, 16, 128], F32)
    biasT = c.sb("biasT", [128, 16, TT], F32)
    ones = c.sb("ones", [128, 128], BF16)
    ones32 = c.sb("ones32", [1, 128], F32)
    wib, wob, wsb, constb, onesb, biasb = c.buf("wi"), c.buf("wo"), c.buf("ws"), c.buf("const"), c.buf("ones"), c.buf("bias")
    dc = c.S.dsem()
    for dt_, src_ in ((gs, W["g"]), (bus, W["b_u"]), (bvs, W["b_v"]), (lgs, W["ln_g"]), (lbs, W["ln_b"])):
        c.dma(dt_[:, :], src_, dc, writes=[constb])
    c.dma(bss[:, :, :], W["b_s"].partition_broadcast(128), dc, writes=[constb])
    c.memset("pool", ones[:, :], 1.0, [onesb])
    c.memset("pool", ones32[:, :], 1.0, [onesb])
    dws = c.S.dsem()
    c.dma(wss[:, :, :], W["wsT"], dws, writes=[wsb], q="pool")
    c.memset("pool", wss[64:128, :, 0:64], 0.0, [wsb])
    NX = 2
    xs = [c.sb("x%d" % i, [128, 8, TT], F32) for i in range(NX)]
    xbs = [c.buf("x%d" % i) for i in range(NX)]
    dxl = [c.S.dsem() for i in range(NX)]
    dxs = [c.S.dsem() for i in range(NX)]
    ntile = NT // TT
    c.dma(xs[0][:, :, :], xv[:, :, 0:TT], dxl[0], writes=[xbs[0]])
    dwi, dwo = c.S.dsem(), c.S.dsem()
    load_weight_bf16(c, wis, W["w_in"], dwi, wib, nsplit=4)
    load_weight_bf16(c, wos, W["w_out"], dwo, wob, nsplit=2)
    sq = c.sb("sq", [128, 8, TT], BF16)
    h = c.sb("h", [128, 8, TT], BF16)
    u = c.sb("u", [128, 16, TT], F32)
    vg = c.sb("vg", [128, E2], F32)
    vn = [c.sb("vn%d" % i, [128, E2], BF16) for i in range(NB)]
    y = c.sb("y", [128, 16, TT], BF16)
    tmp = [c.sb("tmp%d" % i, [128, TT], F32) for i in range(2)]
    rstd = c.sb("rstd", [128, TT], F32)
    stats = c.sb("stats", [128, 4, 6], F32)
    mv = c.sb("mv", [128, 2], F32)
    sqb, hb, ub, vgb, yb, rstdb, statb, mvb = (c.buf(n) for n in ("sq", "h", "u", "vg", "y", "rstd", "stats", "mv"))
    vnb = [c.buf("vn%d" % i) for i in range(NB)]
    tmpb = [c.buf() for i in range(2)]
    NP = 4
    pss = [c.ps("ps%d" % i, [128, 512]) for i in range(NP)]
    psb = [c.buf("ps%d" % i) for i in range(NP)]
    psn = c.ps("psn", [128, 512])
    psnb = c.buf("psn")
    for g in range(16):
        c.mm(psn[:, :128], ones[:, :], wss[:, g, :], True, True, [onesb, wsb], [psnb])
        for tb in range(NB):
            c.stt(biasT[:, g, tb * 128:(tb + 1) * 128], psn[:, :128], lbs[:, g:g + 1], bss[:, g, :],
                  ALU.mult, ALU.add, [psnb, constb], [biasb])
    pi = 0
    for t in range(ntile):
        s = t % NX
        x, xb = xs[s], xbs[s]
        if t + 1 < ntile:
            s2 = (t + 1) % NX
            c.dma(xs[s2][:, :, :], xv[:, :, (t + 1) * TT:(t + 2) * TT], dxl[s2], writes=[xbs[s2]])
        emit_rmsnorm(c, x, xb, TT, sq, sqb, ones, onesb, psn, psnb, rstd, rstdb, gs, constb, h, hb)
        for m in range(16):
            ps, pb = pss[pi % NP], psb[pi % NP]
            pi += 1
            for k in range(8):
                c.mm(ps[:, :TT], wis[:, k, m * 128:(m + 1) * 128], h[:, k, :], k == 0, k == 7, [wib, hb], [pb])
            c.act(u[:, m, :], ps[:, :TT], AF.Gelu, [pb, constb], [ub], bias=bus[:, m:m + 1])
        for tb in range(NB):
            for nb in range(4):
                ps, pb = pss[pi % NP], psb[pi % NP]
                pi += 1
                for k in range(8):
                    c.mm(ps[:, :], h[:, k, tb * 128:(tb + 1) * 128], wis[:, k, E2 + nb * 512:E2 + (nb + 1) * 512],
                         k == 0, False, [wib, hb], [pb])
                c.mm(ps[:, :], ones32[:, :], bvs[:, nb * 512:(nb + 1) * 512], False, True, [onesb, constb], [pb])
                c.act(vg[:, nb * 512:(nb + 1) * 512], ps[:, :], AF.Gelu, [pb], [vgb])
                c.S.op("dve", lambda eng, o=stats[:, nb, :], i=vg[:, nb * 512:(nb + 1) * 512]: eng.bn_stats(o, i),
                       [vgb], [statb])
            c.S.op("dve", lambda eng, o=mv[:, :], i=stats[:, :, :].rearrange("p a b -> p (a b)"): eng.bn_aggr(o, i),
                   [statb], [mvb])
            c.act(mv[:, 1:2], mv[:, 1:2], AF.Sqrt, [mvb], [mvb], bias=1e-5, scale=1.0)
            c.recip(mv[:, 1:2], mv[:, 1:2], [mvb], [mvb])
            c.ts("dve", vn[tb][:, :], vg[:, :], mv[:, 0:1], mv[:, 1:2], ALU.subtract, ALU.mult, [vgb, mvb], [vnb[tb]])
        for g in range(16):
            ps, pb = pss[pi % NP], psb[pi % NP]
            pi += 1
            for tb in range(NB):
                c.mm(ps[:, tb * 128:(tb + 1) * 128], vn[tb][:, g * 128:(g + 1) * 128], wss[:, g, :], True, True,
                     [vnb[tb], wsb], [pb])
            tm, tmb = tmp[g % 2], tmpb[g % 2]
            c.stt(tm[:, :], ps[:, :TT], lgs[:, g:g + 1], biasT[:, g, :], ALU.mult, ALU.add, [pb, constb, biasb], [tmb])
            c.tt("pool", y[:, g, :], tm[:, :], u[:, g, :], ALU.mult, [tmb, ub], [yb])
        for n in range(8):
            ps, pb = pss[pi % NP], psb[pi % NP]
            pi += 1
            for m in range(16):
                c.mm(ps[:, :TT], wos[:, m, n * 128:(n + 1) * 128], y[:, m, :], m == 0, m == 15, [wob, yb], [pb])
            c.tt("dve", x[:, n, :], ps[:, :TT], x[:, n, :], ALU.add, [pb, xb], [xb])
        c.dma(yv[:, :, t * TT:(t + 1) * TT], x[:, :, :], dxs[s], reads=[xb])
    c.phase_end()


RG4 = [[0, 1, 2, 3], [4, 5, 6, 7]]


def emit_rwkv_a(c, src, NT, gin, hb_d, hall_d, TT=512):
    c.phase_begin()
    xv = fm_ap(src)
    TT = min(TT, NT)
    gs = c.sb("gs", [128, 8], F32)
    ones = c.sb("ones", [128, 128], BF16)
    gb, onesb = c.buf("g"), c.buf("ones")
    dg = c.S.dsem()
    c.dma(gs[:, :], gin, dg, writes=[gb])
    c.memset("pool", ones[:, :], 1.0, [onesb])
    NX = 2
    xs = [c.sb("x%d" % i, [128, 8, TT], F32) for i in range(NX)]
    hs = [c.sb("h%d" % i, [128, 8, TT], BF16) for i in range(NX)]
    xbs = [c.buf() for i in range(NX)]
    hbs = [c.buf() for i in range(NX)]
    dxl = [c.S.dsem() for i in range(NX)]
    dhs = [c.S.dsem() for i in range(NX)]
    sq = c.sb("sq", [128, 8, TT], BF16)
    rstd = c.sb("rstd", [128, TT], F32)
    sqb, rstdb = c.buf(), c.buf()
    psn = c.ps("psn", [128, 512])
    psnb = c.buf()
    hdb = c.buf("hb_d")
    for t in range(NT // TT):
        s = t % NX
        c.dma(xs[s][:, :, :], xv[:, :, t * TT:(t + 1) * TT], dxl[s], writes=[xbs[s]])
        emit_rmsnorm(c, xs[s], xbs[s], TT, sq, sqb, ones, onesb, psn, psnb, rstd, rstdb, gs, gb, hs[s], hbs[s])
        c.dma(fm_ap(hb_d)[:, :, t * TT:(t + 1) * TT], hs[s][:, :, :], dhs[s], reads=[hbs[s]], writes=[hdb])
    dcc = c.S.dsem()
    hallb = c.buf("hall")
    for k in range(8):
        c.S.op("pool", lambda eng, i=hb_d[k * 128:(k + 1) * 128, :], o=hall_d[k * 512:(k + 1) * 512, :]:
               eng.collective_compute("AllGather", ALU.bypass, replica_groups=RG4, ins=[i.opt()], outs=[o.opt()]),
               [hdb], [hallb], dsem=dcc, inc=1)
    c.phase_end()


def emit_rwkv_b(c, hall_d, NT, T, W, SC, vmix, TT=256):
    c.phase_begin()
    NM = 2
    DO = 256
    NCK = TT // CH
    hv = hall_d.rearrange("(c r p) n -> r p c n", p=128, c=8, r=4)
    wr = c.sb("wr", [128, 3, 8, DO], BF16)
    wla = c.sb("wla", [128, 8, 64], BF16)
    ala = c.sb("ala", [128, 8, 64], BF16)
    gla = c.sb("gla", [128, 8, 128], BF16)
    wlb = c.sb("wlb", [64, DO], BF16)
    alb = c.sb("alb", [64, DO], BF16)
    glb = c.sb("glb", [128, DO], BF16)
    wb, constb, onesb = c.buf("w"), c.buf("const"), c.buf("ones")
    dw, dc = c.S.dsem(), c.S.dsem()
    mus = c.sb("mus", [128, 6, 8], F32)
    c.dma(mus[:, :, :], W["mu"], dc, writes=[constb])
    pc = {}
    for n in ("w0", "a0", "k_k", "k_a", "r_k") + (("v0",) if vmix else ()):
        pc[n] = c.sb("pc_" + n, [128, NM], F32)
        c.dma(pc[n][:, :], W[n], dc, writes=[constb])
    pc["omka"] = c.sb("pc_omka", [128, NM], F32)
    c.ts("dve", pc["omka"][:, :], pc["k_a"][:, :], -1.0, 1.0, ALU.mult, ALU.add, [constb], [constb])
    for i in range(3):
        c.dma(wr[:, i, :, :], W["w_rkv"][i].rearrange("(c p) n -> p c n", p=128), dw, writes=[wb], q="pool")
    for dst_, src_ in ((wla, W["w_la"]), (ala, W["a_la"]), (gla, W["g_la"])):
        c.dma(dst_[:, :, :], src_.rearrange("(c p) n -> p c n", p=128), dw, writes=[wb], q="pool")
    for dst_, src_ in ((wlb, W["w_lb"]), (alb, W["a_lb"]), (glb, W["g_lb"])):
        c.dma(dst_[:, :], src_, dw, writes=[wb], q="pool")
    if vmix:
        vla = c.sb("vla", [128, 8, 32], BF16)
        vlb = c.sb("vlb", [32, DO], BF16)
        c.dma(vla[:, :, :], W["v_la"].rearrange("(c p) n -> p c n", p=128), dw, writes=[wb], q="pool")
        c.dma(vlb[:, :], W["v_lb"], dw, writes=[wb], q="pool")
    bd = c.sb("bd", [128, 128], F32)
    mreset = c.sb("mreset", [128, TT], F32)
    ident = c.sb("ident", [128, 128], BF16)
    c.memset("pool", bd[:, :], 0.0, [onesb])
    c.memset("pool", bd[0:64, 0:64], 1.0, [onesb])
    c.memset("pool", bd[64:128, 64:128], 1.0, [onesb])
    c.memset("pool", mreset[:, :], 1.0, [onesb])
    c.memset("pool", mreset[:, :].rearrange("p (c t) -> p c t", t=CH)[:, :, 0:1], 0.0, [onesb])
    c.dma(ident[:, :], W["ident"], dc, writes=[constb], q="pool")

    TE = TT + 1
    NX = 2
    hs = [c.sb("h%d" % i, [128, 8, TE], BF16) for i in range(NX)]
    hbs = [c.buf() for i in range(NX)]
    dhl = [c.S.dsem() for i in range(NX)]
    xx = c.sb("xx", [128, 8, TT], F32)
    xm = c.sb("xm", [128, 8, TT], F32)
    xxb, xmb = c.buf("xx"), c.buf("xm")
    xs6 = [c.sb("xs%d" % i, [128, 8, TT], BF16) for i in range(6)]
    xs6b = [c.buf("xs%d" % i) for i in range(6)]
    t_w = c.sb("t_w", [64, TT], BF16)
    t_a = c.sb("t_a", [64, TT], BF16)
    t_g = c.sb("t_g", [128, TT], BF16)
    t_wb, t_ab, t_gb = c.buf("t_w"), c.buf("t_a"), c.buf("t_g")
    if vmix:
        t_v = c.sb("t_v", [32, TT], BF16)
        t_vb = c.buf("t_v")
        vf = c.sb("vf", [128, NM, TT], F32)
        vfb = c.buf("vf")
        dvf = c.S.dsem()
    tn = ["r32", "k32", "v32", "sg", "a", "cs", "e1", "e2", "gin_", "gex", "ginv", "gsuf", "kk0", "sqk", "nrm",
          "kk", "tk", "k", "b", "rkr", "sv", "d"]
    Tt = {n: c.sb("T_" + n, [128, TT], F32) for n in tn}
    Tb = {n: c.buf("T_" + n) for n in tn}
    fnames = ["at", "rt", "bt", "kt"]
    tnames = ["at", "bh", "vv", "kh"]
    stg = {n: c.sb("S_" + n, [128, NM, TT], BF16) for n in ("at", "rt", "bt", "kt", "bh", "kh", "vv")}
    stgb = {n: c.buf("S_" + n) for n in stg}
    dst_ = {n: c.S.dsem() for n in fnames}
    STM = c.sb("STM", [128, 4, TT // 128, DO], BF16)
    STMb = c.buf("STM")
    dstm = c.S.dsem()
    s_gl = c.sb("s_gl", [128, NM, NCK], F32)
    s_bonus = c.sb("s_bonus", [128, NM, TT], F32)
    s_gate = c.sb("s_gate", [128, NM, TT], F32)
    s_glb, s_bonusb, s_gateb = c.buf(), c.buf(), c.buf()
    d_gl, d_bonus, d_gate = c.S.dsem(), c.S.dsem(), c.S.dsem()
    if not vmix:
        s_vf = c.sb("s_vf", [128, NM, TT], F32)
        s_vfb = c.buf()
        d_vf = c.S.dsem()
    pnames = ["r", "k", "v", "w", "a", "g", "vm", "tw", "ta", "tg", "tv", "n2", "rk"]
    banks = [c.ps("bank%d" % i, [128, 512]) for i in range(7)]
    P = {n: banks[i // 2][:, (i % 2) * 256:(i % 2) * 256 + TT] for i, n in enumerate(pnames)}
    Pb = {n: c.buf("P_" + n) for n in pnames}
    PT = c.ps("PT", [128, 4, TT // 128, 128], BF16)
    PTb = c.buf("PT")
    f2 = lambda ap: ap.rearrange("(c p) n -> p c n", p=128)

    def load_h(t):
        s = t % NX
        t0 = t * TT
        r, o = divmod(t0, NT)
        c.dma(hs[s][:, :, 1:TE], hv[r, :, :, o:o + TT], dhl[s], writes=[hbs[s]])
        if t0 == 0:
            c.memset("pool", hs[s][:, :, 0:1], 0.0, [hbs[s]])
        else:
            r2, o2 = divmod(t0 - 1, NT)
            c.dma(hs[s][:, :, 0:1], hv[r2, :, :, o2:o2 + 1], dhl[s], writes=[hbs[s]], slow=True)
    ntile = T // TT
    load_h(0)
    for t in range(ntile):
        s = t % NX
        h, hb = hs[s], hbs[s]
        if t + 1 < ntile:
            load_h(t + 1)
        tsl = slice(t * TT, (t + 1) * TT)
        if vmix:
            c.dma(vf[:, :, :], f2(SC["vfirst"])[:, :, tsl], dvf, writes=[vfb])
        c.tt("pool", xx[:, :, :], h[:, :, 0:TT], h[:, :, 1:TE], ALU.subtract, [hb], [xxb])
        for i in range(6):
            eng = "dve" if i % 2 == 0 else "pool"
            c.tt(eng, xm[:, :, :], xx[:, :, :], mus[:, i, :].unsqueeze(2).to_broadcast([128, 8, TT]), ALU.mult,
                 [xxb, constb], [xmb])
            c.tt(eng, xs6[i][:, :, :], xm[:, :, :], h[:, :, 1:TE], ALU.add, [xmb, hb], [xs6b[i]])
        for k in range(8):
            c.mm(P["tw"][0:64, :], wla[:, k, :], xs6[3][:, k, :], k == 0, k == 7, [wb, xs6b[3]], [Pb["tw"]])
        c.act(t_w[:, :], P["tw"][0:64, :], AF.Tanh, [Pb["tw"]], [t_wb])
        for k in range(8):
            c.mm(P["ta"][0:64, :], ala[:, k, :], xs6[4][:, k, :], k == 0, k == 7, [wb, xs6b[4]], [Pb["ta"]])
        c.copy("dve", t_a[:, :], P["ta"][0:64, :], [Pb["ta"]], [t_ab])
        for k in range(8):
            c.mm(P["tg"][:, :], gla[:, k, :], xs6[5][:, k, :], k == 0, k == 7, [wb, xs6b[5]], [Pb["tg"]])
        c.act(t_g[:, :], P["tg"][:, :], AF.Sigmoid, [Pb["tg"]], [t_gb])
        if vmix:
            for k in range(8):
                c.mm(P["tv"][0:32, :], vla[:, k, :], xs6[2][:, k, :], k == 0, k == 7, [wb, xs6b[2]], [Pb["tv"]])
            c.copy("dve", t_v[:, :], P["tv"][0:32, :], [Pb["tv"]], [t_vb])
        for m in range(NM):
            ms = slice(m * 128, (m + 1) * 128)
            col = lambda n: pc[n][:, m:m + 1]
            for i, n in enumerate(("r", "k", "v")):
                for k in range(8):
                    c.mm(P[n][:, :], wr[:, i, k, ms], xs6[i][:, k, :], k == 0, k == 7, [wb, xs6b[i]], [Pb[n]])
            c.mm(P["w"][:, :], wlb[:, ms], t_w[:, :], True, True, [wb, t_wb], [Pb["w"]])
            c.mm(P["a"][:, :], alb[:, ms], t_a[:, :], True, True, [wb, t_ab], [Pb["a"]])
            c.mm(P["g"][:, :], glb[:, ms], t_g[:, :], True, True, [wb, t_gb], [Pb["g"]])
            if vmix:
                c.mm(P["vm"][:, :], vlb[:, ms], t_v[:, :], True, True, [wb, t_vb], [Pb["vm"]])
            c.act(Tt["sg"][:, :], P["w"][:, :], AF.Sigmoid, [Pb["w"], constb], [Tb["sg"]], bias=col("w0"))
            c.act(Tt["a"][:, :], P["a"][:, :], AF.Sigmoid, [Pb["a"], constb], [Tb["a"]], bias=col("a0"))
            if vmix:
                c.act(Tt["sv"][:, :], P["vm"][:, :], AF.Sigmoid, [Pb["vm"], constb], [Tb["sv"]], bias=col("v0"))
            c.copy("act", Tt["r32"][:, :], P["r"][:, :], [Pb["r"]], [Tb["r32"]])
            c.copy("act", Tt["k32"][:, :], P["k"][:, :], [Pb["k"]], [Tb["k32"]])
            c.copy("act", Tt["v32"][:, :], P["v"][:, :], [Pb["v"]], [Tb["v32"]])
            c.copy("act", s_gate[:, m, :], P["g"][:, :], [Pb["g"]], [s_gateb])
            if vmix:
                c.tt("pool", Tt["d"][:, :], vf[:, m, :], Tt["v32"][:, :], ALU.subtract, [vfb, Tb["v32"]], [Tb["d"]])
                c.tt("pool", Tt["d"][:, :], Tt["d"][:, :], Tt["sv"][:, :], ALU.mult, [Tb["d"], Tb["sv"]], [Tb["d"]])
                c.tt("pool", Tt["v32"][:, :], Tt["v32"][:, :], Tt["d"][:, :], ALU.add, [Tb["v32"], Tb["d"]], [Tb["v32"]])
            else:
                c.copy("pool", s_vf[:, m, :], Tt["v32"][:, :], [Tb["v32"]], [s_vfb])
            c.S.op("dve", lambda eng, o=Tt["cs"][:, :], a=mreset[:, :], b=Tt["sg"][:, :]:
                   eng.tensor_tensor_scan(o, a, b, 0.0, ALU.mult, ALU.add), [onesb, Tb["sg"]], [Tb["cs"]])
            cs3 = Tt["cs"][:, :].rearrange("p (c t) -> p c t", t=CH)
            c.tt("pool", Tt["e1"][:, :], Tt["cs"][:, :], Tt["sg"][:, :], ALU.subtract, [Tb["cs"], Tb["sg"]], [Tb["e1"]])
            c.tt("pool", Tt["e2"][:, :].rearrange("p (c t) -> p c t", t=CH), cs3[:, :, CH - 1:CH].to_broadcast([128, NCK, CH]),
                 cs3, ALU.subtract, [Tb["cs"]], [Tb["e2"]])
            c.act(Tt["gin_"][:, :], Tt["cs"][:, :], AF.Exp, [Tb["cs"]], [Tb["gin_"]], scale=-C0)
            c.act(Tt["ginv"][:, :], Tt["cs"][:, :], AF.Exp, [Tb["cs"]], [Tb["ginv"]], scale=C0)
            c.act(Tt["gex"][:, :], Tt["e1"][:, :], AF.Exp, [Tb["e1"]], [Tb["gex"]], scale=-C0)
            c.act(Tt["gsuf"][:, :], Tt["e2"][:, :], AF.Exp, [Tb["e2"]], [Tb["gsuf"]], scale=-C0)
            c.act(s_gl[:, m, :], cs3[:, :, CH - 1], AF.Exp, [Tb["cs"]], [s_glb], scale=-C0)
            c.ts("pool", Tt["kk0"][:, :], Tt["k32"][:, :], col("k_k"), None, ALU.mult, None, [Tb["k32"], constb], [Tb["kk0"]])
            c.tt("pool", Tt["sqk"][:, :], Tt["kk0"][:, :], Tt["kk0"][:, :], ALU.mult, [Tb["kk0"]], [Tb["sqk"]])
            c.mm(P["n2"][:, :], bd[:, :], Tt["sqk"][:, :], True, True, [onesb, Tb["sqk"]], [Pb["n2"]])
            c.act(Tt["nrm"][:, :], P["n2"][:, :], AF.Sqrt, [Pb["n2"]], [Tb["nrm"]])
            c.ts("dve", Tt["nrm"][:, :], Tt["nrm"][:, :], 1e-12, None, ALU.max, None, [Tb["nrm"]], [Tb["nrm"]])
            c.recip(Tt["nrm"][:, :], Tt["nrm"][:, :], [Tb["nrm"]], [Tb["nrm"]])
            c.tt("dve", Tt["kk"][:, :], Tt["kk0"][:, :], Tt["nrm"][:, :], ALU.mult, [Tb["kk0"], Tb["nrm"]], [Tb["kk"]])
            c.ts("dve", Tt["tk"][:, :], Tt["a"][:, :], col("k_a"), col("omka"), ALU.mult, ALU.add, [Tb["a"], constb], [Tb["tk"]])
            c.tt("dve", Tt["k"][:, :], Tt["k32"][:, :], Tt["tk"][:, :], ALU.mult, [Tb["k32"], Tb["tk"]], [Tb["k"]])
            c.tt("pool", Tt["b"][:, :], Tt["kk"][:, :], Tt["a"][:, :], ALU.mult, [Tb["kk"], Tb["a"]], [Tb["b"]])
            c.stt(stg["at"][:, m, :], Tt["kk"][:, :], -1.0, Tt["gex"][:, :], ALU.mult, ALU.mult, [Tb["kk"], Tb["gex"]], [stgb["at"]])
            c.tt("pool", stg["rt"][:, m, :], Tt["r32"][:, :], Tt["gin_"][:, :], ALU.mult, [Tb["r32"], Tb["gin_"]], [stgb["rt"]])
            c.tt("dve", stg["bt"][:, m, :], Tt["b"][:, :], Tt["ginv"][:, :], ALU.mult, [Tb["b"], Tb["ginv"]], [stgb["bt"]])
            c.tt("pool", stg["kt"][:, m, :], Tt["k"][:, :], Tt["ginv"][:, :], ALU.mult, [Tb["k"], Tb["ginv"]], [stgb["kt"]])
            c.tt("dve", stg["bh"][:, m, :], Tt["b"][:, :], Tt["gsuf"][:, :], ALU.mult, [Tb["b"], Tb["gsuf"]], [stgb["bh"]])
            c.tt("pool", stg["kh"][:, m, :], Tt["k"][:, :], Tt["gsuf"][:, :], ALU.mult, [Tb["k"], Tb["gsuf"]], [stgb["kh"]])
            c.copy("pool", stg["vv"][:, m, :], Tt["v32"][:, :], [Tb["v32"]], [stgb["vv"]])
            c.stt(Tt["rkr"][:, :], Tt["r32"][:, :], col("r_k"), Tt["k"][:, :], ALU.mult, ALU.mult, [Tb["r32"], Tb["k"], constb], [Tb["rkr"]])
            c.mm(P["rk"][:, :], bd[:, :], Tt["rkr"][:, :], True, True, [onesb, Tb["rkr"]], [Pb["rk"]])
            c.tt("dve", s_bonus[:, m, :], P["rk"][:, :], Tt["v32"][:, :], ALU.mult, [Pb["rk"], Tb["v32"]], [s_bonusb])
            for j, n in enumerate(tnames):
                for blk in range(TT // 128):
                    c.S.op("pe", lambda eng, o=PT[:, j, blk, :], i=stg[n][:, m, blk * 128:(blk + 1) * 128], idn=ident[:, :]:
                           eng.transpose(o, i, idn), [stgb[n], constb], [PTb])
            c.copy("act", STM[:, :, :, ms], PT[:, :, :, :], [PTb], [STMb])
        for n in fnames:
            c.dma(f2(SC[n])[:, :, tsl], stg[n][:, :, :], dst_[n], reads=[stgb[n]])
        for j, n in enumerate(tnames):
            c.dma(SC[n + "_tm"][t * TT:(t + 1) * TT, :].rearrange("(b p) c -> p b c", p=128), STM[:, j, :, :], dstm, reads=[STMb])
        c.dma(f2(SC["gl"])[:, :, t * NCK:(t + 1) * NCK], s_gl[:, :, :], d_gl, reads=[s_glb])
        c.dma(f2(SC["bonusv"])[:, :, tsl], s_bonus[:, :, :], d_bonus, reads=[s_bonusb])
        c.dma(f2(SC["gate"])[:, :, tsl], s_gate[:, :, :], d_gate, reads=[s_gateb])
        if not vmix:
            c.dma(f2(SC["vfirst"])[:, :, tsl], s_vf[:, :, :], d_vf, reads=[s_vfb])
    c.phase_end()


def emit_rwkv_scan(c, T, SC, CMh, NI=2):
    c.phase_begin()
    NB = T // CH
    G = 4
    yv = SC["ysc"].rearrange("h v t -> v h t")
    cm = c.sb("cm", [128, 4, 64], F32)
    gl = c.sb("gl", [64, G, NB], F32)
    constb = c.buf("const")
    dc = c.S.dsem()
    c.dma(cm[:, :, :], CMh, dc, writes=[constb])
    c.dma(gl[:, :, :], SC["gl"].rearrange("(h k) n -> k h n", k=64), dc, writes=[constb])
    fmv = {n: SC[n].rearrange("(h k) t -> k h t", k=64) for n in ("bt", "kt", "at", "rt")}
    tmv = {n: SC[n + "_tm"].rearrange("t (h k) -> t h k", k=64) for n in ("at", "bh", "vv", "kh")}
    NSET = NI + 1
    sets = []
    for i in range(NSET):
        st = {}
        st["W"] = c.sb("W%d" % i, [64, G, 640], BF16)
        st["B"] = c.sb("B%d" % i, [128, G, 192], BF16)
        st["ZTt"] = [c.sb("ZTt%d_%d" % (i, j), [64, G, 128], BF16) for j in range(2)]
        st["ZT"] = [c.sb("ZT%d_%d" % (i, j), [64, G, 64], BF16) for j in range(2)]
        st["MT"] = c.sb("MT%d" % i, [64, G, 64], BF16)
        st["AN"] = c.sb("AN%d" % i, [64, G, 128], BF16)
        st["PW"] = c.sb("PW%d" % i, [128, G, 128], BF16)
        st["YT"] = c.sb("YT%d" % i, [64, G, 64], F32)
        for n in ("Wd", "Wa", "Wc", "Bd", "Bs", "Bp", "Bb", "ZTt0", "ZTt1", "ZT0", "ZT1", "MT", "AN", "PW", "YT"):
            st["b_" + n] = c.buf(n + str(i))
        st["dW"] = c.S.dsem()
        st["dB"] = c.S.dsem()
        st["dY"] = c.S.dsem()
        sets.append(st)
    NPS = min(NI, 2)
    pset = []
    for i in range(NPS):
        p = {"P1": c.ps("P1_%d" % i, [128, G, 128]), "PA": c.ps("PA_%d" % i, [64, G, 128]),
             "PB": c.ps("PB_%d" % i, [64, G, 128])}
        for n in ("P1", "PA", "PB"):
            p["b_" + n] = c.buf(n + str(i))
        pset.append(p)
    PSY = [c.ps("PSS", [64, G, 128]), c.ps("PSY", [64, G, 128])]
    psyb = [c.buf("pss"), c.buf("psy")]
    c.memset("pool", sets[0]["B"][0:64, :, 0:64], 0.0, [sets[0]["b_Bs"]])

    def batch(bi):
        st = sets[bi % NSET]
        nx = sets[(bi + 1) % NSET]
        p = pset[bi % NPS]
        W, B = st["W"], st["B"]
        csl = slice(bi * CH, (bi + 1) * CH)
        for j, n in enumerate(("bt", "kt", "at", "rt")):
            c.dma(W[:, :, j * 64:(j + 1) * 64], fmv[n][:, :, csl], st["dW"], writes=[st["b_Wd"]])
        c.dma(W[:, :, 320:384], tmv["at"][csl, :, :], st["dW"], writes=[st["b_Wd"]])
        c.dma(W[:, :, 576:640], tmv["bh"][csl, :, :], st["dW"], writes=[st["b_Wd"]])
        c.dma(B[64:128, :, 0:64], tmv["vv"][csl, :, :], st["dB"], writes=[st["b_Bd"]])
        c.dma(B[64:128, :, 128:192], tmv["kh"][csl, :, :], st["dB"], writes=[st["b_Bd"]])
        Wv = W[:, :, 256:640].rearrange("p g (a b) -> p g a b", b=64)
        c.copy("pool", B[0:64, :, 64:128], W[:, :, 192:256], [st["b_Wd"]], [st["b_Bp"]])
        c.tt("pool", B[0:64, :, 128:192], cm[0:64, 3:4, :].to_broadcast([64, G, 64]),
             gl[:, :, bi:bi + 1].to_broadcast([64, G, 64]), ALU.mult, [constb], [st["b_Bp"]])
        yield
        for g in range(G):
            c.mm(p["P1"][:, g, :], W[:, g, 0:128], W[:, g, 128:256], True, True, [st["b_Wd"]], [p["b_P1"]])
        for g in range(G):
            c.mm(p["PA"][:, g, :], W[:, g, 128:192], W[:, g, 0:128], True, True, [st["b_Wd"]], [p["b_PA"]])
        c.tt("dve", W[:, :, 448:576], p["P1"][0:64, :, :],
             cm[0:64, 0:2, :].rearrange("p a b -> p (a b)").unsqueeze(1).to_broadcast([64, G, 128]), ALU.mult,
             [p["b_P1"], constb], [st["b_Wa"]])
        c.tt("dve", B[64:128, :, 64:128], p["P1"][64:128, :, 64:128], cm[64:128, 1:2, :].to_broadcast([64, G, 64]),
             ALU.mult, [p["b_P1"], constb], [st["b_Bb"]])
        c.tt("dve", Wv[:, :, 0:3:2, :], p["PA"][:, :, :].rearrange("p g (a b) -> p g a b", b=64),
             cm[0:64, 2:3, :].unsqueeze(1).to_broadcast([64, G, 2, 64]), ALU.mult, [p["b_PA"], constb], [st["b_Wc"]])
        c.tt("pool", st["ZTt"][1][:, :, 64:128], W[:, :, 448:512], cm[0:64, 3:4, :].to_broadcast([64, G, 64]), ALU.add,
             [st["b_Wa"], constb], [st["b_ZTt1"]])
        yield
        for g in range(G):
            c.mm(p["PA"][:, g, 0:64], W[:, g, 256:320], W[:, g, 448:512], True, True, [st["b_Wc"], st["b_Wa"]], [p["b_PA"]])
        for g in range(G):
            c.mm(p["PB"][:, g, 0:64], W[:, g, 448:512], W[:, g, 256:320], True, True, [st["b_Wc"], st["b_Wa"]], [p["b_PB"]])
        c.copy("act", st["ZTt"][1][:, :, 0:64], p["PA"][:, :, 0:64], [p["b_PA"]], [st["b_ZTt1"]])
        c.copy("dve", st["ZT"][1][:, :, :], p["PB"][:, :, 0:64], [p["b_PB"]], [st["b_ZT1"]])
        yield
        for j in range(1, 6):
            cur, nxt = j % 2, (j + 1) % 2
            ZTt, ZT = st["ZTt"][cur], st["ZT"][cur]
            bz, bzt = st["b_ZTt%d" % cur], st["b_ZT%d" % cur]
            nz, nzt = st["b_ZTt%d" % nxt], st["b_ZT%d" % nxt]
            last = j == 5
            for g in range(G):
                if j <= 3:
                    c.mm(p["PA"][:, g, :], ZT[:, g, :], ZTt[:, g, :], True, True, [bz, bzt], [p["b_PA"]])
                else:
                    c.mm(p["PA"][:, g, 64:128], ZT[:, g, :], ZTt[:, g, 64:128], True, True, [bz, bzt], [p["b_PA"]])
            if j <= 4:
                for g in range(G):
                    c.mm(p["PB"][:, g, 0:64], ZTt[:, g, 0:64], ZT[:, g, :], True, True, [bz, bzt], [p["b_PB"]])
            if j <= 3:
                c.copy("act", st["ZTt"][nxt][:, :, 0:64], p["PA"][:, :, 0:64], [p["b_PA"]], [nz])
            if last:
                c.tt("dve", st["MT"][:, :, :], p["PA"][:, :, 64:128], ZTt[:, :, 64:128], ALU.add, [p["b_PA"], bz], [st["b_MT"]])
            else:
                c.tt("dve", st["ZTt"][nxt][:, :, 64:128], p["PA"][:, :, 64:128], ZTt[:, :, 64:128], ALU.add, [p["b_PA"], bz], [nz])
            if j <= 4:
                c.copy("act", st["ZT"][nxt][:, :, :], p["PB"][:, :, 0:64], [p["b_PB"]], [nzt])
            yield
        for g in range(G):
            c.mm(p["PA"][:, g, :], st["MT"][:, g, :], W[:, g, 320:448], True, True, [st["b_MT"], st["b_Wd"], st["b_Wc"]], [p["b_PA"]])
        c.copy("act", st["AN"][:, :, :], p["PA"][:, :, :], [p["b_PA"]], [st["b_AN"]])
        yield
        for g in range(G):
            c.mm(p["P1"][:, g, :], st["AN"][:, g, :], W[:, g, 512:640], True, True, [st["b_AN"], st["b_Wa"], st["b_Wd"]], [p["b_P1"]])
        c.tt("dve", st["PW"][:, :, :], p["P1"][:, :, :], B[:, :, 64:192], ALU.add,
             [p["b_P1"], st["b_Bp"], st["b_Bb"], st["b_Bd"]], [st["b_PW"]])
        yield
        for g in range(G):
            c.mm(PSY[0][:, g, 0:64], st["PW"][:, g, 64:128], B[:, g, 0:64], True, True, [st["b_PW"], st["b_Bs"], st["b_Bd"]], [psyb[0]])
        for g in range(G):
            c.mm(PSY[1][:, g, 0:64], B[:, g, 0:64], st["PW"][:, g, 0:64], True, True, [st["b_PW"], st["b_Bs"], st["b_Bd"]], [psyb[1]])
        if bi + 1 < NB:
            c.copy("act", nx["B"][0:64, :, 0:64], PSY[0][:, :, 0:64], [psyb[0]], [nx["b_Bs"]])
        c.copy("dve", st["YT"][:, :, :], PSY[1][:, :, 0:64], [psyb[1]], [st["b_YT"]])
        c.dma(yv[:, :, csl], st["YT"][:, :, :], st["dY"], reads=[st["b_YT"]])
        yield

    gens = []
    nxt_b = 0
    while nxt_b < NB or gens:
        if len(gens) < NI and nxt_b < NB:
            gens.append(batch(nxt_b))
            nxt_b += 1
        alive = []
        for gnr in gens:
            try:
                next(gnr)
                alive.append(gnr)
            except StopIteration:
                pass
        gens = alive
    c.phase_end()


def emit_rwkv_d(c, NT, T, W, SC, part_d, rs_d, TT=512):
    c.phase_begin()
    TT = min(TT, NT)
    NM = 2
    f2 = lambda ap: ap.rearrange("(c p) n -> p c n", p=128)
    wos = c.sb("wos", [128, NM, D], BF16)
    lgs = c.sb("lgs", [128, NM], F32)
    lbs = c.sb("lbs", [128, NM], F32)
    bd = c.sb("bd", [128, 128], F32)
    wob, constb, onesb = c.buf("wo"), c.buf("const"), c.buf("ones")
    dc, dwo = c.S.dsem(), c.S.dsem()
    c.dma(lgs[:, :], W["ln_g"], dc, writes=[constb])
    c.dma(lbs[:, :], W["ln_b"], dc, writes=[constb])
    c.memset("pool", bd[:, :], 0.0, [onesb])
    c.memset("pool", bd[0:64, 0:64], 1.0, [onesb])
    c.memset("pool", bd[64:128, 64:128], 1.0, [onesb])
    c.dma(wos[:, :, :], W["w_o"].rearrange("(c p) n -> p c n", p=128), dwo, writes=[wob], q="pool")
    NX = 2
    names = ("y", "bo", "ga")
    srcs = {"y": f2(SC["ysc"].rearrange("h v t -> (h v) t")), "bo": f2(SC["bonusv"]), "ga": f2(SC["gate"])}
    tiles = {n: [c.sb("%s%d" % (n, i), [128, NM, TT], F32) for i in range(NX)] for n in names}
    tb = {n: [c.buf() for i in range(NX)] for n in names}
    dl = {n: [c.S.dsem() for i in range(NX)] for n in names}
    ntile = T // TT

    def load(t):
        s = t % NX
        for n in names:
            c.dma(tiles[n][s][:, :, :], srcs[n][:, :, t * TT:(t + 1) * TT], dl[n][s], writes=[tb[n][s]])
    load(0)
    o = c.sb("o", [128, NM, TT], BF16)
    ob = c.buf("o")
    tn = ["ysq", "mean", "m2", "var", "d", "o1"]
    Tt = {n: c.sb("T_" + n, [128, TT], F32) for n in tn}
    Tb = {n: c.buf("T_" + n) for n in tn}
    pm = [c.ps("pm%d" % i, [128, 512]) for i in range(2)]
    pq = [c.ps("pq%d" % i, [128, 512]) for i in range(2)]
    pmb = [c.buf() for i in range(2)]
    pqb = [c.buf() for i in range(2)]
    NP = 3
    pss = [c.ps("ps%d" % i, [128, 512]) for i in range(NP)]
    psb = [c.buf() for i in range(NP)]
    NS = 2
    stage = [c.sb("stage%d" % i, [128, 8, TT], F32) for i in range(NS)]
    stageb = [c.buf() for i in range(NS)]
    dst_ = [c.S.dsem() for i in range(NS)]
    partb = c.buf("part")
    pv = part_d.rearrange("(s c p) n -> s p c n", p=128, c=8)
    pi = 0
    for t in range(ntile):
        s = t % NX
        if t + 1 < ntile:
            load(t + 1)
        y, bo, ga = (tiles[n][s] for n in names)
        yb, bob, gab = (tb[n][s] for n in names)
        for m in range(NM):
            P1, P1b, P2, P2b = pm[m % 2], pmb[m % 2], pq[m % 2], pqb[m % 2]
            c.tt("pool", Tt["ysq"][:, :], y[:, m, :], y[:, m, :], ALU.mult, [yb], [Tb["ysq"]])
            c.mm(P1[:, :TT], bd[:, :], y[:, m, :], True, True, [onesb, yb], [P1b])
            c.mm(P2[:, :TT], bd[:, :], Tt["ysq"][:, :], True, True, [onesb, Tb["ysq"]], [P2b])
            c.act(Tt["mean"][:, :], P1[:, :TT], AF.Copy, [P1b], [Tb["mean"]], scale=1.0 / 64)
            c.tt("pool", Tt["m2"][:, :], Tt["mean"][:, :], Tt["mean"][:, :], ALU.mult, [Tb["mean"]], [Tb["m2"]])
            c.stt(Tt["var"][:, :], P2[:, :TT], 1.0 / 64, Tt["m2"][:, :], ALU.mult, ALU.subtract, [P2b, Tb["m2"]], [Tb["var"]])
            c.act(Tt["var"][:, :], Tt["var"][:, :], AF.Sqrt, [Tb["var"]], [Tb["var"]], bias=64e-5, scale=1.0)
            c.recip(Tt["var"][:, :], Tt["var"][:, :], [Tb["var"]], [Tb["var"]])
            c.tt("pool", Tt["d"][:, :], y[:, m, :], Tt["mean"][:, :], ALU.subtract, [yb, Tb["mean"]], [Tb["d"]])
            c.tt("dve", Tt["d"][:, :], Tt["d"][:, :], Tt["var"][:, :], ALU.mult, [Tb["d"], Tb["var"]], [Tb["d"]])
            c.ts("dve", Tt["o1"][:, :], Tt["d"][:, :], lgs[:, m:m + 1], lbs[:, m:m + 1], ALU.mult, ALU.add, [Tb["d"], constb], [Tb["o1"]])
            c.tt("pool", Tt["o1"][:, :], Tt["o1"][:, :], bo[:, m, :], ALU.add, [Tb["o1"], bob], [Tb["o1"]])
            c.tt("dve", o[:, m, :], Tt["o1"][:, :], ga[:, m, :], ALU.mult, [Tb["o1"], gab], [ob])
        sg, sgb = stage[t % NS], stageb[t % NS]
        for n in range(8):
            ps, pb = pss[pi % NP], psb[pi % NP]
            pi += 1
            for m in range(NM):
                c.mm(ps[:, :TT], wos[:, m, n * 128:(n + 1) * 128], o[:, m, :], m == 0, m == NM - 1, [wob, ob], [pb])
            c.copy("act" if n % 2 == 0 else "dve", sg[:, n, :], ps[:, :TT], [pb], [sgb])
        seg, off = divmod(t * TT, NT)
        c.dma(pv[seg, :, :, off:off + TT], sg[:, :, :], dst_[t % NS], reads=[sgb], writes=[partb])
    dcc = c.S.dsem()
    rsb = c.buf("rs")
    c.S.op("pool", lambda eng: eng.collective_compute("ReduceScatter", ALU.add, replica_groups=RG4,
                                                      ins=[part_d.opt()], outs=[rs_d.opt()]),
           [partb], [rsb], dsem=dcc, inc=1)
    c.phase_end()


SGU_KEYS = ("g", "w_in", "b_u", "b_v", "ln_g", "ln_b", "wsT", "b_s", "w_out")
RW_KEYS = ("mu", "w_rkv", "w0", "a0", "k_k", "k_a", "r_k", "w_la", "w_lb", "a_la", "a_lb", "g_la", "g_lb", "ln_g", "ln_b", "w_o")
RW_SHAPES = {"mu": [128, 6, 8], "w_rkv": [3, D, 256], "w0": [128, 2], "a0": [128, 2], "k_k": [128, 2], "k_a": [128, 2],
             "r_k": [128, 2], "v0": [128, 2], "w_la": [D, 64], "w_lb": [64, 256], "a_la": [D, 64], "a_lb": [64, 256],
             "g_la": [D, 128], "g_lb": [128, 256], "v_la": [D, 32], "v_lb": [32, 256], "ln_g": [128, 2], "ln_b": [128, 2],
             "w_o": [256, D]}
SGU_SHAPES = {"g": [128, 8], "w_in": [D, 2 * E2], "b_u": [128, 16], "b_v": [1, E2], "ln_g": [128, 16], "ln_b": [128, 16],
              "wsT": [128, 16, 128], "b_s": [16, 128], "w_out": [E2, D]}


def build_fused(T, NI=2):
    NT = T // 4
    c = Ctx()
    xT = c.dram_in("xT", [D, NT])
    yT = c.dram_out("yT", [D, NT])
    sgu = [{k: c.dram_in("s%d_%s" % (j, k), SGU_SHAPES[k]) for k in SGU_KEYS} for j in range(2)]
    ffn = [{"g": c.dram_in("f%d_g" % i, [128, 8]), "w1": c.dram_in("f%d_w1" % i, [D, FF]),
            "w2": c.dram_in("f%d_w2" % i, [FF, D])} for i in range(4)]
    gf = c.dram_in("gf", [128, 8])
    rw = []
    for j in range(2):
        keys = RW_KEYS + (("v0", "v_la", "v_lb") if j > 0 else ())
        d = {k: c.dram_in("r%d_%s" % (j, k), RW_SHAPES[k]) for k in keys}
        d["g"] = c.dram_in("r%d_g" % j, [128, 8])
        rw.append(d)
    ident = c.dram_in("ident", [128, 128], BF16)
    CMh = c.dram_in("CM", [128, 4, 64])
    for d in rw:
        d["ident"] = ident
    xs_d = c.scratch("xs_d", [D, NT], F32)
    hb_d = c.scratch("hb_d", [D, NT], BF16)
    hall_d = c.scratch("hall_d", [4 * D, NT], BF16)
    part_d = c.scratch("part_d", [4 * D, NT], F32)
    rs_d = c.scratch("rs_d", [D, NT], F32)
    SC = {n: c.scratch("sc_" + n, [256, T], BF16) for n in ("at", "rt", "bt", "kt")}
    SC.update({n + "_tm": c.scratch("sc_" + n + "_tm", [T, 256], BF16) for n in ("at", "bh", "vv", "kh")})
    SC["gl"] = c.scratch("sc_gl", [256, T // CH], F32)
    SC["bonusv"] = c.scratch("sc_bonusv", [256, T], F32)
    SC["gate"] = c.scratch("sc_gate", [256, T], F32)
    SC["vfirst"] = c.scratch("sc_vfirst", [256, T], F32)
    SC["ysc"] = c.scratch("sc_ysc", [4, 64, T], F32)

    emit_sgu(c, xT, xs_d, NT, sgu[0])
    emit_ffn(c, xs_d, xs_d, NT, ffn[0]["g"], ffn[0]["w1"], ffn[0]["w2"], TT=min(512, NT))
    for j in range(2):
        emit_rwkv_a(c, xs_d, NT, rw[j]["g"], hb_d, hall_d)
        emit_rwkv_b(c, hall_d, NT, T, rw[j], SC, vmix=(j > 0))
        emit_rwkv_scan(c, T, SC, CMh, NI)
        emit_rwkv_d(c, NT, T, rw[j], SC, part_d, rs_d)
        i = 2 * j + 1
        last = j == 1
        emit_ffn(c, xs_d, yT if last else xs_d, NT, ffn[i]["g"], ffn[i]["w1"], ffn[i]["w2"], add=rs_d,
                 gfin=gf if last else None, TT=min(512, NT))
        if not last:
            emit_sgu(c, xs_d, xs_d, NT, sgu[1])
            emit_ffn(c, xs_d, xs_d, NT, ffn[2]["g"], ffn[2]["w1"], ffn[2]["w2"], TT=min(512, NT))
    c.S.emit()
    c.es.close()
    return c.nc


def to_pm(v, hg):
    return np.ascontiguousarray(np.asarray(v, np.float32).reshape(-1)[hg * 256:(hg + 1) * 256].reshape(2, 128).T)


def fused_inputs(inp, B, T):
    import ml_dtypes
    f = np.float32
    A = lambda a: np.ascontiguousarray(np.asarray(a), dtype=f)
    NT = T // 4
    x = np.asarray(inp["x"], f)
    shared = {"gf": to_pc(inp["final_norm_g"]), "CM": scan_consts(),
              "ident": np.eye(128, dtype=f).astype(ml_dtypes.bfloat16)}
    for j in range(2):
        shared.update({"s%d_g" % j: to_pc(inp["norm_mix_g"][2 * j]), "s%d_w_in" % j: A(inp["sgu_w_in"][j]),
                       "s%d_b_u" % j: to_pc(np.asarray(inp["sgu_b_in"][j])[:E2]),
                       "s%d_b_v" % j: A(np.asarray(inp["sgu_b_in"][j])[None, E2:]),
                       "s%d_ln_g" % j: to_pc(inp["sgu_ln_g"][j]), "s%d_ln_b" % j: to_pc(inp["sgu_ln_b"][j]),
                       "s%d_wsT" % j: A(np.transpose(np.asarray(inp["sgu_w_s"][j]), (2, 0, 1))),
                       "s%d_b_s" % j: A(inp["sgu_b_s"][j]), "s%d_w_out" % j: A(inp["sgu_w_out"][j])})
        shared.update({"r%d_g" % j: to_pc(inp["norm_mix_g"][2 * j + 1]),
                       "r%d_mu" % j: np.ascontiguousarray(np.stack([to_pc(inp["rwkv_mu"][j][i]) for i in range(6)], axis=1)),
                       "r%d_w_la" % j: A(inp["rwkv_w_lora_a"][j]), "r%d_a_la" % j: A(inp["rwkv_a_lora_a"][j]),
                       "r%d_g_la" % j: A(inp["rwkv_g_lora_a"][j])})
        if j > 0:
            shared["r%d_v_la" % j] = A(inp["rwkv_v_lora_a"][j - 1])
    for i in range(4):
        shared.update({"f%d_g" % i: to_pc(inp["norm_ffn_g"][i]), "f%d_w1" % i: A(inp["ffn_w1"][i]),
                       "f%d_w2" % i: A(inp["ffn_w2"][i])})
    per_hg = []
    for hg in range(4):
        d = {}
        cs = slice(hg * 256, (hg + 1) * 256)
        for j in range(2):
            pre = "r%d_" % j
            d[pre + "w_rkv"] = A(np.asarray(inp["rwkv_w_rkv"][j])[:, :, cs])
            for k, src in (("w0", "rwkv_w0"), ("a0", "rwkv_a0"), ("k_k", "rwkv_k_k"), ("k_a", "rwkv_k_a"),
                           ("r_k", "rwkv_r_k"), ("ln_g", "rwkv_ln_g"), ("ln_b", "rwkv_ln_b")):
                d[pre + k] = to_pm(inp[src][j], hg)
            d[pre + "w_lb"] = A(np.asarray(inp["rwkv_w_lora_b"][j])[:, cs])
            d[pre + "a_lb"] = A(np.asarray(inp["rwkv_a_lora_b"][j])[:, cs])
            d[pre + "g_lb"] = A(np.asarray(inp["rwkv_g_lora_b"][j])[:, cs])
            d[pre + "w_o"] = A(np.asarray(inp["rwkv_w_o"][j])[cs, :])
            if j > 0:
                d[pre + "v0"] = to_pm(inp["rwkv_v0"][j - 1], hg)
                d[pre + "v_lb"] = A(np.asarray(inp["rwkv_v_lora_b"][j - 1])[:, cs])
        per_hg.append(d)
    in_maps = []
    for b in range(B):
        for q in range(4):
            m = dict(shared)
            m.update(per_hg[q])
            m["xT"] = np.ascontiguousarray(x[b, q * NT:(q + 1) * NT].T)
            in_maps.append(m)
    return in_maps


def kernel_fused(inp, NI=2):
    x = np.asarray(inp["x"])
    B, T, _ = x.shape
    assert B == 2
    NT = T // 4
    nc = get_prog(("fused", T, NI), lambda: build_fused(T, NI))
    in_maps = fused_inputs(inp, B, T)
    res = run_bass_kernel_spmd(nc, in_maps, core_ids=list(range(NCORES)))
    out = np.empty((B, T, D), np.float32)
    for b in range(B):
        for q in range(4):
            out[b, q * NT:(q + 1) * NT] = res.results[b * 4 + q]["yT"].T
    return out


def kernel(**inputs):
    return kernel_fused({k: np.asarray(v) for k, v in inputs.items()})
```

```python
import numpy as np
from contextlib import ExitStack
import concourse.bass as bass
import concourse.mybir as mybir
from concourse.bass_utils import run_bass_kernel_spmd

F32 = mybir.dt.float32
BF16 = mybir.dt.bfloat16
AF = mybir.ActivationFunctionType
ALU = mybir.AluOpType

NCORES = 8
D = 1024
FF = 4096
SAME_ENGINE_SYNC = True


class Buf:
    __slots__ = ("w", "r", "name")

    def __init__(self, name=""):
        self.w = None
        self.r = {}
        self.name = name


class DSem:
    def __init__(self, sem, name):
        self.sem = sem
        self.n = 0
        self.name = name


class Sched:
    ENGS = ("pe", "act", "dve", "pool", "sp")

    def __init__(self, nc, es):
        self.nc = nc
        self.es = es
        self.ops = {e: [] for e in self.ENGS}
        self.cnt = {e: 0 for e in self.ENGS}
        self.sem = {e: es.enter_context(nc.semaphore("s_" + e)) for e in self.ENGS if e != "sp"}
        self.seen = {e: {} for e in self.ENGS}
        self.pool = []
        self.ptr = 0
        self.nphase = 0

    def dsem(self, name=None):
        if self.ptr == len(self.pool):
            nm = "p%d" % len(self.pool)
            self.pool.append(DSem(self.es.enter_context(self.nc.semaphore("ds_" + nm)), nm))
        d = self.pool[self.ptr]
        self.ptr += 1
        return d

    def new_engine_sems(self):
        for e in self.ENGS:
            if e == "sp":
                continue
            self.sem[e] = self.es.enter_context(self.nc.semaphore("s_%s_%d" % (e, self.nphase)))
            self.cnt[e] = 0
        self.nphase += 1
        for e in self.ENGS:
            for f in self.ENGS:
                self.seen[e].pop("E" + f, None)

    def barrier(self):
        for e in self.ENGS:
            waits = []
            for f in ("pe", "act", "dve", "pool"):
                if f != e and self.cnt[f] > self.seen[e].get("E" + f, 0):
                    waits.append((self.sem[f], self.cnt[f]))
                    self.seen[e]["E" + f] = self.cnt[f]
            for d in self.pool:
                if d.n > self.seen[e].get("D" + d.name, 0):
                    waits.append((d.sem, d.n))
                    self.seen[e]["D" + d.name] = d.n
            self.ops[e].append((waits, None, None))

    def _need(self, e, ev, waits):
        if ev is None:
            return
        key, sem, val, owner = ev
        if owner == e and (e == "pe" or not SAME_ENGINE_SYNC):
            return
        if self.seen[e].get(key, 0) >= val:
            return
        if key not in waits or waits[key][1] < val:
            waits[key] = (sem, val)

    def op(self, e, fn, reads=(), writes=(), dsem=None, inc=16):
        waits = {}
        for b in reads:
            self._need(e, b.w, waits)
        for b in writes:
            self._need(e, b.w, waits)
            for ev in b.r.values():
                self._need(e, ev, waits)
        for key, (sem, val) in waits.items():
            self.seen[e][key] = val
        if dsem is None:
            self.cnt[e] += 1
            ev = ("E" + e, self.sem[e], self.cnt[e], e)
            inc = (self.sem[e], 1)
        else:
            dsem.n += inc
            ev = ("D" + dsem.name, dsem.sem, dsem.n, None)
            inc = (dsem.sem, inc)
        self.ops[e].append((list(waits.values()), fn, inc))
        for b in reads:
            b.r[ev[0]] = ev
        for b in writes:
            b.w = ev
            b.r = {}

    def finish(self, dsems):
        waits = [(d.sem, d.n) for d in dsems if d.n > 0]
        self.ops["sp"].append((waits, None, None))

    def _run(self, e, eng):
        for waits, fn, inc in self.ops[e]:
            for sem, val in waits:
                eng.wait_ge(sem, val)
            if fn is not None:
                ins = fn(eng)
                ins.then_inc(inc[0], inc[1])

    def emit(self):
        nc = self.nc
        with nc.Block() as block:
            @block.tensor
            def _(eng):
                self._run("pe", eng)

            @block.scalar
            def _(eng):
                self._run("act", eng)

            @block.vector
            def _(eng):
                self._run("dve", eng)

            @block.gpsimd
            def _(eng):
                self._run("pool", eng)

            @block.sync
            def _(eng):
                self._run("sp", eng)


class Ctx:
    def __init__(self):
        self.nc = bass.Bass("TRN2", target_bir_lowering=False)
        self.es = ExitStack()
        self.S = Sched(self.nc, self.es)
        self.nbuf = 0

    def dram_in(self, name, shape, dt=F32):
        return self.nc.dram_tensor(name, list(shape), dt, kind="ExternalInput").ap()

    def dram_out(self, name, shape, dt=F32):
        return self.nc.dram_tensor(name, list(shape), dt, kind="ExternalOutput").ap()

    def sb(self, name, shape, dt):
        return self.es.enter_context(self.nc.sbuf_tensor("p%d_%s" % (getattr(self, "phase", 0), name), list(shape), dt))

    def ps(self, name, shape, dt=F32):
        return self.es.enter_context(self.nc.psum_tensor("p%d_%s" % (getattr(self, "phase", 0), name), list(shape), dt))

    def buf(self, name=""):
        self.nbuf += 1
        return Buf(name or "b%d" % self.nbuf)

    def dma(self, out, in_, dsem, reads=(), writes=(), q="sp", slow=False):
        self.S.op(q, lambda eng, o=out, i=in_: eng.dma_start(out=o, in_=i, allow_slow_non_contiguous=slow), reads, writes, dsem=dsem)

    def mm(self, out, lhsT, rhs, start, stop, reads, writes):
        self.S.op("pe", lambda eng, o=out, l=lhsT, r=rhs, s=start, t=stop: eng.matmul(o, l, r, start=s, stop=t),
                  reads, writes)

    def act(self, out, in_, func, reads, writes, bias=None, scale=None):
        kw = {}
        if bias is not None:
            kw["bias"] = bias
        if scale is not None:
            kw["scale"] = scale
        self.S.op("act", lambda eng, o=out, i=in_, f=func, kw=kw: eng.activation(o, i, f, **kw), reads, writes)

    def tt(self, e, out, in0, in1, op, reads, writes):
        self.S.op(e, lambda eng, o=out, a=in0, b=in1, p=op: eng.tensor_tensor(o, a, b, p), reads, writes)

    def ts(self, e, out, in0, s1, s2, op0, op1, reads, writes):
        if op1 is None:
            self.S.op(e, lambda eng, o=out, a=in0, x=s1, p=op0: eng.tensor_scalar(o, a, x, None, p), reads, writes)
        else:
            self.S.op(e, lambda eng, o=out, a=in0, x=s1, y=s2, p=op0, q=op1: eng.tensor_scalar(o, a, x, y, p, q),
                      reads, writes)

    def stt(self, out, in0, scalar, in1, op0, op1, reads, writes):
        self.S.op("dve", lambda eng, o=out, a=in0, s=scalar, b=in1, p=op0, q=op1:
                  eng.scalar_tensor_tensor(o, a, s, b, p, q), reads, writes)

    def copy(self, e, out, in_, reads, writes):
        if e == "act":
            self.S.op(e, lambda eng, o=out, i=in_: eng.copy(o, i), reads, writes)
        else:
            self.S.op(e, lambda eng, o=out, i=in_: eng.tensor_copy(o, i), reads, writes)

    def memset(self, e, ap, val, writes):
        self.S.op(e, lambda eng, a=ap, v=val: eng.memset(a, v), (), writes)

    def recip(self, out, in_, reads, writes):
        self.S.op("dve", lambda eng, o=out, i=in_: eng.reciprocal(o, i), reads, writes)

    def finish(self, dsems):
        self.S.finish(dsems)
        self.S.emit()
        self.es.close()
        return self.nc

    def phase_begin(self):
        self.phase = getattr(self, "phase", 0) + 1
        self._saved = self.es
        self.es = ExitStack()
        self.S.ptr = 0
        self.S.new_engine_sems()

    def phase_end(self):
        self.S.barrier()
        self.es.close()
        self.es = self._saved

    def scratch(self, name, shape, dt):
        return self.nc.dram_tensor(name, list(shape), dt).ap()


def load_weight_bf16(c, dst, src, dsem, wbuf, nsplit=1):
    kc = src.shape[0] // 128
    srcv = src.rearrange("(c p) n -> p c n", p=128)
    step = max(1, kc // nsplit)
    for k0 in range(0, kc, step):
        c.dma(dst[:, k0:k0 + step, :], srcv[:, k0:k0 + step, :], dsem, writes=[wbuf], q="pool")


def emit_rmsnorm(c, x, xb, n, sq, sqb, ones, onesb, ps, psb, rstd, rstdb, g, gb, h, hb):
    if isinstance(sq, list):
        for k in range(8):
            c.act(sq[k % 2][:, :n], x[:, k, :n], AF.Square, [xb], [sqb[k % 2]])
            c.mm(ps[:, :n], ones[:, :], sq[k % 2][:, :n], k == 0, k == 7, [sqb[k % 2], onesb], [psb])
    else:
        for k in range(8):
            c.act(sq[:, k, :n], x[:, k, :n], AF.Square, [xb], [sqb])
        for k in range(8):
            c.mm(ps[:, :n], ones[:, :], sq[:, k, :n], k == 0, k == 7, [sqb, onesb], [psb])
    c.act(rstd[:, :n], ps[:, :n], AF.Sqrt, [psb], [rstdb], bias=1e-6, scale=1.0 / D)
    c.recip(rstd[:, :n], rstd[:, :n], [rstdb], [rstdb])
    for k in range(8):
        c.stt(h[:, k, :n], x[:, k, :n], g[:, k:k + 1], rstd[:, :n], ALU.mult, ALU.mult, [xb, gb, rstdb], [hb])


def build_ffn(NT, TT=256, final=False):
    c = Ctx()
    nc = c.nc
    xT = c.dram_in("xT", [D, NT])
    gin = c.dram_in("g", [128, 8])
    w1 = c.dram_in("w1", [D, FF])
    w2 = c.dram_in("w2", [FF, D])
    if final:
        gfin = c.dram_in("gf", [128, 8])
    yT = c.dram_out("yT", [D, NT])
    xv = xT.rearrange("(c p) n -> p c n", p=128)
    yv = yT.rearrange("(c p) n -> p c n", p=128)

    w1s = c.sb("w1s", [128, 8, FF], BF16)
    w2s = c.sb("w2s", [128, 32, D], BF16)
    gs = c.sb("gs", [128, 8], F32)
    ones = c.sb("ones", [128, 128], BF16)
    w1b, w2b, gb, onesb = c.buf("w1"), c.buf("w2"), c.buf("g"), c.buf("ones")
    dw1, dw2, dg = c.S.dsem("w1"), c.S.dsem("w2"), c.S.dsem("g")
    c.dma(gs[:, :], gin, dg, writes=[gb])
    if final:
        gfs = c.sb("gfs", [128, 8], F32)
        gfb = c.buf("gf")
        dgf = c.S.dsem("gf")
        c.dma(gfs[:, :], gfin, dgf, writes=[gfb])
    c.memset("pool", ones[:, :], 1.0, [onesb])
    NX = 2
    xs = [c.sb("x%d" % i, [128, 8, TT], F32) for i in range(NX)]
    xbs = [c.buf("x%d" % i) for i in range(NX)]
    dxl = [c.S.dsem("xl%d" % i) for i in range(NX)]
    dxs = [c.S.dsem("xs%d" % i) for i in range(NX)]
    ntile = NT // TT
    c.dma(xs[0][:, :, :], xv[:, :, 0:TT], dxl[0], writes=[xbs[0]])
    load_weight_bf16(c, w1s, w1, dw1, w1b, nsplit=4)
    load_weight_bf16(c, w2s, w2, dw2, w2b, nsplit=4)

    sq = c.sb("sq", [128, 8, TT], BF16)
    h = c.sb("h", [128, 8, TT], BF16)
    hid = c.sb("hid", [128, 32, TT], BF16)
    rstd = c.sb("rstd", [128, TT], F32)
    NR = 2
    rl = [c.sb("rl%d" % i, [128, TT], F32) for i in range(NR)]
    rlb = [c.buf() for i in range(NR)]
    sqb, hb, hidb, rstdb = c.buf("sq"), c.buf("h"), c.buf("hid"), c.buf("rstd")
    NP = 3
    pss = [c.ps("ps%d" % i, [128, 512]) for i in range(NP)]
    psb = [c.buf("ps%d" % i) for i in range(NP)]
    psn = c.ps("psn", [128, 512])
    psnb = c.buf("psn")
    if final:
        xo = c.sb("xo", [128, 8, TT], F32)
        xob = c.buf("xo")
    pi = 0
    for t in range(ntile):
        s = t % NX
        x, xb = xs[s], xbs[s]
        if t + 1 < ntile:
            s2 = (t + 1) % NX
            c.dma(xs[s2][:, :, :], xv[:, :, (t + 1) * TT:(t + 2) * TT], dxl[s2], writes=[xbs[s2]])
        emit_rmsnorm(c, x, xb, TT, sq, sqb, ones, onesb, psn, psnb, rstd, rstdb, gs, gb, h, hb)
        for m in range(32):
            ps, pb = pss[pi % NP], psb[pi % NP]
            pi += 1
            for k in range(8):
                c.mm(ps[:, :TT], w1s[:, k, m * 128:(m + 1) * 128], h[:, k, :], k == 0, k == 7, [w1b, hb], [pb])
            r, rb = rl[m % NR], rlb[m % NR]
            c.act(r[:, :], ps[:, :TT], AF.Relu, [pb], [rb])
            c.tt("dve", hid[:, m, :], ps[:, :TT], r[:, :], ALU.mult, [pb, rb], [hidb])
        for n in range(8):
            ps, pb = pss[pi % NP], psb[pi % NP]
            pi += 1
            for m in range(32):
                c.mm(ps[:, :TT], w2s[:, m, n * 128:(n + 1) * 128], hid[:, m, :], m == 0, m == 31, [w2b, hidb], [pb])
            c.tt("dve", x[:, n, :], ps[:, :TT], x[:, n, :], ALU.add, [pb, xb], [xb])
        if final:
            emit_rmsnorm(c, x, xb, TT, sq, sqb, ones, onesb, psn, psnb, rstd, rstdb, gfs, gfb, xo, xob)
            c.dma(yv[:, :, t * TT:(t + 1) * TT], xo[:, :, :], dxs[s], reads=[xob])
        else:
            c.dma(yv[:, :, t * TT:(t + 1) * TT], x[:, :, :], dxs[s], reads=[xb])
    return c.finish(dxs)


_PROG_CACHE = {}


def get_prog(key, builder):
    if key not in _PROG_CACHE:
        _PROG_CACHE[key] = builder()
    return _PROG_CACHE[key]


def to_pc(v):
    return np.ascontiguousarray(np.asarray(v, np.float32).reshape(-1, 128).T)


def run_ffn(xT_cores, g, w1, w2, gf=None):
    NT = xT_cores[0].shape[1]
    final = gf is not None
    nc = get_prog(("ffn", NT, final), lambda: build_ffn(NT, final=final))
    base = {"g": to_pc(g), "w1": np.ascontiguousarray(w1, dtype=np.float32),
            "w2": np.ascontiguousarray(w2, dtype=np.float32)}
    if final:
        base["gf"] = to_pc(gf)
    in_maps = [dict(base, xT=np.ascontiguousarray(x)) for x in xT_cores]
    res = run_bass_kernel_spmd(nc, in_maps, core_ids=list(range(len(xT_cores))))
    return [r["yT"] for r in res.results]


E2 = 2048


def build_sgu(NT, TT=256):
    c = Ctx()
    xT = c.dram_in("xT", [D, NT])
    gin = c.dram_in("g", [128, 8])
    w_in = c.dram_in("w_in", [D, 2 * E2])
    b_u = c.dram_in("b_u", [128, 16])
    b_v = c.dram_in("b_v", [1, E2])
    lng = c.dram_in("ln_g", [128, 16])
    lnb = c.dram_in("ln_b", [128, 16])
    wsT = c.dram_in("wsT", [128, 16, 128])
    b_s = c.dram_in("b_s", [16, 128])
    w_out = c.dram_in("w_out", [E2, D])
    yT = c.dram_out("yT", [D, NT])
    xv = xT.rearrange("(c p) n -> p c n", p=128)
    yv = yT.rearrange("(c p) n -> p c n", p=128)
    NB = TT // 128

    wis = c.sb("wis", [128, 8, 2 * E2], BF16)
    wos = c.sb("wos", [128, 16, D], BF16)
    wss = c.sb("wss", [128, 16, 128], BF16)
    gs = c.sb("gs", [128, 8], F32)
    bus = c.sb("bus", [128, 16], F32)
    bvs = c.sb("bvs", [1, E2], F32)
    lgs = c.sb("lgs", [128, 16], F32)
    lbs = c.sb("lbs", [128, 16], F32)
    bss = c.sb("bss", [128, 16, 128], F32)
    biasT = c.sb("biasT", [128, 16, TT], F32)
    ones = c.sb("ones", [128, 128], BF16)
    ones32 = c.sb("ones32", [1, 128], F32)
    wib, wob, wsb, constb, onesb, biasb = c.buf("wi"), c.buf("wo"), c.buf("ws"), c.buf("const"), c.buf("ones"), c.buf("bias")
    dc = c.S.dsem("const")
    for dst, src in ((gs, gin), (bus, b_u), (bvs, b_v), (lgs, lng), (lbs, lnb)):
        c.dma(dst[:, :], src, dc, writes=[constb])
    c.dma(bss[:, :, :], b_s.partition_broadcast(128), dc, writes=[constb])
    c.memset("pool", ones[:, :], 1.0, [onesb])
    c.memset("pool", ones32[:, :], 1.0, [onesb])
    dws = c.S.dsem("ws")
    c.dma(wss[:, :, :], wsT, dws, writes=[wsb], q="pool")
    c.memset("pool", wss[64:128, :, 0:64], 0.0, [wsb])

    NX = 2
    xs = [c.sb("x%d" % i, [128, 8, TT], F32) for i in range(NX)]
    xbs = [c.buf("x%d" % i) for i in range(NX)]
    dxl = [c.S.dsem("xl%d" % i) for i in range(NX)]
    dxs = [c.S.dsem("xs%d" % i) for i in range(NX)]
    ntile = NT // TT
    c.dma(xs[0][:, :, :], xv[:, :, 0:TT], dxl[0], writes=[xbs[0]])
    dwi, dwo = c.S.dsem("wi"), c.S.dsem("wo")
    load_weight_bf16(c, wis, w_in, dwi, wib, nsplit=4)
    load_weight_bf16(c, wos, w_out, dwo, wob, nsplit=2)

    sq = c.sb("sq", [128, 8, TT], BF16)
    h = c.sb("h", [128, 8, TT], BF16)
    u = c.sb("u", [128, 16, TT], F32)
    vg = c.sb("vg", [128, E2], F32)
    vn = [c.sb("vn%d" % i, [128, E2], BF16) for i in range(NB)]
    y = c.sb("y", [128, 16, TT], BF16)
    tmp = [c.sb("tmp%d" % i, [128, TT], F32) for i in range(2)]
    rstd = c.sb("rstd", [128, TT], F32)
    stats = c.sb("stats", [128, 4, 6], F32)
    mv = c.sb("mv", [128, 2], F32)
    sqb, hb, ub, vgb, yb, rstdb, statb, mvb = (c.buf(n) for n in ("sq", "h", "u", "vg", "y", "rstd", "stats", "mv"))
    vnb = [c.buf("vn%d" % i) for i in range(NB)]
    tmpb = [c.buf() for i in range(2)]
    NP = 4
    pss = [c.ps("ps%d" % i, [128, 512]) for i in range(NP)]
    psb = [c.buf("ps%d" % i) for i in range(NP)]
    psn = c.ps("psn", [128, 512])
    psnb = c.buf("psn")

    for g in range(16):
        c.mm(psn[:, :128], ones[:, :], wss[:, g, :], True, True, [onesb, wsb], [psnb])
        for tb in range(NB):
            c.stt(biasT[:, g, tb * 128:(tb + 1) * 128], psn[:, :128], lbs[:, g:g + 1], bss[:, g, :],
                  ALU.mult, ALU.add, [psnb, constb], [biasb])

    pi = 0
    for t in range(ntile):
        s = t % NX
        x, xb = xs[s], xbs[s]
        if t + 1 < ntile:
            s2 = (t + 1) % NX
            c.dma(xs[s2][:, :, :], xv[:, :, (t + 1) * TT:(t + 2) * TT], dxl[s2], writes=[xbs[s2]])
        emit_rmsnorm(c, x, xb, TT, sq, sqb, ones, onesb, psn, psnb, rstd, rstdb, gs, constb, h, hb)
        for m in range(16):
            ps, pb = pss[pi % NP], psb[pi % NP]
            pi += 1
            for k in range(8):
                c.mm(ps[:, :TT], wis[:, k, m * 128:(m + 1) * 128], h[:, k, :], k == 0, k == 7, [wib, hb], [pb])
            c.act(u[:, m, :], ps[:, :TT], AF.Gelu, [pb, constb], [ub], bias=bus[:, m:m + 1])
        for tb in range(NB):
            for nb in range(4):
                ps, pb = pss[pi % NP], psb[pi % NP]
                pi += 1
                for k in range(8):
                    c.mm(ps[:, :], h[:, k, tb * 128:(tb + 1) * 128], wis[:, k, E2 + nb * 512:E2 + (nb + 1) * 512],
                         k == 0, False, [wib, hb], [pb])
                c.mm(ps[:, :], ones32[:, :], bvs[:, nb * 512:(nb + 1) * 512], False, True, [onesb, constb], [pb])
                c.act(vg[:, nb * 512:(nb + 1) * 512], ps[:, :], AF.Gelu, [pb], [vgb])
                c.S.op("dve", lambda eng, o=stats[:, nb, :], i=vg[:, nb * 512:(nb + 1) * 512]: eng.bn_stats(o, i),
                       [vgb], [statb])
            c.S.op("dve", lambda eng, o=mv[:, :], i=stats[:, :, :].rearrange("p a b -> p (a b)"): eng.bn_aggr(o, i),
                   [statb], [mvb])
            c.act(mv[:, 1:2], mv[:, 1:2], AF.Sqrt, [mvb], [mvb], bias=1e-5, scale=1.0)
            c.recip(mv[:, 1:2], mv[:, 1:2], [mvb], [mvb])
            c.ts("dve", vn[tb][:, :], vg[:, :], mv[:, 0:1], mv[:, 1:2], ALU.subtract, ALU.mult, [vgb, mvb], [vnb[tb]])
        for g in range(16):
            ps, pb = pss[pi % NP], psb[pi % NP]
            pi += 1
            for tb in range(NB):
                c.mm(ps[:, tb * 128:(tb + 1) * 128], vn[tb][:, g * 128:(g + 1) * 128], wss[:, g, :], True, True,
                     [vnb[tb], wsb], [pb])
            tm, tmb = tmp[g % 2], tmpb[g % 2]
            c.stt(tm[:, :], ps[:, :TT], lgs[:, g:g + 1], biasT[:, g, :], ALU.mult, ALU.add, [pb, constb, biasb], [tmb])
            c.tt("pool", y[:, g, :], tm[:, :], u[:, g, :], ALU.mult, [tmb, ub], [yb])
        for n in range(8):
            ps, pb = pss[pi % NP], psb[pi % NP]
            pi += 1
            for m in range(16):
                c.mm(ps[:, :TT], wos[:, m, n * 128:(n + 1) * 128], y[:, m, :], m == 0, m == 15, [wob, yb], [pb])
            c.tt("dve", x[:, n, :], ps[:, :TT], x[:, n, :], ALU.add, [pb, xb], [xb])
        c.dma(yv[:, :, t * TT:(t + 1) * TT], x[:, :, :], dxs[s], reads=[xb])
    return c.finish(dxs)


def run_sgu(xT_cores, g, w_in, b_in, ln_g, ln_b, w_s, b_s, w_out):
    NT = xT_cores[0].shape[1]
    nc = get_prog(("sgu", NT), lambda: build_sgu(NT))
    f = np.float32
    base = {"g": to_pc(g), "w_in": np.ascontiguousarray(w_in, dtype=f),
            "b_u": to_pc(np.asarray(b_in)[:E2]), "b_v": np.ascontiguousarray(np.asarray(b_in, f)[None, E2:]),
            "ln_g": to_pc(ln_g), "ln_b": to_pc(ln_b),
            "wsT": np.ascontiguousarray(np.transpose(np.asarray(w_s, f), (2, 0, 1))),
            "b_s": np.ascontiguousarray(b_s, dtype=f), "w_out": np.ascontiguousarray(w_out, dtype=f)}
    in_maps = [dict(base, xT=np.ascontiguousarray(x)) for x in xT_cores]
    res = run_bass_kernel_spmd(nc, in_maps, core_ids=list(range(len(xT_cores))))
    return [r["yT"] for r in res.results]


C0 = float(np.exp(-0.5))
CH = 64


def build_rwkv_proj(NT, vmix, TT=256):
    c = Ctx()
    xT = c.dram_in("xT", [D, NT + 1])
    gin = c.dram_in("g", [128, 8])
    mu = c.dram_in("mu", [128, 6, 8])
    w_rkv = c.dram_in("w_rkv", [3, D, D])
    pcs = {n: c.dram_in(n, [128, 8]) for n in ("w0", "a0", "k_k", "k_a", "r_k") + (("v0",) if vmix else ())}
    w_la = c.dram_in("w_la", [D, 64])
    w_lb = c.dram_in("w_lb", [64, D])
    a_la = c.dram_in("a_la", [D, 64])
    a_lb = c.dram_in("a_lb", [64, D])
    g_la = c.dram_in("g_la", [D, 128])
    g_lb = c.dram_in("g_lb", [128, D])
    if vmix:
        v_la = c.dram_in("v_la", [D, 32])
        v_lb = c.dram_in("v_lb", [32, D])
        vfT = c.dram_in("vfT", [D, NT])
    onames = ["at", "rt", "bt", "kt", "bh", "kh", "vv"]
    outs = {n: c.dram_out(n, [D, NT], BF16) for n in onames}
    o_gl = c.dram_out("gl", [D, NT // CH])
    o_bonus = c.dram_out("bonusv", [D, NT])
    o_gate = c.dram_out("gate", [D, NT])
    if not vmix:
        o_vf = c.dram_out("vfirst", [D, NT])
    fm = lambda ap: ap.rearrange("(c p) n -> p c n", p=128)
    xv = fm(xT)
    NCK = TT // CH

    wr = c.sb("wr", [128, 3, 8, D], BF16)
    wla = c.sb("wla", [128, 8, 64], BF16)
    ala = c.sb("ala", [128, 8, 64], BF16)
    gla = c.sb("gla", [128, 8, 128], BF16)
    wlb = c.sb("wlb", [64, D], BF16)
    alb = c.sb("alb", [64, D], BF16)
    glb = c.sb("glb", [128, D], BF16)
    wb, constb, onesb = c.buf("w"), c.buf("const"), c.buf("ones")
    dw, dc = c.S.dsem("w"), c.S.dsem("const")
    gs = c.sb("gs", [128, 8], F32)
    mus = c.sb("mus", [128, 6, 8], F32)
    c.dma(gs[:, :], gin, dc, writes=[constb])
    c.dma(mus[:, :, :], mu, dc, writes=[constb])
    pc = {}
    for n, ap in pcs.items():
        pc[n] = c.sb("pc_" + n, [128, 8], F32)
        c.dma(pc[n][:, :], ap, dc, writes=[constb])
    pc["omka"] = c.sb("pc_omka", [128, 8], F32)
    c.ts("dve", pc["omka"][:, :], pc["k_a"][:, :], -1.0, 1.0, ALU.mult, ALU.add, [constb], [constb])
    NX = 2
    xs_ = [c.sb("x%d" % i, [128, 8, TT + 1], F32) for i in range(NX)]
    xbs = [c.buf("x%d" % i) for i in range(NX)]
    dxl = [c.S.dsem("xl%d" % i) for i in range(NX)]
    c.dma(xs_[0][:, :, :], xv[:, :, 0:TT + 1], dxl[0], writes=[xbs[0]])
    for i in range(3):
        c.dma(wr[:, i, :, :], w_rkv[i].rearrange("(c p) n -> p c n", p=128), dw, writes=[wb], q="pool")
    for dst, src in ((wla, w_la), (ala, a_la), (gla, g_la)):
        c.dma(dst[:, :, :], src.rearrange("(c p) n -> p c n", p=128), dw, writes=[wb], q="pool")
    for dst, src in ((wlb, w_lb), (alb, a_lb), (glb, g_lb)):
        c.dma(dst[:, :], src, dw, writes=[wb], q="pool")
    if vmix:
        vla = c.sb("vla", [128, 8, 32], BF16)
        vlb = c.sb("vlb", [32, D], BF16)
        c.dma(vla[:, :, :], v_la.rearrange("(c p) n -> p c n", p=128), dw, writes=[wb], q="pool")
        c.dma(vlb[:, :], v_lb, dw, writes=[wb], q="pool")
    ones = c.sb("ones", [128, 128], BF16)
    bd = c.sb("bd", [128, 128], F32)
    mreset = c.sb("mreset", [128, TT], F32)
    c.memset("pool", ones[:, :], 1.0, [onesb])
    c.memset("pool", bd[:, :], 0.0, [onesb])
    c.memset("pool", bd[0:64, 0:64], 1.0, [onesb])
    c.memset("pool", bd[64:128, 64:128], 1.0, [onesb])
    c.memset("pool", mreset[:, :], 1.0, [onesb])
    c.memset("pool", mreset[:, :].rearrange("p (c t) -> p c t", t=CH)[:, :, 0:1], 0.0, [onesb])

    TE = TT + 1
    sq = c.sb("sq", [128, 8, TE], BF16)
    h32 = c.sb("h32", [128, 8, TE], F32)
    xx = c.sb("xx", [128, 8, TT], F32)
    rstd = c.sb("rstd", [128, TE], F32)
    xs6 = [c.sb("xs%d" % i, [128, 8, TT], BF16) for i in range(6)]
    sqb, hb, xxb, rstdb = c.buf("sq"), c.buf("h32"), c.buf("xx"), c.buf("rstd")
    xs6b = [c.buf("xs%d" % i) for i in range(6)]
    t_w = c.sb("t_w", [64, TT], BF16)
    t_a = c.sb("t_a", [64, TT], BF16)
    t_g = c.sb("t_g", [128, TT], BF16)
    t_wb, t_ab, t_gb = c.buf("t_w"), c.buf("t_a"), c.buf("t_g")
    if vmix:
        t_v = c.sb("t_v", [32, TT], BF16)
        t_vb = c.buf("t_v")
        vf = c.sb("vf", [128, 8, TT], F32)
        vfb = c.buf("vf")
        dvf = c.S.dsem("vf")
    tn = ["r32", "k32", "v32", "sg", "a", "cs", "e1", "e2", "gin_", "gex", "ginv", "gsuf", "kk0", "sqk", "nrm",
          "kk", "tk", "k", "b", "rkr", "sv", "d"]
    T = {n: c.sb("T_" + n, [128, TT], F32) for n in tn}
    Tb = {n: c.buf("T_" + n) for n in tn}
    stg = {n: c.sb("S_" + n, [128, 8, TT], BF16) for n in onames}
    stgb = {n: c.buf("S_" + n) for n in onames}
    dst_ = {n: c.S.dsem("o_" + n) for n in onames}
    s_gl = c.sb("s_gl", [128, 8, NCK], F32)
    s_bonus = c.sb("s_bonus", [128, 8, TT], F32)
    s_gate = c.sb("s_gate", [128, 8, TT], F32)
    s_glb, s_bonusb, s_gateb = c.buf("s_gl"), c.buf("s_bonus"), c.buf("s_gate")
    d_gl, d_bonus, d_gate = c.S.dsem("o_gl"), c.S.dsem("o_bonus"), c.S.dsem("o_gate")
    if not vmix:
        s_vf = c.sb("s_vf", [128, 8, TT], F32)
        s_vfb = c.buf("s_vf")
        d_vf = c.S.dsem("o_vf")
    psn = c.ps("psn", [128, 512])
    psnb = c.buf("psn")
    pnames = ["r", "k", "v", "w", "a", "g", "vm", "tw", "ta", "tg", "tv", "n2", "rk"]
    banks = [c.ps("bank%d" % i, [128, 512]) for i in range(7)]
    P = {n: banks[i // 2][:, (i % 2) * 256:(i % 2) * 256 + TT] for i, n in enumerate(pnames)}
    Pb = {n: c.buf("P_" + n) for n in pnames}

    ntile = NT // TT
    for t in range(ntile):
        s = t % NX
        x, xb = xs_[s], xbs[s]
        if t + 1 < ntile:
            s2 = (t + 1) % NX
            c.dma(xs_[s2][:, :, :], xv[:, :, (t + 1) * TT:(t + 1) * TT + TE], dxl[s2], writes=[xbs[s2]])
        if vmix:
            c.dma(vf[:, :, :], fm(vfT)[:, :, t * TT:(t + 1) * TT], dvf, writes=[vfb])
        emit_rmsnorm(c, x, xb, TE, sq, sqb, ones, onesb, psn, psnb, rstd, rstdb, gs, constb, h32, hb)
        for k in range(8):
            c.tt("pool", xx[:, k, :], h32[:, k, 0:TT], h32[:, k, 1:TE], ALU.subtract, [hb], [xxb])
        for i in range(6):
            for k in range(8):
                c.stt(xs6[i][:, k, :], xx[:, k, :], mus[:, i, k:k + 1], h32[:, k, 1:TE], ALU.mult, ALU.add,
                      [xxb, hb, constb], [xs6b[i]])
        for k in range(8):
            c.mm(P["tw"][0:64, :], wla[:, k, :], xs6[3][:, k, :], k == 0, k == 7, [wb, xs6b[3]], [Pb["tw"]])
        c.act(t_w[:, :], P["tw"][0:64, :], AF.Tanh, [Pb["tw"]], [t_wb])
        for k in range(8):
            c.mm(P["ta"][0:64, :], ala[:, k, :], xs6[4][:, k, :], k == 0, k == 7, [wb, xs6b[4]], [Pb["ta"]])
        c.copy("dve", t_a[:, :], P["ta"][0:64, :], [Pb["ta"]], [t_ab])
        for k in range(8):
            c.mm(P["tg"][:, :], gla[:, k, :], xs6[5][:, k, :], k == 0, k == 7, [wb, xs6b[5]], [Pb["tg"]])
        c.act(t_g[:, :], P["tg"][:, :], AF.Sigmoid, [Pb["tg"]], [t_gb])
        if vmix:
            for k in range(8):
                c.mm(P["tv"][0:32, :], vla[:, k, :], xs6[2][:, k, :], k == 0, k == 7, [wb, xs6b[2]], [Pb["tv"]])
            c.copy("dve", t_v[:, :], P["tv"][0:32, :], [Pb["tv"]], [t_vb])
        for m in range(8):
            ms = slice(m * 128, (m + 1) * 128)
            col = lambda n: pc[n][:, m:m + 1]
            for i, n in enumerate(("r", "k", "v")):
                for k in range(8):
                    c.mm(P[n][:, :], wr[:, i, k, ms], xs6[i][:, k, :], k == 0, k == 7, [wb, xs6b[i]], [Pb[n]])
            c.mm(P["w"][:, :], wlb[:, ms], t_w[:, :], True, True, [wb, t_wb], [Pb["w"]])
            c.mm(P["a"][:, :], alb[:, ms], t_a[:, :], True, True, [wb, t_ab], [Pb["a"]])
            c.mm(P["g"][:, :], glb[:, ms], t_g[:, :], True, True, [wb, t_gb], [Pb["g"]])
            if vmix:
                c.mm(P["vm"][:, :], vlb[:, ms], t_v[:, :], True, True, [wb, t_vb], [Pb["vm"]])
            c.act(T["sg"][:, :], P["w"][:, :], AF.Sigmoid, [Pb["w"], constb], [Tb["sg"]], bias=col("w0"))
            c.act(T["a"][:, :], P["a"][:, :], AF.Sigmoid, [Pb["a"], constb], [Tb["a"]], bias=col("a0"))
            if vmix:
                c.act(T["sv"][:, :], P["vm"][:, :], AF.Sigmoid, [Pb["vm"], constb], [Tb["sv"]], bias=col("v0"))
            c.copy("act", T["r32"][:, :], P["r"][:, :], [Pb["r"]], [Tb["r32"]])
            c.copy("act", T["k32"][:, :], P["k"][:, :], [Pb["k"]], [Tb["k32"]])
            c.copy("act", T["v32"][:, :], P["v"][:, :], [Pb["v"]], [Tb["v32"]])
            c.copy("act", s_gate[:, m, :], P["g"][:, :], [Pb["g"]], [s_gateb])
            if vmix:
                c.tt("pool", T["d"][:, :], vf[:, m, :], T["v32"][:, :], ALU.subtract, [vfb, Tb["v32"]], [Tb["d"]])
                c.tt("pool", T["d"][:, :], T["d"][:, :], T["sv"][:, :], ALU.mult, [Tb["d"], Tb["sv"]], [Tb["d"]])
                c.tt("pool", T["v32"][:, :], T["v32"][:, :], T["d"][:, :], ALU.add, [Tb["v32"], Tb["d"]], [Tb["v32"]])
            else:
                c.copy("pool", s_vf[:, m, :], T["v32"][:, :], [Tb["v32"]], [s_vfb])
            c.S.op("dve", lambda eng, o=T["cs"][:, :], a=mreset[:, :], b=T["sg"][:, :]:
                   eng.tensor_tensor_scan(o, a, b, 0.0, ALU.mult, ALU.add), [onesb, Tb["sg"]], [Tb["cs"]])
            cs3 = T["cs"][:, :].rearrange("p (c t) -> p c t", t=CH)
            c.tt("pool", T["e1"][:, :], T["cs"][:, :], T["sg"][:, :], ALU.subtract, [Tb["cs"], Tb["sg"]], [Tb["e1"]])
            c.tt("pool", T["e2"][:, :].rearrange("p (c t) -> p c t", t=CH), cs3[:, :, CH - 1:CH].to_broadcast([128, NCK, CH]),
                 cs3, ALU.subtract, [Tb["cs"]], [Tb["e2"]])
            c.act(T["gin_"][:, :], T["cs"][:, :], AF.Exp, [Tb["cs"]], [Tb["gin_"]], scale=-C0)
            c.act(T["ginv"][:, :], T["cs"][:, :], AF.Exp, [Tb["cs"]], [Tb["ginv"]], scale=C0)
            c.act(T["gex"][:, :], T["e1"][:, :], AF.Exp, [Tb["e1"]], [Tb["gex"]], scale=-C0)
            c.act(T["gsuf"][:, :], T["e2"][:, :], AF.Exp, [Tb["e2"]], [Tb["gsuf"]], scale=-C0)
            c.act(s_gl[:, m, :], cs3[:, :, CH - 1], AF.Exp, [Tb["cs"]], [s_glb], scale=-C0)
            c.ts("pool", T["kk0"][:, :], T["k32"][:, :], col("k_k"), None, ALU.mult, None, [Tb["k32"], constb], [Tb["kk0"]])
            c.tt("pool", T["sqk"][:, :], T["kk0"][:, :], T["kk0"][:, :], ALU.mult, [Tb["kk0"]], [Tb["sqk"]])
            c.mm(P["n2"][:, :], bd[:, :], T["sqk"][:, :], True, True, [onesb, Tb["sqk"]], [Pb["n2"]])
            c.act(T["nrm"][:, :], P["n2"][:, :], AF.Sqrt, [Pb["n2"]], [Tb["nrm"]])
            c.ts("dve", T["nrm"][:, :], T["nrm"][:, :], 1e-12, None, ALU.max, None, [Tb["nrm"]], [Tb["nrm"]])
            c.recip(T["nrm"][:, :], T["nrm"][:, :], [Tb["nrm"]], [Tb["nrm"]])
            c.tt("dve", T["kk"][:, :], T["kk0"][:, :], T["nrm"][:, :], ALU.mult, [Tb["kk0"], Tb["nrm"]], [Tb["kk"]])
            c.ts("dve", T["tk"][:, :], T["a"][:, :], col("k_a"), col("omka"), ALU.mult, ALU.add, [Tb["a"], constb], [Tb["tk"]])
            c.tt("dve", T["k"][:, :], T["k32"][:, :], T["tk"][:, :], ALU.mult, [Tb["k32"], Tb["tk"]], [Tb["k"]])
            c.tt("pool", T["b"][:, :], T["kk"][:, :], T["a"][:, :], ALU.mult, [Tb["kk"], Tb["a"]], [Tb["b"]])
            c.stt(stg["at"][:, m, :], T["kk"][:, :], -1.0, T["gex"][:, :], ALU.mult, ALU.mult, [Tb["kk"], Tb["gex"]], [stgb["at"]])
            c.tt("pool", stg["rt"][:, m, :], T["r32"][:, :], T["gin_"][:, :], ALU.mult, [Tb["r32"], Tb["gin_"]], [stgb["rt"]])
            c.tt("dve", stg["bt"][:, m, :], T["b"][:, :], T["ginv"][:, :], ALU.mult, [Tb["b"], Tb["ginv"]], [stgb["bt"]])
            c.tt("pool", stg["kt"][:, m, :], T["k"][:, :], T["ginv"][:, :], ALU.mult, [Tb["k"], Tb["ginv"]], [stgb["kt"]])
            c.tt("dve", stg["bh"][:, m, :], T["b"][:, :], T["gsuf"][:, :], ALU.mult, [Tb["b"], Tb["gsuf"]], [stgb["bh"]])
            c.tt("pool", stg["kh"][:, m, :], T["k"][:, :], T["gsuf"][:, :], ALU.mult, [Tb["k"], Tb["gsuf"]], [stgb["kh"]])
            c.copy("pool", stg["vv"][:, m, :], T["v32"][:, :], [Tb["v32"]], [stgb["vv"]])
            c.stt(T["rkr"][:, :], T["r32"][:, :], col("r_k"), T["k"][:, :], ALU.mult, ALU.mult, [Tb["r32"], Tb["k"], constb], [Tb["rkr"]])
            c.mm(P["rk"][:, :], bd[:, :], T["rkr"][:, :], True, True, [onesb, Tb["rkr"]], [Pb["rk"]])
            c.tt("dve", s_bonus[:, m, :], P["rk"][:, :], T["v32"][:, :], ALU.mult, [Pb["rk"], Tb["v32"]], [s_bonusb])
        tsl = slice(t * TT, (t + 1) * TT)
        for n in onames:
            c.dma(fm(outs[n])[:, :, tsl], stg[n][:, :, :], dst_[n], reads=[stgb[n]])
        c.dma(fm(o_gl)[:, :, t * NCK:(t + 1) * NCK], s_gl[:, :, :], d_gl, reads=[s_glb])
        c.dma(fm(o_bonus)[:, :, tsl], s_bonus[:, :, :], d_bonus, reads=[s_bonusb])
        c.dma(fm(o_gate)[:, :, tsl], s_gate[:, :, :], d_gate, reads=[s_gateb])
        if not vmix:
            c.dma(fm(o_vf)[:, :, tsl], s_vf[:, :, :], d_vf, reads=[s_vfb])
    return c.finish(list(dst_.values()) + [d_gl, d_bonus, d_gate] + ([] if vmix else [d_vf]))


def run_rwkv_proj(xT_ext_cores, p, vfT_cores=None):
    NT = xT_ext_cores[0].shape[1] - 1
    vmix = vfT_cores is not None
    nc = get_prog(("rproj", NT, vmix), lambda: build_rwkv_proj(NT, vmix))
    f = np.float32
    A = lambda a: np.ascontiguousarray(a, dtype=f)
    base = {"g": to_pc(p["g"]), "mu": np.ascontiguousarray(np.stack([to_pc(p["mu"][i]) for i in range(6)], axis=1)),
            "w_rkv": A(p["w_rkv"]), "w0": to_pc(p["w0"]), "a0": to_pc(p["a0"]), "k_k": to_pc(p["k_k"]),
            "k_a": to_pc(p["k_a"]), "r_k": to_pc(np.asarray(p["r_k"]).reshape(-1)),
            "w_la": A(p["w_la"]), "w_lb": A(p["w_lb"]), "a_la": A(p["a_la"]), "a_lb": A(p["a_lb"]),
            "g_la": A(p["g_la"]), "g_lb": A(p["g_lb"])}
    if vmix:
        base.update({"v0": to_pc(p["v0"]), "v_la": A(p["v_la"]), "v_lb": A(p["v_lb"])})
    in_maps = []
    for i, x in enumerate(xT_ext_cores):
        m = dict(base, xT=np.ascontiguousarray(x))
        if vmix:
            m["vfT"] = np.ascontiguousarray(vfT_cores[i])
        in_maps.append(m)
    res = run_bass_kernel_spmd(nc, in_maps, core_ids=list(range(len(in_maps))))
    return res.results


def build_rwkv_scan(T, NI=2):
    c = Ctx()
    NB = T // CH
    G = 4
    FMh = c.dram_in("FM", [NB, 64, G, 256], BF16)
    ABh = c.dram_in("AB", [NB, 64, G, 2, 64], BF16)
    VKh = c.dram_in("VK", [NB, 64, G, 2, 64], BF16)
    GLh = c.dram_in("GL", [64, NB * G])
    CMh = c.dram_in("CM", [128, 4, 64])
    yT = c.dram_out("yT", [G, 64, T])
    yv = yT.rearrange("h v t -> v h t")

    cm = c.sb("cm", [128, 4, 64], F32)
    gl = c.sb("gl", [64, NB * G], F32)
    constb = c.buf("const")
    dc = c.S.dsem("const")
    c.dma(cm[:, :, :], CMh, dc, writes=[constb])
    c.dma(gl[:, :], GLh, dc, writes=[constb])

    NSET = NI + 1
    sets = []
    for i in range(NSET):
        st = {}
        st["W"] = c.sb("W%d" % i, [64, G, 640], BF16)
        st["B"] = c.sb("B%d" % i, [128, G, 192], BF16)
        st["ZTt"] = [c.sb("ZTt%d_%d" % (i, j), [64, G, 128], BF16) for j in range(2)]
        st["ZT"] = [c.sb("ZT%d_%d" % (i, j), [64, G, 64], BF16) for j in range(2)]
        st["MT"] = c.sb("MT%d" % i, [64, G, 64], BF16)
        st["AN"] = c.sb("AN%d" % i, [64, G, 128], BF16)
        st["PW"] = c.sb("PW%d" % i, [128, G, 128], BF16)
        st["YT"] = c.sb("YT%d" % i, [64, G, 64], F32)
        for n in ("Wd", "Wa", "Wc", "Bd", "Bs", "Bp", "Bb", "ZTt0", "ZTt1", "ZT0", "ZT1", "MT", "AN", "PW", "YT"):
            st["b_" + n] = c.buf(n + str(i))
        st["dW"] = c.S.dsem("W%d" % i)
        st["dB"] = c.S.dsem("B%d" % i)
        st["dY"] = c.S.dsem("Y%d" % i)
        sets.append(st)
    NPS = min(NI, 2)
    pset = []
    for i in range(NPS):
        p = {"P1": c.ps("P1_%d" % i, [128, G, 128]), "PA": c.ps("PA_%d" % i, [64, G, 128]),
             "PB": c.ps("PB_%d" % i, [64, G, 128])}
        for n in ("P1", "PA", "PB"):
            p["b_" + n] = c.buf(n + str(i))
        pset.append(p)
    PSY = [c.ps("PSS", [64, G, 128]), c.ps("PSY", [64, G, 128])]
    psyb = [c.buf("pss"), c.buf("psy")]

    c.memset("pool", sets[0]["B"][0:64, :, 0:64], 0.0, [sets[0]["b_Bs"]])

    def batch(bi):
        st = sets[bi % NSET]
        nx = sets[(bi + 1) % NSET]
        p = pset[bi % NPS]
        W, B = st["W"], st["B"]
        c.dma(W[:, :, 0:256], FMh[bi], st["dW"], writes=[st["b_Wd"]])
        Wv = W[:, :, 256:640].rearrange("p g (a b) -> p g a b", b=64)
        c.dma(W[:, :, 320:384], ABh[bi][:, :, 0, :], st["dW"], writes=[st["b_Wd"]])
        c.dma(W[:, :, 576:640], ABh[bi][:, :, 1, :], st["dW"], writes=[st["b_Wd"]])
        Bv = B[64:128, :, :].rearrange("p g (a b) -> p g a b", b=64)
        c.dma(B[64:128, :, 0:64], VKh[bi][:, :, 0, :], st["dB"], writes=[st["b_Bd"]])
        c.dma(B[64:128, :, 128:192], VKh[bi][:, :, 1, :], st["dB"], writes=[st["b_Bd"]])
        c.copy("pool", B[0:64, :, 64:128], W[:, :, 192:256], [st["b_Wd"]], [st["b_Bp"]])
        c.tt("pool", B[0:64, :, 128:192], cm[0:64, 3:4, :].to_broadcast([64, G, 64]),
             gl[:, bi * G:(bi + 1) * G].unsqueeze(2).to_broadcast([64, G, 64]), ALU.mult, [constb], [st["b_Bp"]])
        yield
        for g in range(G):
            c.mm(p["P1"][:, g, :], W[:, g, 0:128], W[:, g, 128:256], True, True, [st["b_Wd"]], [p["b_P1"]])
        for g in range(G):
            c.mm(p["PA"][:, g, :], W[:, g, 128:192], W[:, g, 0:128], True, True, [st["b_Wd"]], [p["b_PA"]])
        c.tt("dve", W[:, :, 448:576], p["P1"][0:64, :, :],
             cm[0:64, 0:2, :].rearrange("p a b -> p (a b)").unsqueeze(1).to_broadcast([64, G, 128]), ALU.mult,
             [p["b_P1"], constb], [st["b_Wa"]])
        c.tt("dve", B[64:128, :, 64:128], p["P1"][64:128, :, 64:128], cm[64:128, 1:2, :].to_broadcast([64, G, 64]),
             ALU.mult, [p["b_P1"], constb], [st["b_Bb"]])
        c.tt("dve", Wv[:, :, 0:3:2, :], p["PA"][:, :, :].rearrange("p g (a b) -> p g a b", b=64),
             cm[0:64, 2:3, :].unsqueeze(1).to_broadcast([64, G, 2, 64]), ALU.mult, [p["b_PA"], constb], [st["b_Wc"]])
        c.tt("pool", st["ZTt"][1][:, :, 64:128], W[:, :, 448:512], cm[0:64, 3:4, :].to_broadcast([64, G, 64]), ALU.add,
             [st["b_Wa"], constb], [st["b_ZTt1"]])
        yield
        for g in range(G):
            c.mm(p["PA"][:, g, 0:64], W[:, g, 256:320], W[:, g, 448:512], True, True, [st["b_Wc"], st["b_Wa"]], [p["b_PA"]])
        for g in range(G):
            c.mm(p["PB"][:, g, 0:64], W[:, g, 448:512], W[:, g, 256:320], True, True, [st["b_Wc"], st["b_Wa"]], [p["b_PB"]])
        c.copy("act", st["ZTt"][1][:, :, 0:64], p["PA"][:, :, 0:64], [p["b_PA"]], [st["b_ZTt1"]])
        c.copy("dve", st["ZT"][1][:, :, :], p["PB"][:, :, 0:64], [p["b_PB"]], [st["b_ZT1"]])
        yield
        for j in range(1, 6):
            cur, nxt = j % 2, (j + 1) % 2
            ZTt, ZT = st["ZTt"][cur], st["ZT"][cur]
            bz, bzt = st["b_ZTt%d" % cur], st["b_ZT%d" % cur]
            nz, nzt = st["b_ZTt%d" % nxt], st["b_ZT%d" % nxt]
            last = j == 5
            for g in range(G):
                if j <= 3:
                    c.mm(p["PA"][:, g, :], ZT[:, g, :], ZTt[:, g, :], True, True, [bz, bzt], [p["b_PA"]])
                else:
                    c.mm(p["PA"][:, g, 64:128], ZT[:, g, :], ZTt[:, g, 64:128], True, True, [bz, bzt], [p["b_PA"]])
            if j <= 4:
                for g in range(G):
                    c.mm(p["PB"][:, g, 0:64], ZTt[:, g, 0:64], ZT[:, g, :], True, True, [bz, bzt], [p["b_PB"]])
            if j <= 3:
                c.copy("act", st["ZTt"][nxt][:, :, 0:64], p["PA"][:, :, 0:64], [p["b_PA"]], [nz])
            if last:
                c.tt("dve", st["MT"][:, :, :], p["PA"][:, :, 64:128], ZTt[:, :, 64:128], ALU.add, [p["b_PA"], bz], [st["b_MT"]])
            else:
                c.tt("dve", st["ZTt"][nxt][:, :, 64:128], p["PA"][:, :, 64:128], ZTt[:, :, 64:128], ALU.add, [p["b_PA"], bz], [nz])
            if j <= 4:
                c.copy("act", st["ZT"][nxt][:, :, :], p["PB"][:, :, 0:64], [p["b_PB"]], [nzt])
            yield
        for g in range(G):
            c.mm(p["PA"][:, g, :], st["MT"][:, g, :], W[:, g, 320:448], True, True, [st["b_MT"], st["b_Wd"], st["b_Wc"]], [p["b_PA"]])
        c.copy("act", st["AN"][:, :, :], p["PA"][:, :, :], [p["b_PA"]], [st["b_AN"]])
        yield
        for g in range(G):
            c.mm(p["P1"][:, g, :], st["AN"][:, g, :], W[:, g, 512:640], True, True, [st["b_AN"], st["b_Wa"], st["b_Wd"]], [p["b_P1"]])
        c.tt("dve", st["PW"][:, :, :], p["P1"][:, :, :], B[:, :, 64:192], ALU.add,
             [p["b_P1"], st["b_Bp"], st["b_Bb"], st["b_Bd"]], [st["b_PW"]])
        yield
        for g in range(G):
            c.mm(PSY[0][:, g, 0:64], st["PW"][:, g, 64:128], B[:, g, 0:64], True, True, [st["b_PW"], st["b_Bs"], st["b_Bd"]], [psyb[0]])
        for g in range(G):
            c.mm(PSY[1][:, g, 0:64], B[:, g, 0:64], st["PW"][:, g, 0:64], True, True, [st["b_PW"], st["b_Bs"], st["b_Bd"]], [psyb[1]])
        if bi + 1 < NB:
            c.copy("act", nx["B"][0:64, :, 0:64], PSY[0][:, :, 0:64], [psyb[0]], [nx["b_Bs"]])
        c.copy("dve", st["YT"][:, :, :], PSY[1][:, :, 0:64], [psyb[1]], [st["b_YT"]])
        c.dma(yv[:, :, bi * CH:(bi + 1) * CH], st["YT"][:, :, :], st["dY"], reads=[st["b_YT"]])
        yield

    gens = []
    nxt_b = 0
    while nxt_b < NB or gens:
        if len(gens) < NI and nxt_b < NB:
            gens.append(batch(nxt_b))
            nxt_b += 1
        alive = []
        for gnr in gens:
            try:
                next(gnr)
                alive.append(gnr)
            except StopIteration:
                pass
        gens = alive
    return c.finish([st["dY"] for st in sets])


def scan_consts():
    i = np.arange(64)
    mu_s = (i[:, None] < i[None, :]).astype(np.float32)
    mu_i = (i[:, None] <= i[None, :]).astype(np.float32)
    ml_s = (i[None, :] < i[:, None]).astype(np.float32)
    ident = np.eye(64, dtype=np.float32)
    cmh = np.stack([mu_s, mu_i, ml_s, ident], axis=1)
    return np.ascontiguousarray(np.concatenate([cmh, cmh], axis=0))


def prep_scan_core(at, rt, bt, kt, bh, kh, vv, gl):
    T = at.shape[1]
    NB = T // CH
    fmr = lambda a: a.reshape(4, 64, NB, 64)
    FM = np.stack([fmr(bt), fmr(kt), fmr(at), fmr(rt)], axis=0)
    FM = np.ascontiguousarray(FM.transpose(3, 2, 1, 0, 4)).reshape(NB, 64, 4, 256)
    tm = lambda a: a.reshape(4, 64, NB, 64).transpose(2, 3, 0, 1)
    AB = np.ascontiguousarray(np.stack([tm(at), tm(bh)], axis=3))
    VK = np.ascontiguousarray(np.stack([tm(vv), tm(kh)], axis=3))
    GL = np.ascontiguousarray(gl.reshape(4, 64, NB).transpose(1, 2, 0)).reshape(64, NB * 4)
    return {"FM": FM, "AB": AB, "VK": VK, "GL": GL.astype(np.float32), "CM": scan_consts()}


def run_rwkv_scan(core_inputs, NI=2):
    T = core_inputs[0]["FM"].shape[0] * CH
    nc = get_prog(("rscan", T, NI), lambda: build_rwkv_scan(T, NI))
    res = run_bass_kernel_spmd(nc, core_inputs, core_ids=list(range(len(core_inputs))))
    return [r["yT"] for r in res.results]


def build_rwkv_out(NT, TT=256):
    c = Ctx()
    xT = c.dram_in("xT", [D, NT])
    yTi = c.dram_in("yin", [D, NT])
    bon = c.dram_in("bonusv", [D, NT])
    gate = c.dram_in("gate", [D, NT])
    lng = c.dram_in("ln_g", [128, 8])
    lnb = c.dram_in("ln_b", [128, 8])
    w_o = c.dram_in("w_o", [D, D])
    yT = c.dram_out("yT", [D, NT])
    fm = lambda ap: ap.rearrange("(c p) n -> p c n", p=128)

    wos = c.sb("wos", [128, 8, D], BF16)
    lgs = c.sb("lgs", [128, 8], F32)
    lbs = c.sb("lbs", [128, 8], F32)
    bd = c.sb("bd", [128, 128], F32)
    wob, constb, onesb = c.buf("wo"), c.buf("const"), c.buf("ones")
    dc, dwo = c.S.dsem("const"), c.S.dsem("wo")
    c.dma(lgs[:, :], lng, dc, writes=[constb])
    c.dma(lbs[:, :], lnb, dc, writes=[constb])
    c.memset("pool", bd[:, :], 0.0, [onesb])
    c.memset("pool", bd[0:64, 0:64], 1.0, [onesb])
    c.memset("pool", bd[64:128, 64:128], 1.0, [onesb])
    NX = 2
    names = ("x", "y", "bo", "ga")
    srcs = {"x": fm(xT), "y": fm(yTi), "bo": fm(bon), "ga": fm(gate)}
    tiles = {n: [c.sb("%s%d" % (n, i), [128, 8, TT], F32) for i in range(NX)] for n in names}
    tb = {n: [c.buf("%s%d" % (n, i)) for i in range(NX)] for n in names}
    dl = {n: [c.S.dsem("l%s%d" % (n, i)) for i in range(NX)] for n in names}
    dxs = [c.S.dsem("xs%d" % i) for i in range(NX)]
    ntile = NT // TT

    def load(t):
        s = t % NX
        for n in names:
            c.dma(tiles[n][s][:, :, :], srcs[n][:, :, t * TT:(t + 1) * TT], dl[n][s], writes=[tb[n][s]])
    load(0)
    load_weight_bf16(c, wos, w_o, dwo, wob, nsplit=2)
    o = c.sb("o", [128, 8, TT], BF16)
    ob = c.buf("o")
    tn = ["ysq", "mean", "m2", "var", "d", "o1"]
    Tt = {n: c.sb("T_" + n, [128, TT], F32) for n in tn}
    Tb = {n: c.buf("T_" + n) for n in tn}
    pm = [c.ps("pm%d" % i, [128, 512]) for i in range(2)]
    pq = [c.ps("pq%d" % i, [128, 512]) for i in range(2)]
    pmb = [c.buf() for i in range(2)]
    pqb = [c.buf() for i in range(2)]
    NP = 3
    pss = [c.ps("ps%d" % i, [128, 512]) for i in range(NP)]
    psb = [c.buf("ps%d" % i) for i in range(NP)]
    pi = 0
    for t in range(ntile):
        s = t % NX
        if t + 1 < ntile:
            load(t + 1)
        x, y, bo, ga = (tiles[n][s] for n in names)
        xb, yb, bob, gab = (tb[n][s] for n in names)
        for m in range(8):
            P1, P1b, P2, P2b = pm[m % 2], pmb[m % 2], pq[m % 2], pqb[m % 2]
            c.tt("pool", Tt["ysq"][:, :], y[:, m, :], y[:, m, :], ALU.mult, [yb], [Tb["ysq"]])
            c.mm(P1[:, :TT], bd[:, :], y[:, m, :], True, True, [onesb, yb], [P1b])
            c.mm(P2[:, :TT], bd[:, :], Tt["ysq"][:, :], True, True, [onesb, Tb["ysq"]], [P2b])
            c.act(Tt["mean"][:, :], P1[:, :TT], AF.Copy, [P1b], [Tb["mean"]], scale=1.0 / 64)
            c.tt("pool", Tt["m2"][:, :], Tt["mean"][:, :], Tt["mean"][:, :], ALU.mult, [Tb["mean"]], [Tb["m2"]])
            c.stt(Tt["var"][:, :], P2[:, :TT], 1.0 / 64, Tt["m2"][:, :], ALU.mult, ALU.subtract, [P2b, Tb["m2"]], [Tb["var"]])
            c.act(Tt["var"][:, :], Tt["var"][:, :], AF.Sqrt, [Tb["var"]], [Tb["var"]], bias=64e-5, scale=1.0)
            c.recip(Tt["var"][:, :], Tt["var"][:, :], [Tb["var"]], [Tb["var"]])
            c.tt("pool", Tt["d"][:, :], y[:, m, :], Tt["mean"][:, :], ALU.subtract, [yb, Tb["mean"]], [Tb["d"]])
            c.tt("dve", Tt["d"][:, :], Tt["d"][:, :], Tt["var"][:, :], ALU.mult, [Tb["d"], Tb["var"]], [Tb["d"]])
            c.ts("dve", Tt["o1"][:, :], Tt["d"][:, :], lgs[:, m:m + 1], lbs[:, m:m + 1], ALU.mult, ALU.add, [Tb["d"], constb], [Tb["o1"]])
            c.tt("pool", Tt["o1"][:, :], Tt["o1"][:, :], bo[:, m, :], ALU.add, [Tb["o1"], bob], [Tb["o1"]])
            c.tt("dve", o[:, m, :], Tt["o1"][:, :], ga[:, m, :], ALU.mult, [Tb["o1"], gab], [ob])
        for n in range(8):
            ps, pb = pss[pi % NP], psb[pi % NP]
            pi += 1
            for m in range(8):
                c.mm(ps[:, :TT], wos[:, m, n * 128:(n + 1) * 128], o[:, m, :], m == 0, m == 7, [wob, ob], [pb])
            c.tt("dve", x[:, n, :], ps[:, :TT], x[:, n, :], ALU.add, [pb, xb], [xb])
        c.dma(fm(yT)[:, :, t * TT:(t + 1) * TT], x[:, :, :], dxs[s], reads=[xb])
    return c.finish(dxs)


def run_rwkv_out(cores, ln_g, ln_b, w_o):
    NT = cores[0]["xT"].shape[1]
    nc = get_prog(("rout", NT), lambda: build_rwkv_out(NT))
    base = {"ln_g": to_pc(ln_g), "ln_b": to_pc(ln_b), "w_o": np.ascontiguousarray(w_o, dtype=np.float32)}
    in_maps = [dict(base, **{k: np.ascontiguousarray(v) for k, v in cm.items()}) for cm in cores]
    res = run_bass_kernel_spmd(nc, in_maps, core_ids=list(range(len(in_maps))))
    return [r["yT"] for r in res.results]


def rwkv_layer(xT_tok, tok_cores, B, T, p, vfirst_tok=None):
    NT = xT_tok[0].shape[1]
    ext = []
    for i, (b, t0) in enumerate(tok_cores):
        if t0 == 0:
            prev = np.zeros((D, 1), np.float32)
        else:
            j = tok_cores.index((b, t0 - NT))
            prev = xT_tok[j][:, -1:]
        ext.append(np.concatenate([prev, xT_tok[i]], axis=1))
    pr = run_rwkv_proj(ext, p, vfirst_tok)
    names = ("at", "rt", "bt", "kt", "bh", "kh", "vv", "gl")
    full = {}
    for n in names:
        w = NT // CH if n == "gl" else NT
        arr = np.empty((B, D, (T // CH) if n == "gl" else T), dtype=pr[0][n].dtype)
        for i, (b, t0) in enumerate(tok_cores):
            o0 = (t0 // CH) if n == "gl" else t0
            arr[b, :, o0:o0 + w] = pr[i][n]
        full[n] = arr
    scan_cores = [(b, hg) for b in range(B) for hg in range(4)]
    y_full = np.empty((B, D, T), np.float32)
    for s0 in range(0, len(scan_cores), NCORES):
        grp = scan_cores[s0:s0 + NCORES]
        cin = [prep_scan_core(*[full[n][b, hg * 256:(hg + 1) * 256] for n in names]) for (b, hg) in grp]
        ys = run_rwkv_scan(cin)
        for (b, hg), yy in zip(grp, ys):
            y_full[b, hg * 256:(hg + 1) * 256] = yy.reshape(256, T)
    cores = []
    for i, (b, t0) in enumerate(tok_cores):
        cores.append({"xT": xT_tok[i], "yin": y_full[b, :, t0:t0 + NT], "bonusv": pr[i]["bonusv"], "gate": pr[i]["gate"]})
    out = run_rwkv_out(cores, p["ln_g"], p["ln_b"], p["w_o"])
    vf = vfirst_tok if vfirst_tok is not None else [pr[i]["vfirst"] for i in range(len(tok_cores))]
    return out, vf


def fm_ap(ap):
    return ap.rearrange("(c p) n -> p c n", p=128)


def emit_ffn(c, src, dst, NT, gin, w1, w2, add=None, gfin=None, TT=512):
    c.phase_begin()
    xv, yv = fm_ap(src), fm_ap(dst)
    w1s = c.sb("w1s", [128, 8, FF], BF16)
    w2s = c.sb("w2s", [128, 32, D], BF16)
    gs = c.sb("gs", [128, 8], F32)
    ones = c.sb("ones", [128, 128], BF16)
    w1b, w2b, gb, onesb = c.buf("w1"), c.buf("w2"), c.buf("g"), c.buf("ones")
    dw1, dw2, dg = c.S.dsem(), c.S.dsem(), c.S.dsem()
    c.dma(gs[:, :], gin, dg, writes=[gb])
    final = gfin is not None
    if final:
        gfs = c.sb("gfs", [128, 8], F32)
        c.dma(gfs[:, :], gfin, dg, writes=[gb])
    c.memset("pool", ones[:, :], 1.0, [onesb])
    NX = 2 if (add is None or TT < 512) else 1
    xs = [c.sb("x%d" % i, [128, 8, TT], F32) for i in range(NX)]
    xbs = [c.buf("x%d" % i) for i in range(NX)]
    dxl = [c.S.dsem() for i in range(NX)]
    dxs = [c.S.dsem() for i in range(NX)]
    if add is not None:
        av = fm_ap(add)
        ad = c.sb("ad", [128, 8, TT], F32)
        adb = c.buf("ad")
        dad = c.S.dsem()
    ntile = NT // TT

    def load(t):
        s = t % NX
        c.dma(xs[s][:, :, :], xv[:, :, t * TT:(t + 1) * TT], dxl[s], writes=[xbs[s]])
    load(0)
    load_weight_bf16(c, w1s, w1, dw1, w1b, nsplit=4)
    load_weight_bf16(c, w2s, w2, dw2, w2b, nsplit=4)
    sq = [c.sb("sq%d" % i, [128, TT], BF16) for i in range(2)]
    h = c.sb("h", [128, 8, TT], BF16)
    hid = c.sb("hid", [128, 32, TT], BF16)
    rstd = c.sb("rstd", [128, TT], F32)
    NR = 1 if TT >= 512 else 2
    rl = [c.sb("rl%d" % i, [128, TT], F32) for i in range(NR)]
    rlb = [c.buf() for i in range(NR)]
    hb, hidb, rstdb = c.buf("h"), c.buf("hid"), c.buf("rstd")
    sqb = [c.buf(), c.buf()]
    NP = 3
    pss = [c.ps("ps%d" % i, [128, 512]) for i in range(NP)]
    psb = [c.buf("ps%d" % i) for i in range(NP)]
    psn = c.ps("psn", [128, 512])
    psnb = c.buf("psn")
    pi = 0
    for t in range(ntile):
        s = t % NX
        x, xb = xs[s], xbs[s]
        if NX == 1:
            if t > 0:
                load(t)
        elif t + 1 < ntile:
            load(t + 1)
        if add is not None:
            c.dma(ad[:, :, :], av[:, :, t * TT:(t + 1) * TT], dad, writes=[adb])
            c.tt("pool", x[:, :, :], x[:, :, :], ad[:, :, :], ALU.add, [xb, adb], [xb])
        emit_rmsnorm(c, x, xb, TT, sq, sqb, ones, onesb, psn, psnb, rstd, rstdb, gs, gb, h, hb)
        for m in range(32):
            ps, pb = pss[pi % NP], psb[pi % NP]
            pi += 1
            for k in range(8):
                c.mm(ps[:, :TT], w1s[:, k, m * 128:(m + 1) * 128], h[:, k, :], k == 0, k == 7, [w1b, hb], [pb])
            r, rb = rl[m % NR], rlb[m % NR]
            c.act(r[:, :], ps[:, :TT], AF.Relu, [pb], [rb])
            c.tt("dve", hid[:, m, :], ps[:, :TT], r[:, :], ALU.mult, [pb, rb], [hidb])
        for n in range(8):
            ps, pb = pss[pi % NP], psb[pi % NP]
            pi += 1
            for m in range(32):
                c.mm(ps[:, :TT], w2s[:, m, n * 128:(n + 1) * 128], hid[:, m, :], m == 0, m == 31, [w2b, hidb], [pb])
            c.tt("dve", x[:, n, :], ps[:, :TT], x[:, n, :], ALU.add, [pb, xb], [xb])
        if final:
            emit_rmsnorm(c, x, xb, TT, sq, sqb, ones, onesb, psn, psnb, rstd, rstdb, gfs, gb, x, xb)
        c.dma(yv[:, :, t * TT:(t + 1) * TT], x[:, :, :], dxs[s], reads=[xb])
    c.phase_end()


def emit_sgu(c, src, dst, NT, W, TT=256):
    c.phase_begin()
    xv, yv = fm_ap(src), fm_ap(dst)
    NB = TT // 128
    wis = c.sb("wis", [128, 8, 2 * E2], BF16)
    wos = c.sb("wos", [128, 16, D], BF16)
    wss = c.sb("wss", [128, 16, 128], BF16)
    gs = c.sb("gs", [128, 8], F32)
    bus = c.sb("bus", [128, 16], F32)
    bvs = c.sb("bvs", [1, E2], F32)
    lgs = c.sb("lgs", [128, 16], F32)
    lbs = c.sb("lbs", [128, 16], F32)
    bss = c.sb("bss", [128, 16, 128], F32)
    biasT = c.sb("biasT", [128, 16, TT], F32)
    ones = c.sb("ones", [128, 128], BF16)
    ones32 = c.sb("ones32", [1, 128], F32)
    wib, wob, wsb, constb, onesb, biasb = c.buf("wi"), c.buf("wo"), c.buf("ws"), c.buf("const"), c.buf("ones"), c.buf("bias")
    dc = c.S.dsem()
    for dt_, src_ in ((gs, W["g"]), (bus, W["b_u"]), (bvs, W["b_v"]), (lgs, W["ln_g"]), (lbs, W["ln_b"])):
        c.dma(dt_[:, :], src_, dc, writes=[constb])
    c.dma(bss[:, :, :], W["b_s"].partition_broadcast(128), dc, writes=[constb])
    c.memset("pool", ones[:, :], 1.0, [onesb])
    c.memset("pool", ones32[:, :], 1.0, [onesb])
    dws = c.S.dsem()
    c.dma(wss[:, :, :], W["wsT"], dws, writes=[wsb], q="pool")
    c.memset("pool", wss[64:128, :, 0:64], 0.0, [wsb])
    NX = 2
    xs = [c.sb("x%d" % i, [128, 8, TT], F32) for i in range(NX)]
    xbs = [c.buf("x%d" % i) for i in range(NX)]
    dxl = [c.S.dsem() for i in range(NX)]
    dxs = [c.S.dsem() for i in range(NX)]
    ntile = NT // TT
    c.dma(xs[0][:, :, :], xv[:, :, 0:TT], dxl[0], writes=[xbs[0]])
    dwi, dwo = c.S.dsem(), c.S.dsem()
    load_weight_bf16(c, wis, W["w_in"], dwi, wib, nsplit=4)
    load_weight_bf16(c, wos, W["w_out"], dwo, wob, nsplit=2)
    sq = c.sb("sq", [128, 8, TT], BF16)
    h = c.sb("h", [128, 8, TT], BF16)
    u = c.sb("u", [128, 16, TT], F32)
    vg = c.sb("vg", [128, E2], F32)
    vn = [c.sb("vn%d" % i, [128, E2], BF16) for i in range(NB)]
    y = c.sb("y", [128, 16, TT], BF16)
    tmp = [c.sb("tmp%d" % i, [128, TT], F32) for i in range(2)]
    rstd = c.sb("rstd", [128, TT], F32)
    stats = c.sb("stats", [128, 4, 6], F32)
    mv = c.sb("mv", [128, 2], F32)
    sqb, hb, ub, vgb, yb, rstdb, statb, mvb = (c.buf(n) for n in ("sq", "h", "u", "vg", "y", "rstd", "stats", "mv"))
    vnb = [c.buf("vn%d" % i) for i in range(NB)]
    tmpb = [c.buf() for i in range(2)]
    NP = 4
    pss = [c.ps("ps%d" % i, [128, 512]) for i in range(NP)]
    psb = [c.buf("ps%d" % i) for i in range(NP)]
    psn = c.ps("psn", [128, 512])
    psnb = c.buf("psn")
    for g in range(16):
        c.mm(psn[:, :128], ones[:, :], wss[:, g, :], True, True, [onesb, wsb], [psnb])
        for tb in range(NB):
            c.stt(biasT[:, g, tb * 128:(tb + 1) * 128], psn[:, :128], lbs[:, g:g + 1], bss[:, g, :],
                  ALU.mult, ALU.add, [psnb, constb], [biasb])
    pi = 0
    for t in range(ntile):
        s = t % NX
        x, xb = xs[s], xbs[s]
        if t + 1 < ntile:
            s2 = (t + 1) % NX
            c.dma(xs[s2][:, :, :], xv[:, :, (t + 1) * TT:(t + 2) * TT], dxl[s2], writes=[xbs[s2]])
        emit_rmsnorm(c, x, xb, TT, sq, sqb, ones, onesb, psn, psnb, rstd, rstdb, gs, constb, h, hb)
        for m in range(16):
            ps, pb = pss[pi % NP], psb[pi % NP]
            pi += 1
            for k in range(8):
                c.mm(ps[:, :TT], wis[:, k, m * 128:(m + 1) * 128], h[:, k, :], k == 0, k == 7, [wib, hb], [pb])
            c.act(u[:, m, :], ps[:, :TT], AF.Gelu, [pb, constb], [ub], bias=bus[:, m:m + 1])
        for tb in range(NB):
            for nb in range(4):
                ps, pb = pss[pi % NP], psb[pi % NP]
                pi += 1
                for k in range(8):
                    c.mm(ps[:, :], h[:, k, tb * 128:(tb + 1) * 128], wis[:, k, E2 + nb * 512:E2 + (nb + 1) * 512],
                         k == 0, False, [wib, hb], [pb])
                c.mm(ps[:, :], ones32[:, :], bvs[:, nb * 512:(nb + 1) * 512], False, True, [onesb, constb], [pb])
                c.act(vg[:, nb * 512:(nb + 1) * 512], ps[:, :], AF.Gelu, [pb], [vgb])
                c.S.op("dve", lambda eng, o=stats[:, nb, :], i=vg[:, nb * 512:(nb + 1) * 512]: eng.bn_stats(o, i),
                       [vgb], [statb])
            c.S.op("dve", lambda eng, o=mv[:, :], i=stats[:, :, :].rearrange("p a b -> p (a b)"): eng.bn_aggr(o, i),
                   [statb], [mvb])
            c.act(mv[:, 1:2], mv[:, 1:2], AF.Sqrt, [mvb], [mvb], bias=1e-5, scale=1.0)
            c.recip(mv[:, 1:2], mv[:, 1:2], [mvb], [mvb])
            c.ts("dve", vn[tb][:, :], vg[:, :], mv[:, 0:1], mv[:, 1:2], ALU.subtract, ALU.mult, [vgb, mvb], [vnb[tb]])
        for g in range(16):
            ps, pb = pss[pi % NP], psb[pi % NP]
            pi += 1
            for tb in range(NB):
                c.mm(ps[:, tb * 128:(tb + 1) * 128], vn[tb][:, g * 128:(g + 1) * 128], wss[:, g, :], True, True,
                     [vnb[tb], wsb], [pb])
            tm, tmb = tmp[g % 2], tmpb[g % 2]
            c.stt(tm[:, :], ps[:, :TT], lgs[:, g:g + 1], biasT[:, g, :], ALU.mult, ALU.add, [pb, constb, biasb], [tmb])
            c.tt("pool", y[:, g, :], tm[:, :], u[:, g, :], ALU.mult, [tmb, ub], [yb])
        for n in range(8):
            ps, pb = pss[pi % NP], psb[pi % NP]
            pi += 1
            for m in range(16):
                c.mm(ps[:, :TT], wos[:, m, n * 128:(n + 1) * 128], y[:, m, :], m == 0, m == 15, [wob, yb], [pb])
            c.tt("dve", x[:, n, :], ps[:, :TT], x[:, n, :], ALU.add, [pb, xb], [xb])
        c.dma(yv[:, :, t * TT:(t + 1) * TT], x[:, :, :], dxs[s], reads=[xb])
    c.phase_end()


RG4 = [[0, 1, 2, 3], [4, 5, 6, 7]]


def emit_rwkv_a(c, src, NT, gin, hb_d, hall_d, TT=512):
    c.phase_begin()
    xv = fm_ap(src)
    TT = min(TT, NT)
    gs = c.sb("gs", [128, 8], F32)
    ones = c.sb("ones", [128, 128], BF16)
    gb, onesb = c.buf("g"), c.buf("ones")
    dg = c.S.dsem()
    c.dma(gs[:, :], gin, dg, writes=[gb])
    c.memset("pool", ones[:, :], 1.0, [onesb])
    NX = 2
    xs = [c.sb("x%d" % i, [128, 8, TT], F32) for i in range(NX)]
    hs = [c.sb("h%d" % i, [128, 8, TT], BF16) for i in range(NX)]
    xbs = [c.buf() for i in range(NX)]
    hbs = [c.buf() for i in range(NX)]
    dxl = [c.S.dsem() for i in range(NX)]
    dhs = [c.S.dsem() for i in range(NX)]
    sq = c.sb("sq", [128, 8, TT], BF16)
    rstd = c.sb("rstd", [128, TT], F32)
    sqb, rstdb = c.buf(), c.buf()
    psn = c.ps("psn", [128, 512])
    psnb = c.buf()
    hdb = c.buf("hb_d")
    for t in range(NT // TT):
        s = t % NX
        c.dma(xs[s][:, :, :], xv[:, :, t * TT:(t + 1) * TT], dxl[s], writes=[xbs[s]])
        emit_rmsnorm(c, xs[s], xbs[s], TT, sq, sqb, ones, onesb, psn, psnb, rstd, rstdb, gs, gb, hs[s], hbs[s])
        c.dma(fm_ap(hb_d)[:, :, t * TT:(t + 1) * TT], hs[s][:, :, :], dhs[s], reads=[hbs[s]], writes=[hdb])
    dcc = c.S.dsem()
    hallb = c.buf("hall")
    for k in range(8):
        c.S.op("pool", lambda eng, i=hb_d[k * 128:(k + 1) * 128, :], o=hall_d[k * 512:(k + 1) * 512, :]:
               eng.collective_compute("AllGather", ALU.bypass, replica_groups=RG4, ins=[i.opt()], outs=[o.opt()]),
               [hdb], [hallb], dsem=dcc, inc=1)
    c.phase_end()


def emit_rwkv_b(c, hall_d, NT, T, W, SC, vmix, TT=256):
    c.phase_begin()
    NM = 2
    DO = 256
    NCK = TT // CH
    hv = hall_d.rearrange("(c r p) n -> r p c n", p=128, c=8, r=4)
    wr = c.sb("wr", [128, 3, 8, DO], BF16)
    wla = c.sb("wla", [128, 8, 64], BF16)
    ala = c.sb("ala", [128, 8, 64], BF16)
    gla = c.sb("gla", [128, 8, 128], BF16)
    wlb = c.sb("wlb", [64, DO], BF16)
    alb = c.sb("alb", [64, DO], BF16)
    glb = c.sb("glb", [128, DO], BF16)
    wb, constb, onesb = c.buf("w"), c.buf("const"), c.buf("ones")
    dw, dc = c.S.dsem(), c.S.dsem()
    mus = c.sb("mus", [128, 6, 8], F32)
    c.dma(mus[:, :, :], W["mu"], dc, writes=[constb])
    pc = {}
    for n in ("w0", "a0", "k_k", "k_a", "r_k") + (("v0",) if vmix else ()):
        pc[n] = c.sb("pc_" + n, [128, NM], F32)
        c.dma(pc[n][:, :], W[n], dc, writes=[constb])
    pc["omka"] = c.sb("pc_omka", [128, NM], F32)
    c.ts("dve", pc["omka"][:, :], pc["k_a"][:, :], -1.0, 1.0, ALU.mult, ALU.add, [constb], [constb])
    for i in range(3):
        c.dma(wr[:, i, :, :], W["w_rkv"][i].rearrange("(c p) n -> p c n", p=128), dw, writes=[wb], q="pool")
    for dst_, src_ in ((wla, W["w_la"]), (ala, W["a_la"]), (gla, W["g_la"])):
        c.dma(dst_[:, :, :], src_.rearrange("(c p) n -> p c n", p=128), dw, writes=[wb], q="pool")
    for dst_, src_ in ((wlb, W["w_lb"]), (alb, W["a_lb"]), (glb, W["g_lb"])):
        c.dma(dst_[:, :], src_, dw, writes=[wb], q="pool")
    if vmix:
        vla = c.sb("vla", [128, 8, 32], BF16)
        vlb = c.sb("vlb", [32, DO], BF16)
        c.dma(vla[:, :, :], W["v_la"].rearrange("(c p) n -> p c n", p=128), dw, writes=[wb], q="pool")
        c.dma(vlb[:, :], W["v_lb"], dw, writes=[wb], q="pool")
    bd = c.sb("bd", [128, 128], F32)
    mreset = c.sb("mreset", [128, TT], F32)
    ident = c.sb("ident", [128, 128], BF16)
    c.memset("pool", bd[:, :], 0.0, [onesb])
    c.memset("pool", bd[0:64, 0:64], 1.0, [onesb])
    c.memset("pool", bd[64:128, 64:128], 1.0, [onesb])
    c.memset("pool", mreset[:, :], 1.0, [onesb])
    c.memset("pool", mreset[:, :].rearrange("p (c t) -> p c t", t=CH)[:, :, 0:1], 0.0, [onesb])
    c.dma(ident[:, :], W["ident"], dc, writes=[constb], q="pool")

    TE = TT + 1
    NX = 2
    hs = [c.sb("h%d" % i, [128, 8, TE], BF16) for i in range(NX)]
    hbs = [c.buf() for i in range(NX)]
    dhl = [c.S.dsem() for i in range(NX)]
    xx = c.sb("xx", [128, 8, TT], F32)
    xm = c.sb("xm", [128, 8, TT], F32)
    xxb, xmb = c.buf("xx"), c.buf("xm")
    xs6 = [c.sb("xs%d" % i, [128, 8, TT], BF16) for i in range(6)]
    xs6b = [c.buf("xs%d" % i) for i in range(6)]
    t_w = c.sb("t_w", [64, TT], BF16)
    t_a = c.sb("t_a", [64, TT], BF16)
    t_g = c.sb("t_g", [128, TT], BF16)
    t_wb, t_ab, t_gb = c.buf("t_w"), c.buf("t_a"), c.buf("t_g")
    if vmix:
        t_v = c.sb("t_v", [32, TT], BF16)
        t_vb = c.buf("t_v")
        vf = c.sb("vf", [128, NM, TT], F32)
        vfb = c.buf("vf")
        dvf = c.S.dsem()
    tn = ["r32", "k32", "v32", "sg", "a", "cs", "e1", "e2", "gin_", "gex", "ginv", "gsuf", "kk0", "sqk", "nrm",
          "kk", "tk", "k", "b", "rkr", "sv", "d"]
    Tt = {n: c.sb("T_" + n, [128, TT], F32) for n in tn}
    Tb = {n: c.buf("T_" + n) for n in tn}
    fnames = ["at", "rt", "bt", "kt"]
    tnames = ["at", "bh", "vv", "kh"]
    stg = {n: c.sb("S_" + n, [128, NM, TT], BF16) for n in ("at", "rt", "bt", "kt", "bh", "kh", "vv")}
    stgb = {n: c.buf("S_" + n) for n in stg}
    dst_ = {n: c.S.dsem() for n in fnames}
    STM = c.sb("STM", [128, 4, TT // 128, DO], BF16)
    STMb = c.buf("STM")
    dstm = c.S.dsem()
    s_gl = c.sb("s_gl", [128, NM, NCK], F32)
    s_bonus = c.sb("s_bonus", [128, NM, TT], F32)
    s_gate = c.sb("s_gate", [128, NM, TT], F32)
    s_glb, s_bonusb, s_gateb = c.buf(), c.buf(), c.buf()
    d_gl, d_bonus, d_gate = c.S.dsem(), c.S.dsem(), c.S.dsem()
    if not vmix:
        s_vf = c.sb("s_vf", [128, NM, TT], F32)
        s_vfb = c.buf()
        d_vf = c.S.dsem()
    pnames = ["r", "k", "v", "w", "a", "g", "vm", "tw", "ta", "tg", "tv", "n2", "rk"]
    banks = [c.ps("bank%d" % i, [128, 512]) for i in range(7)]
    P = {n: banks[i // 2][:, (i % 2) * 256:(i % 2) * 256 + TT] for i, n in enumerate(pnames)}
    Pb = {n: c.buf("P_" + n) for n in pnames}
    PT = c.ps("PT", [128, 4, TT // 128, 128], BF16)
    PTb = c.buf("PT")
    f2 = lambda ap: ap.rearrange("(c p) n -> p c n", p=128)

    def load_h(t):
        s = t % NX
        t0 = t * TT
        r, o = divmod(t0, NT)
        c.dma(hs[s][:, :, 1:TE], hv[r, :, :, o:o + TT], dhl[s], writes=[hbs[s]])
        if t0 == 0:
            c.memset("pool", hs[s][:, :, 0:1], 0.0, [hbs[s]])
        else:
            r2, o2 = divmod(t0 - 1, NT)
            c.dma(hs[s][:, :, 0:1], hv[r2, :, :, o2:o2 + 1], dhl[s], writes=[hbs[s]], slow=True)
    ntile = T // TT
    load_h(0)
    for t in range(ntile):
        s = t % NX
        h, hb = hs[s], hbs[s]
        if t + 1 < ntile:
            load_h(t + 1)
        tsl = slice(t * TT, (t + 1) * TT)
        if vmix:
            c.dma(vf[:, :, :], f2(SC["vfirst"])[:, :, tsl], dvf, writes=[vfb])
        c.tt("pool", xx[:, :, :], h[:, :, 0:TT], h[:, :, 1:TE], ALU.subtract, [hb], [xxb])
        for i in range(6):
            eng = "dve" if i % 2 == 0 else "pool"
            c.tt(eng, xm[:, :, :], xx[:, :, :], mus[:, i, :].unsqueeze(2).to_broadcast([128, 8, TT]), ALU.mult,
                 [xxb, constb], [xmb])
            c.tt(eng, xs6[i][:, :, :], xm[:, :, :], h[:, :, 1:TE], ALU.add, [xmb, hb], [xs6b[i]])
        for k in range(8):
            c.mm(P["tw"][0:64, :], wla[:, k, :], xs6[3][:, k, :], k == 0, k == 7, [wb, xs6b[3]], [Pb["tw"]])
        c.act(t_w[:, :], P["tw"][0:64, :], AF.Tanh, [Pb["tw"]], [t_wb])
        for k in range(8):
            c.mm(P["ta"][0:64, :], ala[:, k, :], xs6[4][:, k, :], k == 0, k == 7, [wb, xs6b[4]], [Pb["ta"]])
        c.copy("dve", t_a[:, :], P["ta"][0:64, :], [Pb["ta"]], [t_ab])
        for k in range(8):
            c.mm(P["tg"][:, :], gla[:, k, :], xs6[5][:, k, :], k == 0, k == 7, [wb, xs6b[5]], [Pb["tg"]])
        c.act(t_g[:, :], P["tg"][:, :], AF.Sigmoid, [Pb["tg"]], [t_gb])
        if vmix:
            for k in range(8):
                c.mm(P["tv"][0:32, :], vla[:, k, :], xs6[2][:, k, :], k == 0, k == 7, [wb, xs6b[2]], [Pb["tv"]])
            c.copy("dve", t_v[:, :], P["tv"][0:32, :], [Pb["tv"]], [t_vb])
        for m in range(NM):
            ms = slice(m * 128, (m + 1) * 128)
            col = lambda n: pc[n][:, m:m + 1]
            for i, n in enumerate(("r", "k", "v")):
                for k in range(8):
                    c.mm(P[n][:, :], wr[:, i, k, ms], xs6[i][:, k, :], k == 0, k == 7, [wb, xs6b[i]], [Pb[n]])
            c.mm(P["w"][:, :], wlb[:, ms], t_w[:, :], True, True, [wb, t_wb], [Pb["w"]])
            c.mm(P["a"][:, :], alb[:, ms], t_a[:, :], True, True, [wb, t_ab], [Pb["a"]])
            c.mm(P["g"][:, :], glb[:, ms], t_g[:, :], True, True, [wb, t_gb], [Pb["g"]])
            if vmix:
                c.mm(P["vm"][:, :], vlb[:, ms], t_v[:, :], True, True, [wb, t_vb], [Pb["vm"]])
            c.act(Tt["sg"][:, :], P["w"][:, :], AF.Sigmoid, [Pb["w"], constb], [Tb["sg"]], bias=col("w0"))
            c.act(Tt["a"][:, :], P["a"][:, :], AF.Sigmoid, [Pb["a"], constb], [Tb["a"]], bias=col("a0"))
            if vmix:
                c.act(Tt["sv"][:, :], P["vm"][:, :], AF.Sigmoid, [Pb["vm"], constb], [Tb["sv"]], bias=col("v0"))
            c.copy("act", Tt["r32"][:, :], P["r"][:, :], [Pb["r"]], [Tb["r32"]])
            c.copy("act", Tt["k32"][:, :], P["k"][:, :], [Pb["k"]], [Tb["k32"]])
            c.copy("act", Tt["v32"][:, :], P["v"][:, :], [Pb["v"]], [Tb["v32"]])
            c.copy("act", s_gate[:, m, :], P["g"][:, :], [Pb["g"]], [s_gateb])
            if vmix:
                c.tt("pool", Tt["d"][:, :], vf[:, m, :], Tt["v32"][:, :], ALU.subtract, [vfb, Tb["v32"]], [Tb["d"]])
                c.tt("pool", Tt["d"][:, :], Tt["d"][:, :], Tt["sv"][:, :], ALU.mult, [Tb["d"], Tb["sv"]], [Tb["d"]])
                c.tt("pool", Tt["v32"][:, :], Tt["v32"][:, :], Tt["d"][:, :], ALU.add, [Tb["v32"], Tb["d"]], [Tb["v32"]])
            else:
                c.copy("pool", s_vf[:, m, :], Tt["v32"][:, :], [Tb["v32"]], [s_vfb])
            c.S.op("dve", lambda eng, o=Tt["cs"][:, :], a=mreset[:, :], b=Tt["sg"][:, :]:
                   eng.tensor_tensor_scan(o, a, b, 0.0, ALU.mult, ALU.add), [onesb, Tb["sg"]], [Tb["cs"]])
            cs3 = Tt["cs"][:, :].rearrange("p (c t) -> p c t", t=CH)
            c.tt("pool", Tt["e1"][:, :], Tt["cs"][:, :], Tt["sg"][:, :], ALU.subtract, [Tb["cs"], Tb["sg"]], [Tb["e1"]])
            c.tt("pool", Tt["e2"][:, :].rearrange("p (c t) -> p c t", t=CH), cs3[:, :, CH - 1:CH].to_broadcast([128, NCK, CH]),
                 cs3, ALU.subtract, [Tb["cs"]], [Tb["e2"]])
            c.act(Tt["gin_"][:, :], Tt["cs"][:, :], AF.Exp, [Tb["cs"]], [Tb["gin_"]], scale=-C0)
            c.act(Tt["ginv"][:, :], Tt["cs"][:, :], AF.Exp, [Tb["cs"]], [Tb["ginv"]], scale=C0)
            c.act(Tt["gex"][:, :], Tt["e1"][:, :], AF.Exp, [Tb["e1"]], [Tb["gex"]], scale=-C0)
            c.act(Tt["gsuf"][:, :], Tt["e2"][:, :], AF.Exp, [Tb["e2"]], [Tb["gsuf"]], scale=-C0)
            c.act(s_gl[:, m, :], cs3[:, :, CH - 1], AF.Exp, [Tb["cs"]], [s_glb], scale=-C0)
            c.ts("pool", Tt["kk0"][:, :], Tt["k32"][:, :], col("k_k"), None, ALU.mult, None, [Tb["k32"], constb], [Tb["kk0"]])
            c.tt("pool", Tt["sqk"][:, :], Tt["kk0"][:, :], Tt["kk0"][:, :], ALU.mult, [Tb["kk0"]], [Tb["sqk"]])
            c.mm(P["n2"][:, :], bd[:, :], Tt["sqk"][:, :], True, True, [onesb, Tb["sqk"]], [Pb["n2"]])
            c.act(Tt["nrm"][:, :], P["n2"][:, :], AF.Sqrt, [Pb["n2"]], [Tb["nrm"]])
            c.ts("dve", Tt["nrm"][:, :], Tt["nrm"][:, :], 1e-12, None, ALU.max, None, [Tb["nrm"]], [Tb["nrm"]])
            c.recip(Tt["nrm"][:, :], Tt["nrm"][:, :], [Tb["nrm"]], [Tb["nrm"]])
            c.tt("dve", Tt["kk"][:, :], Tt["kk0"][:, :], Tt["nrm"][:, :], ALU.mult, [Tb["kk0"], Tb["nrm"]], [Tb["kk"]])
            c.ts("dve", Tt["tk"][:, :], Tt["a"][:, :], col("k_a"), col("omka"), ALU.mult, ALU.add, [Tb["a"], constb], [Tb["tk"]])
            c.tt("dve", Tt["k"][:, :], Tt["k32"][:, :], Tt["tk"][:, :], ALU.mult, [Tb["k32"], Tb["tk"]], [Tb["k"]])
            c.tt("pool", Tt["b"][:, :], Tt["kk"][:, :], Tt["a"][:, :], ALU.mult, [Tb["kk"], Tb["a"]], [Tb["b"]])
            c.stt(stg["at"][:, m, :], Tt["kk"][:, :], -1.0, Tt["gex"][:, :], ALU.mult, ALU.mult, [Tb["kk"], Tb["gex"]], [stgb["at"]])
            c.tt("pool", stg["rt"][:, m, :], Tt["r32"][:, :], Tt["gin_"][:, :], ALU.mult, [Tb["r32"], Tb["gin_"]], [stgb["rt"]])
            c.tt("dve", stg["bt"][:, m, :], Tt["b"][:, :], Tt["ginv"][:, :], ALU.mult, [Tb["b"], Tb["ginv"]], [stgb["bt"]])
            c.tt("pool", stg["kt"][:, m, :], Tt["k"][:, :], Tt["ginv"][:, :], ALU.mult, [Tb["k"], Tb["ginv"]], [stgb["kt"]])
            c.tt("dve", stg["bh"][:, m, :], Tt["b"][:, :], Tt["gsuf"][:, :], ALU.mult, [Tb["b"], Tb["gsuf"]], [stgb["bh"]])
            c.tt("pool", stg["kh"][:, m, :], Tt["k"][:, :], Tt["gsuf"][:, :], ALU.mult, [Tb["k"], Tb["gsuf"]], [stgb["kh"]])
            c.copy("pool", stg["vv"][:, m, :], Tt["v32"][:, :], [Tb["v32"]], [stgb["vv"]])
            c.stt(Tt["rkr"][:, :], Tt["r32"][:, :], col("r_k"), Tt["k"][:, :], ALU.mult, ALU.mult, [Tb["r32"], Tb["k"], constb], [Tb["rkr"]])
            c.mm(P["rk"][:, :], bd[:, :], Tt["rkr"][:, :], True, True, [onesb, Tb["rkr"]], [Pb["rk"]])
            c.tt("dve", s_bonus[:, m, :], P["rk"][:, :], Tt["v32"][:, :], ALU.mult, [Pb["rk"], Tb["v32"]], [s_bonusb])
            for j, n in enumerate(tnames):
                for blk in range(TT // 128):
                    c.S.op("pe", lambda eng, o=PT[:, j, blk, :], i=stg[n][:, m, blk * 128:(blk + 1) * 128], idn=ident[:, :]:
                           eng.transpose(o, i, idn), [stgb[n], constb], [PTb])
            c.copy("act", STM[:, :, :, ms], PT[:, :, :, :], [PTb], [STMb])
        for n in fnames:
            c.dma(f2(SC[n])[:, :, tsl], stg[n][:, :, :], dst_[n], reads=[stgb[n]])
        for j, n in enumerate(tnames):
            c.dma(SC[n + "_tm"][t * TT:(t + 1) * TT, :].rearrange("(b p) c -> p b c", p=128), STM[:, j, :, :], dstm, reads=[STMb])
        c.dma(f2(SC["gl"])[:, :, t * NCK:(t + 1) * NCK], s_gl[:, :, :], d_gl, reads=[s_glb])
        c.dma(f2(SC["bonusv"])[:, :, tsl], s_bonus[:, :, :], d_bonus, reads=[s_bonusb])
        c.dma(f2(SC["gate"])[:, :, tsl], s_gate[:, :, :], d_gate, reads=[s_gateb])
        if not vmix:
            c.dma(f2(SC["vfirst"])[:, :, tsl], s_vf[:, :, :], d_vf, reads=[s_vfb])
    c.phase_end()


def emit_rwkv_scan(c, T, SC, CMh, NI=2):
    c.phase_begin()
    NB = T // CH
    G = 4
    yv = SC["ysc"].rearrange("h v t -> v h t")
    cm = c.sb("cm", [128, 4, 64], F32)
    gl = c.sb("gl", [64, G, NB], F32)
    constb = c.buf("const")
    dc = c.S.dsem()
    c.dma(cm[:, :, :], CMh, dc, writes=[constb])
    c.dma(gl[:, :, :], SC["gl"].rearrange("(h k) n -> k h n", k=64), dc, writes=[constb])
    fmv = {n: SC[n].rearrange("(h k) t -> k h t", k=64) for n in ("bt", "kt", "at", "rt")}
    tmv = {n: SC[n + "_tm"].rearrange("t (h k) -> t h k", k=64) for n in ("at", "bh", "vv", "kh")}
    NSET = NI + 1
    sets = []
    for i in range(NSET):
        st = {}
        st["W"] = c.sb("W%d" % i, [64, G, 640], BF16)
        st["B"] = c.sb("B%d" % i, [128, G, 192], BF16)
        st["ZTt"] = [c.sb("ZTt%d_%d" % (i, j), [64, G, 128], BF16) for j in range(2)]
        st["ZT"] = [c.sb("ZT%d_%d" % (i, j), [64, G, 64], BF16) for j in range(2)]
        st["MT"] = c.sb("MT%d" % i, [64, G, 64], BF16)
        st["AN"] = c.sb("AN%d" % i, [64, G, 128], BF16)
        st["PW"] = c.sb("PW%d" % i, [128, G, 128], BF16)
        st["YT"] = c.sb("YT%d" % i, [64, G, 64], F32)
        for n in ("Wd", "Wa", "Wc", "Bd", "Bs", "Bp", "Bb", "ZTt0", "ZTt1", "ZT0", "ZT1", "MT", "AN", "PW", "YT"):
            st["b_" + n] = c.buf(n + str(i))
        st["dW"] = c.S.dsem()
        st["dB"] = c.S.dsem()
        st["dY"] = c.S.dsem()
        sets.append(st)
    NPS = min(NI, 2)
    pset = []
    for i in range(NPS):
        p = {"P1": c.ps("P1_%d" % i, [128, G, 128]), "PA": c.ps("PA_%d" % i, [64, G, 128]),
             "PB": c.ps("PB_%d" % i, [64, G, 128])}
        for n in ("P1", "PA", "PB"):
            p["b_" + n] = c.buf(n + str(i))
        pset.append(p)
    PSY = [c.ps("PSS", [64, G, 128]), c.ps("PSY", [64, G, 128])]
    psyb = [c.buf("pss"), c.buf("psy")]
    c.memset("pool", sets[0]["B"][0:64, :, 0:64], 0.0, [sets[0]["b_Bs"]])

    def batch(bi):
        st = sets[bi % NSET]
        nx = sets[(bi + 1) % NSET]
        p = pset[bi % NPS]
        W, B = st["W"], st["B"]
        csl = slice(bi * CH, (bi + 1) * CH)
        for j, n in enumerate(("bt", "kt", "at", "rt")):
            c.dma(W[:, :, j * 64:(j + 1) * 64], fmv[n][:, :, csl], st["dW"], writes=[st["b_Wd"]])
        c.dma(W[:, :, 320:384], tmv["at"][csl, :, :], st["dW"], writes=[st["b_Wd"]])
        c.dma(W[:, :, 576:640], tmv["bh"][csl, :, :], st["dW"], writes=[st["b_Wd"]])
        c.dma(B[64:128, :, 0:64], tmv["vv"][csl, :, :], st["dB"], writes=[st["b_Bd"]])
        c.dma(B[64:128, :, 128:192], tmv["kh"][csl, :, :], st["dB"], writes=[st["b_Bd"]])
        Wv = W[:, :, 256:640].rearrange("p g (a b) -> p g a b", b=64)
        c.copy("pool", B[0:64, :, 64:128], W[:, :, 192:256], [st["b_Wd"]], [st["b_Bp"]])
        c.tt("pool", B[0:64, :, 128:192], cm[0:64, 3:4, :].to_broadcast([64, G, 64]),
             gl[:, :, bi:bi + 1].to_broadcast([64, G, 64]), ALU.mult, [constb], [st["b_Bp"]])
        yield
        for g in range(G):
            c.mm(p["P1"][:, g, :], W[:, g, 0:128], W[:, g, 128:256], True, True, [st["b_Wd"]], [p["b_P1"]])
        for g in range(G):
            c.mm(p["PA"][:, g, :], W[:, g, 128:192], W[:, g, 0:128], True, True, [st["b_Wd"]], [p["b_PA"]])
        c.tt("dve", W[:, :, 448:576], p["P1"][0:64, :, :],
             cm[0:64, 0:2, :].rearrange("p a b -> p (a b)").unsqueeze(1).to_broadcast([64, G, 128]), ALU.mult,
             [p["b_P1"], constb], [st["b_Wa"]])
        c.tt("dve", B[64:128, :, 64:128], p["P1"][64:128, :, 64:128], cm[64:128, 1:2, :].to_broadcast([64, G, 64]),
             ALU.mult, [p["b_P1"], constb], [st["b_Bb"]])
        c.tt("dve", Wv[:, :, 0:3:2, :], p["PA"][:, :, :].rearrange("p g (a b) -> p g a b", b=64),
             cm[0:64, 2:3, :].unsqueeze(1).to_broadcast([64, G, 2, 64]), ALU.mult, [p["b_PA"], constb], [st["b_Wc"]])
        c.tt("pool", st["ZTt"][1][:, :, 64:128], W[:, :, 448:512], cm[0:64, 3:4, :].to_broadcast([64, G, 64]), ALU.add,
             [st["b_Wa"], constb], [st["b_ZTt1"]])
        yield
        for g in range(G):
            c.mm(p["PA"][:, g, 0:64], W[:, g, 256:320], W[:, g, 448:512], True, True, [st["b_Wc"], st["b_Wa"]], [p["b_PA"]])
        for g in range(G):
            c.mm(p["PB"][:, g, 0:64], W[:, g, 448:512], W[:, g, 256:320], True, True, [st["b_Wc"], st["b_Wa"]], [p["b_PB"]])
        c.copy("act", st["ZTt"][1][:, :, 0:64], p["PA"][:, :, 0:64], [p["b_PA"]], [st["b_ZTt1"]])
        c.copy("dve", st["ZT"][1][:, :, :], p["PB"][:, :, 0:64], [p["b_PB"]], [st["b_ZT1"]])
        yield
        for j in range(1, 6):
            cur, nxt = j % 2, (j + 1) % 2
            ZTt, ZT = st["ZTt"][cur], st["ZT"][cur]
            bz, bzt = st["b_ZTt%d" % cur], st["b_ZT%d" % cur]
            nz, nzt = st["b_ZTt%d" % nxt], st["b_ZT%d" % nxt]
            last = j == 5
            for g in range(G):
                if j <= 3:
                    c.mm(p["PA"][:, g, :], ZT[:, g, :], ZTt[:, g, :], True, True, [bz, bzt], [p["b_PA"]])
                else:
                    c.mm(p["PA"][:, g, 64:128], ZT[:, g, :], ZTt[:, g, 64:128], True, True, [bz, bzt], [p["b_PA"]])
            if j <= 4:
                for g in range(G):
                    c.mm(p["PB"][:, g, 0:64], ZTt[:, g, 0:64], ZT[:, g, :], True, True, [bz, bzt], [p["b_PB"]])
            if j <= 3:
                c.copy("act", st["ZTt"][nxt][:, :, 0:64], p["PA"][:, :, 0:64], [p["b_PA"]], [nz])
            if last:
                c.tt("dve", st["MT"][:, :, :], p["PA"][:, :, 64:128], ZTt[:, :, 64:128], ALU.add, [p["b_PA"], bz], [st["b_MT"]])
            else:
                c.tt("dve", st["ZTt"][nxt][:, :, 64:128], p["PA"][:, :, 64:128], ZTt[:, :, 64:128], ALU.add, [p["b_PA"], bz], [nz])
            if j <= 4:
                c.copy("act", st["ZT"][nxt][:, :, :], p["PB"][:, :, 0:64], [p["b_PB"]], [nzt])
            yield
        for g in range(G):
            c.mm(p["PA"][:, g, :], st["MT"][:, g, :], W[:, g, 320:448], True, True, [st["b_MT"], st["b_Wd"], st["b_Wc"]], [p["b_PA"]])
        c.copy("act", st["AN"][:, :, :], p["PA"][:, :, :], [p["b_PA"]], [st["b_AN"]])
        yield
        for g in range(G):
            c.mm(p["P1"][:, g, :], st["AN"][:, g, :], W[:, g, 512:640], True, True, [st["b_AN"], st["b_Wa"], st["b_Wd"]], [p["b_P1"]])
        c.tt("dve", st["PW"][:, :, :], p["P1"][:, :, :], B[:, :, 64:192], ALU.add,
             [p["b_P1"], st["b_Bp"], st["b_Bb"], st["b_Bd"]], [st["b_PW"]])
        yield
        for g in range(G):
            c.mm(PSY[0][:, g, 0:64], st["PW"][:, g, 64:128], B[:, g, 0:64], True, True, [st["b_PW"], st["b_Bs"], st["b_Bd"]], [psyb[0]])
        for g in range(G):
            c.mm(PSY[1][:, g, 0:64], B[:, g, 0:64], st["PW"][:, g, 0:64], True, True, [st["b_PW"], st["b_Bs"], st["b_Bd"]], [psyb[1]])
        if bi + 1 < NB:
            c.copy("act", nx["B"][0:64, :, 0:64], PSY[0][:, :, 0:64], [psyb[0]], [nx["b_Bs"]])
        c.copy("dve", st["YT"][:, :, :], PSY[1][:, :, 0:64], [psyb[1]], [st["b_YT"]])
        c.dma(yv[:, :, csl], st["YT"][:, :, :], st["dY"], reads=[st["b_YT"]])
        yield

    gens = []
    nxt_b = 0
    while nxt_b < NB or gens:
        if len(gens) < NI and nxt_b < NB:
            gens.append(batch(nxt_b))
            nxt_b += 1
        alive = []
        for gnr in gens:
            try:
                next(gnr)
                alive.append(gnr)
            except StopIteration:
                pass
        gens = alive
    c.phase_end()


def emit_rwkv_d(c, NT, T, W, SC, part_d, rs_d, TT=512):
    c.phase_begin()
    TT = min(TT, NT)
    NM = 2
    f2 = lambda ap: ap.rearrange("(c p) n -> p c n", p=128)
    wos = c.sb("wos", [128, NM, D], BF16)
    lgs = c.sb("lgs", [128, NM], F32)
    lbs = c.sb("lbs", [128, NM], F32)
    bd = c.sb("bd", [128, 128], F32)
    wob, constb, onesb = c.buf("wo"), c.buf("const"), c.buf("ones")
    dc, dwo = c.S.dsem(), c.S.dsem()
    c.dma(lgs[:, :], W["ln_g"], dc, writes=[constb])
    c.dma(lbs[:, :], W["ln_b"], dc, writes=[constb])
    c.memset("pool", bd[:, :], 0.0, [onesb])
    c.memset("pool", bd[0:64, 0:64], 1.0, [onesb])
    c.memset("pool", bd[64:128, 64:128], 1.0, [onesb])
    c.dma(wos[:, :, :], W["w_o"].rearrange("(c p) n -> p c n", p=128), dwo, writes=[wob], q="pool")
    NX = 2
    names = ("y", "bo", "ga")
    srcs = {"y": f2(SC["ysc"].rearrange("h v t -> (h v) t")), "bo": f2(SC["bonusv"]), "ga": f2(SC["gate"])}
    tiles = {n: [c.sb("%s%d" % (n, i), [128, NM, TT], F32) for i in range(NX)] for n in names}
    tb = {n: [c.buf() for i in range(NX)] for n in names}
    dl = {n: [c.S.dsem() for i in range(NX)] for n in names}
    ntile = T // TT

    def load(t):
        s = t % NX
        for n in names:
            c.dma(tiles[n][s][:, :, :], srcs[n][:, :, t * TT:(t + 1) * TT], dl[n][s], writes=[tb[n][s]])
    load(0)
    o = c.sb("o", [128, NM, TT], BF16)
    ob = c.buf("o")
    tn = ["ysq", "mean", "m2", "var", "d", "o1"]
    Tt = {n: c.sb("T_" + n, [128, TT], F32) for n in tn}
    Tb = {n: c.buf("T_" + n) for n in tn}
    pm = [c.ps("pm%d" % i, [128, 512]) for i in range(2)]
    pq = [c.ps("pq%d" % i, [128, 512]) for i in range(2)]
    pmb = [c.buf() for i in range(2)]
    pqb = [c.buf() for i in range(2)]
    NP = 3
    pss = [c.ps("ps%d" % i, [128, 512]) for i in range(NP)]
    psb = [c.buf() for i in range(NP)]
    NS = 2
    stage = [c.sb("stage%d" % i, [128, 8, TT], F32) for i in range(NS)]
    stageb = [c.buf() for i in range(NS)]
    dst_ = [c.S.dsem() for i in range(NS)]
    partb = c.buf("part")
    pv = part_d.rearrange("(s c p) n -> s p c n", p=128, c=8)
    pi = 0
    for t in range(ntile):
        s = t % NX
        if t + 1 < ntile:
            load(t + 1)
        y, bo, ga = (tiles[n][s] for n in names)
        yb, bob, gab = (tb[n][s] for n in names)
        for m in range(NM):
            P1, P1b, P2, P2b = pm[m % 2], pmb[m % 2], pq[m % 2], pqb[m % 2]
            c.tt("pool", Tt["ysq"][:, :], y[:, m, :], y[:, m, :], ALU.mult, [yb], [Tb["ysq"]])
            c.mm(P1[:, :TT], bd[:, :], y[:, m, :], True, True, [onesb, yb], [P1b])
            c.mm(P2[:, :TT], bd[:, :], Tt["ysq"][:, :], True, True, [onesb, Tb["ysq"]], [P2b])
            c.act(Tt["mean"][:, :], P1[:, :TT], AF.Copy, [P1b], [Tb["mean"]], scale=1.0 / 64)
            c.tt("pool", Tt["m2"][:, :], Tt["mean"][:, :], Tt["mean"][:, :], ALU.mult, [Tb["mean"]], [Tb["m2"]])
            c.stt(Tt["var"][:, :], P2[:, :TT], 1.0 / 64, Tt["m2"][:, :], ALU.mult, ALU.subtract, [P2b, Tb["m2"]], [Tb["var"]])
            c.act(Tt["var"][:, :], Tt["var"][:, :], AF.Sqrt, [Tb["var"]], [Tb["var"]], bias=64e-5, scale=1.0)
            c.recip(Tt["var"][:, :], Tt["var"][:, :], [Tb["var"]], [Tb["var"]])
            c.tt("pool", Tt["d"][:, :], y[:, m, :], Tt["mean"][:, :], ALU.subtract, [yb, Tb["mean"]], [Tb["d"]])
            c.tt("dve", Tt["d"][:, :], Tt["d"][:, :], Tt["var"][:, :], ALU.mult, [Tb["d"], Tb["var"]], [Tb["d"]])
            c.ts("dve", Tt["o1"][:, :], Tt["d"][:, :], lgs[:, m:m + 1], lbs[:, m:m + 1], ALU.mult, ALU.add, [Tb["d"], constb], [Tb["o1"]])
            c.tt("pool", Tt["o1"][:, :], Tt["o1"][:, :], bo[:, m, :], ALU.add, [Tb["o1"], bob], [Tb["o1"]])
            c.tt("dve", o[:, m, :], Tt["o1"][:, :], ga[:, m, :], ALU.mult, [Tb["o1"], gab], [ob])
        sg, sgb = stage[t % NS], stageb[t % NS]
        for n in range(8):
            ps, pb = pss[pi % NP], psb[pi % NP]
            pi += 1
            for m in range(NM):
                c.mm(ps[:, :TT], wos[:, m, n * 128:(n + 1) * 128], o[:, m, :], m == 0, m == NM - 1, [wob, ob], [pb])
            c.copy("act" if n % 2 == 0 else "dve", sg[:, n, :], ps[:, :TT], [pb], [sgb])
        seg, off = divmod(t * TT, NT)
        c.dma(pv[seg, :, :, off:off + TT], sg[:, :, :], dst_[t % NS], reads=[sgb], writes=[partb])
    dcc = c.S.dsem()
    rsb = c.buf("rs")
    c.S.op("pool", lambda eng: eng.collective_compute("ReduceScatter", ALU.add, replica_groups=RG4,
                                                      ins=[part_d.opt()], outs=[rs_d.opt()]),
           [partb], [rsb], dsem=dcc, inc=1)
    c.phase_end()


SGU_KEYS = ("g", "w_in", "b_u", "b_v", "ln_g", "ln_b", "wsT", "b_s", "w_out")
RW_KEYS = ("mu", "w_rkv", "w0", "a0", "k_k", "k_a", "r_k", "w_la", "w_lb", "a_la", "a_lb", "g_la", "g_lb", "ln_g", "ln_b", "w_o")
RW_SHAPES = {"mu": [128, 6, 8], "w_rkv": [3, D, 256], "w0": [128, 2], "a0": [128, 2], "k_k": [128, 2], "k_a": [128, 2],
             "r_k": [128, 2], "v0": [128, 2], "w_la": [D, 64], "w_lb": [64, 256], "a_la": [D, 64], "a_lb": [64, 256],
             "g_la": [D, 128], "g_lb": [128, 256], "v_la": [D, 32], "v_lb": [32, 256], "ln_g": [128, 2], "ln_b": [128, 2],
             "w_o": [256, D]}
SGU_SHAPES = {"g": [128, 8], "w_in": [D, 2 * E2], "b_u": [128, 16], "b_v": [1, E2], "ln_g": [128, 16], "ln_b": [128, 16],
              "wsT": [128, 16, 128], "b_s": [16, 128], "w_out": [E2, D]}


def build_fused(T, NI=2):
    NT = T // 4
    c = Ctx()
    xT = c.dram_in("xT", [D, NT])
    yT = c.dram_out("yT", [D, NT])
    sgu = [{k: c.dram_in("s%d_%s" % (j, k), SGU_SHAPES[k]) for k in SGU_KEYS} for j in range(2)]
    ffn = [{"g": c.dram_in("f%d_g" % i, [128, 8]), "w1": c.dram_in("f%d_w1" % i, [D, FF]),
            "w2": c.dram_in("f%d_w2" % i, [FF, D])} for i in range(4)]
    gf = c.dram_in("gf", [128, 8])
    rw = []
    for j in range(2):
        keys = RW_KEYS + (("v0", "v_la", "v_lb") if j > 0 else ())
        d = {k: c.dram_in("r%d_%s" % (j, k), RW_SHAPES[k]) for k in keys}
        d["g"] = c.dram_in("r%d_g" % j, [128, 8])
        rw.append(d)
    ident = c.dram_in("ident", [128, 128], BF16)
    CMh = c.dram_in("CM", [128, 4, 64])
    for d in rw:
        d["ident"] = ident
    xs_d = c.scratch("xs_d", [D, NT], F32)
    hb_d = c.scratch("hb_d", [D, NT], BF16)
    hall_d = c.scratch("hall_d", [4 * D, NT], BF16)
    part_d = c.scratch("part_d", [4 * D, NT], F32)
    rs_d = c.scratch("rs_d", [D, NT], F32)
    SC = {n: c.scratch("sc_" + n, [256, T], BF16) for n in ("at", "rt", "bt", "kt")}
    SC.update({n + "_tm": c.scratch("sc_" + n + "_tm", [T, 256], BF16) for n in ("at", "bh", "vv", "kh")})
    SC["gl"] = c.scratch("sc_gl", [256, T // CH], F32)
    SC["bonusv"] = c.scratch("sc_bonusv", [256, T], F32)
    SC["gate"] = c.scratch("sc_gate", [256, T], F32)
    SC["vfirst"] = c.scratch("sc_vfirst", [256, T], F32)
    SC["ysc"] = c.scratch("sc_ysc", [4, 64, T], F32)

    emit_sgu(c, xT, xs_d, NT, sgu[0])
    emit_ffn(c, xs_d, xs_d, NT, ffn[0]["g"], ffn[0]["w1"], ffn[0]["w2"], TT=min(512, NT))
    for j in range(2):
        emit_rwkv_a(c, xs_d, NT, rw[j]["g"], hb_d, hall_d)
        emit_rwkv_b(c, hall_d, NT, T, rw[j], SC, vmix=(j > 0))
        emit_rwkv_scan(c, T, SC, CMh, NI)
        emit_rwkv_d(c, NT, T, rw[j], SC, part_d, rs_d)
        i = 2 * j + 1
        last = j == 1
        emit_ffn(c, xs_d, yT if last else xs_d, NT, ffn[i]["g"], ffn[i]["w1"], ffn[i]["w2"], add=rs_d,
                 gfin=gf if last else None, TT=min(512, NT))
        if not last:
            emit_sgu(c, xs_d, xs_d, NT, sgu[1])
            emit_ffn(c, xs_d, xs_d, NT, ffn[2]["g"], ffn[2]["w1"], ffn[2]["w2"], TT=min(512, NT))
    c.S.emit()
    c.es.close()
    return c.nc


def to_pm(v, hg):
    return np.ascontiguousarray(np.asarray(v, np.float32).reshape(-1)[hg * 256:(hg + 1) * 256].reshape(2, 128).T)


def fused_inputs(inp, B, T):
    import ml_dtypes
    f = np.float32
    A = lambda a: np.ascontiguousarray(np.asarray(a), dtype=f)
    NT = T // 4
    x = np.asarray(inp["x"], f)
    shared = {"gf": to_pc(inp["final_norm_g"]), "CM": scan_consts(),
              "ident": np.eye(128, dtype=f).astype(ml_dtypes.bfloat16)}
    for j in range(2):
        shared.update({"s%d_g" % j: to_pc(inp["norm_mix_g"][2 * j]), "s%d_w_in" % j: A(inp["sgu_w_in"][j]),
                       "s%d_b_u" % j: to_pc(np.asarray(inp["sgu_b_in"][j])[:E2]),
                       "s%d_b_v" % j: A(np.asarray(inp["sgu_b_in"][j])[None, E2:]),
                       "s%d_ln_g" % j: to_pc(inp["sgu_ln_g"][j]), "s%d_ln_b" % j: to_pc(inp["sgu_ln_b"][j]),
                       "s%d_wsT" % j: A(np.transpose(np.asarray(inp["sgu_w_s"][j]), (2, 0, 1))),
                       "s%d_b_s" % j: A(inp["sgu_b_s"][j]), "s%d_w_out" % j: A(inp["sgu_w_out"][j])})
        shared.update({"r%d_g" % j: to_pc(inp["norm_mix_g"][2 * j + 1]),
                       "r%d_mu" % j: np.ascontiguousarray(np.stack([to_pc(inp["rwkv_mu"][j][i]) for i in range(6)], axis=1)),
                       "r%d_w_la" % j: A(inp["rwkv_w_lora_a"][j]), "r%d_a_la" % j: A(inp["rwkv_a_lora_a"][j]),
                       "r%d_g_la" % j: A(inp["rwkv_g_lora_a"][j])})
        if j > 0:
            shared["r%d_v_la" % j] = A(inp["rwkv_v_lora_a"][j - 1])
    for i in range(4):
        shared.update({"f%d_g" % i: to_pc(inp["norm_ffn_g"][i]), "f%d_w1" % i: A(inp["ffn_w1"][i]),
                       "f%d_w2" % i: A(inp["ffn_w2"][i])})
    per_hg = []
    for hg in range(4):
        d = {}
        cs = slice(hg * 256, (hg + 1) * 256)
        for j in range(2):
            pre = "r%d_" % j
            d[pre + "w_rkv"] = A(np.asarray(inp["rwkv_w_rkv"][j])[:, :, cs])
            for k, src in (("w0", "rwkv_w0"), ("a0", "rwkv_a0"), ("k_k", "rwkv_k_k"), ("k_a", "rwkv_k_a"),
                           ("r_k", "rwkv_r_k"), ("ln_g", "rwkv_ln_g"), ("ln_b", "rwkv_ln_b")):
                d[pre + k] = to_pm(inp[src][j], hg)
            d[pre + "w_lb"] = A(np.asarray(inp["rwkv_w_lora_b"][j])[:, cs])
            d[pre + "a_lb"] = A(np.asarray(inp["rwkv_a_lora_b"][j])[:, cs])
            d[pre + "g_lb"] = A(np.asarray(inp["rwkv_g_lora_b"][j])[:, cs])
            d[pre + "w_o"] = A(np.asarray(inp["rwkv_w_o"][j])[cs, :])
            if j > 0:
                d[pre + "v0"] = to_pm(inp["rwkv_v0"][j - 1], hg)
                d[pre + "v_lb"] = A(np.asarray(inp["rwkv_v_lora_b"][j - 1])[:, cs])
        per_hg.append(d)
    in_maps = []
    for b in range(B):
        for q in range(4):
            m = dict(shared)
            m.update(per_hg[q])
            m["xT"] = np.ascontiguousarray(x[b, q * NT:(q + 1) * NT].T)
            in_maps.append(m)
    return in_maps


def kernel_fused(inp, NI=3):
    x = np.asarray(inp["x"])
    B, T, _ = x.shape
    assert B == 2
    NT = T // 4
    nc = get_prog(("fused", T, NI), lambda: build_fused(T, NI))
    in_maps = fused_inputs(inp, B, T)
    res = run_bass_kernel_spmd(nc, in_maps, core_ids=list(range(NCORES)))
    out = np.empty((B, T, D), np.float32)
    for b in range(B):
        for q in range(4):
            out[b, q * NT:(q + 1) * NT] = res.results[b * 4 + q]["yT"].T
    return out


def kernel(**inputs):
    return kernel_fused({k: np.asarray(v) for k, v in inputs.items()})
```
